# Optimizing a Trainium2 kernel written in Bass

```python
import math
import jax, jax.numpy as jnp
from jax import lax
import numpy as np

D_MODEL = 1024
BATCH = 8
SEQ = 2048
DEPTH = 1

GRID_W = 64
CTX_LEN = 256
HEAD_DIM = 64
MIX_WIDTH = D_MODEL
DIFF_HEADS = (MIX_WIDTH // 2) // (2 * HEAD_DIM)
NA_HEADS = (MIX_WIDTH // 2) // HEAD_DIM
DIFF_WIDTH = DIFF_HEADS * 2 * HEAD_DIM
NA_WIDTH = NA_HEADS * HEAD_DIM
IN_WIDTH = 3 * DIFF_WIDTH + 3 * NA_WIDTH
NA_KH_MAX = 8
NA_KW = 16
NA_QCOL_BLOCK = 16
NA_KCOL_BAND = 32
D_FF = 2816
ROPE_BASE = 10000.0
NORM_EPS = 1e-6
Q_BLOCK = 128
N_MOD = 9
NEG_INF = -1e30

kernel_name = "hybrid_diffattn_natten_macaron_dit"


def rms_norm(x, g):
    xf = x.astype(jnp.float32)
    y = xf * lax.rsqrt(jnp.mean(xf * xf, axis=-1, keepdims=True) + NORM_EPS)
    return (y * g.astype(jnp.float32)).astype(x.dtype)


def modulate(h, shift, scale):
    return h * (1.0 + scale) + shift


def adaln(cond, w_ada, b_ada):
    return jax.nn.silu(cond) @ w_ada + b_ada


def swiglu(h, w_gu, w_down):
    g, u = jnp.split(h @ w_gu, 2, axis=-1)
    return (jax.nn.silu(g) * u) @ w_down


def ffn_half_step(h, g, shift, scale, gate, w_gu, w_down):
    return h + 0.5 * gate * swiglu(modulate(rms_norm(h, g), shift, scale), w_gu, w_down)


def axial_rope_angles(n_tokens):
    t = jnp.arange(n_tokens, dtype=jnp.int32)
    row = (t // GRID_W).astype(jnp.float32)
    col = (t % GRID_W).astype(jnp.float32)
    n_freq = HEAD_DIM // 4
    inv_freq = ROPE_BASE ** (-jnp.arange(n_freq, dtype=jnp.float32) / n_freq)
    return row[:, None] * inv_freq, col[:, None] * inv_freq


def rope_rotate(x, ang):
    x1, x2 = jnp.split(x, 2, axis=-1)
    cos = jnp.cos(ang).astype(x.dtype)
    sin = jnp.sin(ang).astype(x.dtype)
    return jnp.concatenate([x1 * cos - x2 * sin, x2 * cos + x1 * sin], axis=-1)


def apply_axial_rope(x, ang_row, ang_col):
    xr, xc = jnp.split(x, 2, axis=-1)
    return jnp.concatenate([rope_rotate(xr, ang_row), rope_rotate(xc, ang_col)], axis=-1)


def split_heads(t, n_heads, hd):
    B, T, _ = t.shape
    return t.reshape(B, T, n_heads, hd).transpose(0, 2, 1, 3)


def merge_heads(t):
    B, H, T, e = t.shape
    return t.transpose(0, 2, 1, 3).reshape(B, T, H * e)


def diff_heads_qk(t, gain):
    B, T, _ = t.shape
    t = t.reshape(B, T, DIFF_HEADS, 2, HEAD_DIM).transpose(0, 2, 3, 1, 4)
    return rms_norm(t, gain)


def diff_lambda(lq1, lk1, lq2, lk2, lam_init):
    f = lambda a, b: jnp.exp(jnp.sum(a.astype(jnp.float32) * b.astype(jnp.float32)))
    return f(lq1, lk1) - f(lq2, lk2) + lam_init


def diff_core(q, k, v, lam):
    s = jnp.einsum('bhcqd,bhckd->bhcqk', q, k).astype(jnp.float32) * (HEAD_DIM ** -0.5)
    p = jax.nn.softmax(s, axis=-1)
    p = p[:, :, 0] - lam * p[:, :, 1]
    return jnp.einsum('bhqk,bhke->bhqe', p.astype(v.dtype), v)


def diff_attention_blocks(q, k, v, lam):
    B, H, _, S, d = q.shape
    nb = S // Q_BLOCK
    qb = jnp.moveaxis(q.reshape(B, H, 2, nb, Q_BLOCK, d), 3, 0)
    out = lax.map(lambda qq: diff_core(qq, k, v, lam), qb)
    return jnp.moveaxis(out, 0, 2).reshape(B, H, S, v.shape[-1])


def dense_attention(q, k, v):
    s = jnp.einsum('bhqd,bhkd->bhqk', q, k).astype(jnp.float32) * (HEAD_DIM ** -0.5)
    p = jax.nn.softmax(s, axis=-1)
    return jnp.einsum('bhqk,bhkd->bhqd', p.astype(v.dtype), v)


def na_column_tables():
    n_cb = GRID_W // NA_QCOL_BLOCK
    qcol = np.arange(GRID_W).reshape(n_cb, NA_QCOL_BLOCK)
    cs = np.clip(qcol - NA_KW // 2, 0, GRID_W - NA_KW)
    bs = np.clip(np.arange(n_cb) * NA_QCOL_BLOCK - NA_KW // 2, 0, GRID_W - NA_KCOL_BAND)
    kcol = bs[:, None] + np.arange(NA_KCOL_BAND)
    valid = (kcol[:, None, :] >= cs[:, :, None]) & (kcol[:, None, :] < cs[:, :, None] + NA_KW)
    col_off = np.clip(kcol[:, None, :] - qcol[:, :, None] + NA_KW - 1, 0, 2 * NA_KW - 2)
    return kcol.astype(np.int32), valid, col_off.astype(np.int32)


def neighborhood_attention(q, k, v, k_ctx, v_ctx, rpb):
    B, H, S, d = q.shape
    rows = S // GRID_W
    kh = min(NA_KH_MAX, rows)
    n_cb = GRID_W // NA_QCOL_BLOCK
    kcol, valid, col_off = na_column_tables()
    qg = q.reshape(B, H, rows, GRID_W, d)
    kg = k.reshape(B, H, rows, GRID_W, d)
    vg = v.reshape(B, H, rows, GRID_W, d)
    rpb_col = jnp.take(rpb, col_off, axis=2).astype(jnp.float32)
    scale = HEAD_DIM ** -0.5
    n_win = kh * NA_KCOL_BAND

    def row_step(r):
        rs = jnp.clip(r - kh // 2, 0, rows - kh)
        k_rows = lax.dynamic_slice_in_dim(kg, rs, kh, axis=2)
        v_rows = lax.dynamic_slice_in_dim(vg, rs, kh, axis=2)
        k_win = jnp.take(k_rows, kcol, axis=3)
        v_win = jnp.take(v_rows, kcol, axis=3)
        q_r = lax.dynamic_index_in_dim(qg, r, axis=2, keepdims=False).reshape(B, H, n_cb, NA_QCOL_BLOCK, d)
        s_win = jnp.einsum('bhnqd,bhinkd->bhnqik', q_r, k_win).astype(jnp.float32) * scale
        row_idx = rs + jnp.arange(kh, dtype=jnp.int32) - r + (NA_KH_MAX - 1)
        bias = jnp.take(rpb_col, row_idx, axis=1).transpose(0, 2, 3, 1, 4)
        s_win = jnp.where(valid[:, :, None, :], s_win + bias[None], NEG_INF)
        s_ctx = jnp.einsum('bhnqd,bhcd->bhnqc', q_r, k_ctx).astype(jnp.float32) * scale
        s = jnp.concatenate([s_win.reshape(B, H, n_cb, NA_QCOL_BLOCK, n_win), s_ctx], axis=-1)
        p = jax.nn.softmax(s, axis=-1).astype(v.dtype)
        p_win = p[..., :n_win].reshape(B, H, n_cb, NA_QCOL_BLOCK, kh, NA_KCOL_BAND)
        p_ctx = p[..., n_win:]
        out = (jnp.einsum('bhnqik,bhinkd->bhnqd', p_win, v_win)
               + jnp.einsum('bhnqc,bhcd->bhnqd', p_ctx, v_ctx))
        return out.reshape(B, H, GRID_W, d)

    out = lax.map(row_step, jnp.arange(rows, dtype=jnp.int32))
    return out.transpose(1, 2, 0, 3, 4).reshape(B, H, S, d)


def setup_inputs(seed: int = 0) -> dict:
    key = jax.random.key(seed)
    ks = jax.random.split(key, 32)
    L, D = DEPTH, D_MODEL
    nrm = lambda k, shape, s: jax.random.normal(k, shape, jnp.float32) * s
    gain = lambda k, shape: 1.0 + 0.02 * jax.random.normal(k, shape, jnp.float32)
    return {
        "x": nrm(ks[0], (BATCH, SEQ, D), 1.0),
        "c": nrm(ks[1], (BATCH, D), 1.0),
        "ctx": nrm(ks[2], (BATCH, CTX_LEN, D), 1.0),
        "c_ctx": nrm(ks[3], (D,), 1.0),
        "w_ada": nrm(ks[4], (L, D, N_MOD * D), 0.5 * D ** -0.5),
        "b_ada": nrm(ks[5], (L, N_MOD * D), 0.01),
        "norm1": gain(ks[6], (L, D)),
        "norm2": gain(ks[7], (L, D)),
        "norm3": gain(ks[8], (L, D)),
        "ffn1_w_gu": nrm(ks[9], (L, D, 2 * D_FF), D ** -0.5),
        "ffn1_w_down": nrm(ks[10], (L, D_FF, D), D_FF ** -0.5),
        "w_in": nrm(ks[11], (L, D, IN_WIDTH), D ** -0.5),
        "diff_q_norm": gain(ks[12], (L, HEAD_DIM)),
        "diff_k_norm": gain(ks[13], (L, HEAD_DIM)),
        "lam_q1": nrm(ks[14], (L, HEAD_DIM), 0.1),
        "lam_k1": nrm(ks[15], (L, HEAD_DIM), 0.1),
        "lam_q2": nrm(ks[16], (L, HEAD_DIM), 0.1),
        "lam_k2": nrm(ks[17], (L, HEAD_DIM), 0.1),
        "diff_out_norm": gain(ks[18], (L, 2 * HEAD_DIM)),
        "na_q_norm": gain(ks[19], (L, HEAD_DIM)),
        "na_k_norm": gain(ks[20], (L, HEAD_DIM)),
        "na_rpb": nrm(ks[21], (L, NA_HEADS, 2 * NA_KH_MAX - 1, 2 * NA_KW - 1), 0.1),
        "w_out": nrm(ks[22], (L, MIX_WIDTH, D), MIX_WIDTH ** -0.5),
        "ffn2_w_gu": nrm(ks[23], (L, D, 2 * D_FF), D ** -0.5),
        "ffn2_w_down": nrm(ks[24], (L, D_FF, D), D_FF ** -0.5),
    }


def reference(x, c, ctx, c_ctx, w_ada, b_ada, norm1, norm2, norm3, ffn1_w_gu, ffn1_w_down, w_in,
              diff_q_norm, diff_k_norm, lam_q1, lam_k1, lam_q2, lam_k2, diff_out_norm,
              na_q_norm, na_k_norm, na_rpb, w_out, ffn2_w_gu, ffn2_w_down):
    S = x.shape[1]
    ang_row, ang_col = axial_rope_angles(S)
    splits = (DIFF_WIDTH, 2 * DIFF_WIDTH, 3 * DIFF_WIDTH, 3 * DIFF_WIDTH + NA_WIDTH, 3 * DIFF_WIDTH + 2 * NA_WIDTH)
    h, hc = x, ctx
    for l in range(DEPTH):
        update_ctx = l < DEPTH - 1
        lam_init = 0.8 - 0.6 * math.exp(-0.3 * l)
        mod = jnp.split(adaln(c, w_ada[l], b_ada[l])[:, None, :], N_MOD, axis=-1)
        mod_c = jnp.split(adaln(c_ctx, w_ada[l], b_ada[l]), N_MOD, axis=-1)

        h = ffn_half_step(h, norm1[l], mod[0], mod[1], mod[2], ffn1_w_gu[l], ffn1_w_down[l])
        hc = ffn_half_step(hc, norm1[l], mod_c[0], mod_c[1], mod_c[2], ffn1_w_gu[l], ffn1_w_down[l])

        y = modulate(rms_norm(h, norm2[l]), mod[3], mod[4]) @ w_in[l]
        yc = modulate(rms_norm(hc, norm2[l]), mod_c[3], mod_c[4]) @ w_in[l]
        dq, dk, dv, nq, nk, nv = jnp.split(y, splits, axis=-1)
        dq_c, dk_c, dv_c, nq_c, nk_c, nv_c = jnp.split(yc, splits, axis=-1)

        lam = diff_lambda(lam_q1[l], lam_k1[l], lam_q2[l], lam_k2[l], lam_init)
        a_q = apply_axial_rope(diff_heads_qk(dq, diff_q_norm[l]), ang_row, ang_col)
        a_k = apply_axial_rope(diff_heads_qk(dk, diff_k_norm[l]), ang_row, ang_col)
        a_v = split_heads(dv, DIFF_HEADS, 2 * HEAD_DIM)
        a_k_c = diff_heads_qk(dk_c, diff_k_norm[l])
        a_v_c = split_heads(dv_c, DIFF_HEADS, 2 * HEAD_DIM)
        k_all = jnp.concatenate([a_k_c, a_k], axis=3)
        v_all = jnp.concatenate([a_v_c, a_v], axis=2)
        o_a = diff_attention_blocks(a_q, k_all, v_all, lam)
        o_a = merge_heads(rms_norm(o_a, diff_out_norm[l]) * (1.0 - lam_init))

        b_q = rms_norm(split_heads(nq, NA_HEADS, HEAD_DIM), na_q_norm[l])
        b_k = rms_norm(split_heads(nk, NA_HEADS, HEAD_DIM), na_k_norm[l])
        b_v = split_heads(nv, NA_HEADS, HEAD_DIM)
        b_k_c = rms_norm(split_heads(nk_c, NA_HEADS, HEAD_DIM), na_k_norm[l])
        b_v_c = split_heads(nv_c, NA_HEADS, HEAD_DIM)
        o_b = merge_heads(neighborhood_attention(b_q, b_k, b_v, b_k_c, b_v_c, na_rpb[l]))

        h = h + mod[5] * (jnp.concatenate([o_a, o_b], axis=-1) @ w_out[l])
        h = ffn_half_step(h, norm3[l], mod[6], mod[7], mod[8], ffn2_w_gu[l], ffn2_w_down[l])

        if update_ctx:
            oc_a = diff_core(diff_heads_qk(dq_c, diff_q_norm[l]), a_k_c, a_v_c, lam)
            oc_a = merge_heads(rms_norm(oc_a, diff_out_norm[l]) * (1.0 - lam_init))
            oc_b = merge_heads(dense_attention(rms_norm(split_heads(nq_c, NA_HEADS, HEAD_DIM), na_q_norm[l]), b_k_c, b_v_c))
            hc = hc + mod_c[5] * (jnp.concatenate([oc_a, oc_b], axis=-1) @ w_out[l])
            hc = ffn_half_step(hc, norm3[l], mod_c[6], mod_c[7], mod_c[8], ffn2_w_gu[l], ffn2_w_down[l])
    return h
```

```python
import math
from contextlib import ExitStack

import numpy as np
import concourse.bass as bass
import concourse.mybir as mybir
from concourse.bass_utils import run_bass_kernel_spmd

F32 = mybir.dt.float32
BF16 = mybir.dt.bfloat16
ALU = mybir.AluOpType
AF = mybir.ActivationFunctionType
AX = mybir.AxisListType

DMA_RING = 8


class _Op:
    __slots__ = ("eng", "fn", "reads", "writes", "dma", "idx", "need", "signal", "seq",
                 "slot", "val", "prewait", "barrier")


class Prog:
    ENG = ("pe", "act", "dve", "pool", "sp")

    def __init__(self):
        self.ops = []

    def add(self, eng, fn, reads=(), writes=(), dma=False):
        op = _Op()
        op.eng, op.fn, op.dma = eng, fn, dma
        rs, ws = list(reads), list(writes)
        for r in list(rs):
            if isinstance(r, tuple) and r[0] == "ps" and r not in ws:
                ws.append(r)
        op.reads, op.writes = tuple(rs), tuple(ws)
        op.idx = len(self.ops)
        op.need = []
        op.signal = False
        op.seq = 0
        op.slot = op.val = op.prewait = None
        op.barrier = False
        self.ops.append(op)
        return op

    def pe(self, fn, reads=(), writes=()):
        return self.add("pe", fn, reads, writes)

    def act(self, fn, reads=(), writes=()):
        return self.add("act", fn, reads, writes)

    def dve(self, fn, reads=(), writes=()):
        return self.add("dve", fn, reads, writes)

    def pool(self, fn, reads=(), writes=()):
        return self.add("pool", fn, reads, writes)

    def dma(self, eng, fn, reads=(), writes=()):
        return self.add(eng, fn, reads, writes, dma=True)

    def barrier(self):
        op = self.add("sp", None)
        op.barrier = True
        return op

    def _analyze(self):
        last_w = {}
        readers = {}
        ops = self.ops
        last_comp = {}
        recent_dma = {e: [] for e in self.ENG}
        bar_deps = []
        for op in ops:
            if op.barrier:
                bar_deps = list(last_comp.values())
                for e in self.ENG:
                    bar_deps += recent_dma[e][-DMA_RING:]
                last_w.clear()
                readers.clear()
                continue
            raw = set()
            other = set()
            for r in op.reads:
                if r in last_w:
                    raw.add(last_w[r])
            for w in op.writes:
                if w in last_w:
                    other.add(last_w[w])
                for rd in readers.get(w, ()):
                    other.add(rd)
            raw.discard(op.idx)
            other.discard(op.idx)
            other -= raw
            need = []
            for p in sorted(raw | other):
                P = ops[p]
                if P.dma or op.dma:
                    need.append(p)
                elif P.eng == op.eng:
                    if P.eng == "pe":
                        continue
                    need.append(p)
                else:
                    need.append(p)
            for p in bar_deps:
                if p not in need and not (ops[p].eng == op.eng and not ops[p].dma and not op.dma
                                          and op.eng == "pe"):
                    need.append(p)
            need.sort()
            op.need = need
            for p in need:
                ops[p].signal = True
            for r in op.reads:
                readers.setdefault(r, []).append(op.idx)
            for w in op.writes:
                last_w[w] = op.idx
                readers[w] = []
            if op.dma:
                recent_dma[op.eng].append(op.idx)
            else:
                last_comp[op.eng] = op.idx
        cnt = {e: 0 for e in self.ENG}
        dcnt = {e: 0 for e in self.ENG}
        for op in ops:
            if op.barrier:
                continue
            if op.dma:
                j = dcnt[op.eng]
                dcnt[op.eng] += 1
                op.slot = j % DMA_RING
                op.val = 16 * (j // DMA_RING + 1)
                op.prewait = 16 * (j // DMA_RING) if j >= DMA_RING else None
            elif op.signal:
                cnt[op.eng] += 1
                op.seq = cnt[op.eng]
        self.counts = cnt
        self.dcounts = dcnt

    def emit(self, nc):
        self._analyze()
        ops = self.ops
        with ExitStack() as es:
            csem = {e: es.enter_context(nc.semaphore("c_" + e)) for e in self.ENG if e != "sp"}
            dsem = {e: [es.enter_context(nc.semaphore("d_%s%d" % (e, i))) for i in range(DMA_RING)]
                    for e in self.ENG if self.dcounts[e] > 0}
            block = es.enter_context(nc.Block())
            for c in self.counts.values():
                assert c < 60000, self.counts

            def run(engname, handle):
                known = {}
                last_dma = {}
                for op in ops:
                    if op.eng != engname or op.barrier:
                        continue
                    waits = []
                    for p in op.need:
                        Pp = ops[p]
                        if Pp.dma:
                            waits.append((dsem[Pp.eng][Pp.slot], Pp.val))
                        else:
                            waits.append((csem[Pp.eng], Pp.seq))
                    if op.dma and op.prewait is not None:
                        waits.append((dsem[op.eng][op.slot], op.prewait))
                    for s, v in waits:
                        k = id(s)
                        if known.get(k, 0) >= v:
                            continue
                        known[k] = v
                        handle.wait_ge(s, v)
                    inst = op.fn(handle)
                    if op.dma:
                        inst.then_inc(dsem[op.eng][op.slot], 16)
                        last_dma[op.slot] = op.val
                    elif op.signal:
                        inst.then_inc(csem[op.eng], 1)
                for slot, v in last_dma.items():
                    if known.get(id(dsem[engname][slot]), 0) < v:
                        handle.wait_ge(dsem[engname][slot], v)

            @block.tensor
            def _(e):
                run("pe", e)

            @block.scalar
            def _(e):
                run("act", e)

            @block.vector
            def _(e):
                run("dve", e)

            @block.gpsimd
            def _(e):
                run("pool", e)

            @block.sync
            def _(e):
                run("sp", e)


D = 1024
S = 2048
C = 256
T = S + C
KC = 8
FF = 2816
NJ = 22
NMOD = 9
EPS = 1e-6
LAM_INIT = 0.8 - 0.6 * math.exp(0.0)
TCH = [(0, 512, 0), (512, 512, 0), (1024, 512, 0), (1536, 512, 0), (2048, 256, 1)]
FGROUPS = [(0, 6), (6, 12), (12, 17), (17, 22)]
NEG = -30000.0


def build(stage=99, dbg=False):
    nc = bass.Bass("TRN2", target_bir_lowering=False)

    def din(name, shape, dt=F32):
        return nc.dram_tensor(name, list(shape), dt, kind="ExternalInput").ap()

    x_d = din("x", [S, D])
    ctx_d = din("ctx", [C, D])
    cc_d = din("cc", [2, D])
    wada_d = din("wada", [18, 128, 4096])
    badaT_d = din("badaT", [128, 72])
    gT_d = din("gT", [128, 24])
    wgu_d = [din("wgu1", [NJ, 128, 2048]), din("wgu2", [NJ, 128, 2048])]
    wd_d = [din("wd1", [FF, D]), din("wd2", [FF, D])]
    win_d = din("win", [8, 128, 3072])
    wout_d = din("wout", [8, 128, 1024])
    qkg_d = din("qkg", [128, 4])
    gout_d = din("gout", [128, 1])
    lamv_d = din("lamv", [1, 256])
    ropec_d = din("ropec", [128, S])
    ropes_d = din("ropes", [128, S])
    rmat_d = din("rmat", [128, 128])
    ident_d = din("ident", [128, 128])
    ones_d = din("ones", [128, 128])
    bones_d = din("bones", [128, 128])
    nab_d = din("nab", [4, 128, 2 * 12 * 128])
    nam_d = din("nam", [128, 12 * 128])
    out_d = nc.dram_tensor("out", [S, D], F32, kind="ExternalOutput").ap()
    dbg_d = {}
    if dbg:
        dbg_d["hT"] = nc.dram_tensor("dbg_hT", [128, KC * T], F32, kind="ExternalOutput").ap()
        dbg_d["mod"] = nc.dram_tensor("dbg_mod", [128, 144], F32, kind="ExternalOutput").ap()
        dbg_d["xn"] = nc.dram_tensor("dbg_xn", [128, KC * T], BF16, kind="ExternalOutput").ap()
        dbg_d["q"] = nc.dram_tensor("dbg_q", [8, 128, S], BF16, kind="ExternalOutput").ap()
        dbg_d["k"] = nc.dram_tensor("dbg_k", [8, 128, T], BF16, kind="ExternalOutput").ap()
        dbg_d["v"] = nc.dram_tensor("dbg_v", [8, 128, 18 * 132], BF16, kind="ExternalOutput").ap()
        dbg_d["o"] = nc.dram_tensor("dbg_o", [8, 128, S], BF16, kind="ExternalOutput").ap()

    P = Prog()
    with ExitStack() as es:
        def sb(name, shape, dt):
            return es.enter_context(nc.sbuf_tensor("s_" + name, list(shape), dt))

        hT = sb("hT", [128, KC, T], F32)
        xnT = sb("xnT", [128, KC, T], BF16)
        ident = sb("ident", [128, 128], F32)
        ones_b = sb("ones_b", [128, 128], BF16)
        bones_b = sb("bones_b", [128, 128], BF16)
        rmat_b = sb("rmat_b", [128, 128], BF16)
        ident_b = sb("ident_b", [128, 128], BF16)
        modT = sb("modT", [128, 72, 2], F32)
        Asc = sb("Asc", [128, 3, KC, 2], F32)
        Gsc = sb("Gsc", [128, 3, KC, 2], F32)
        gT = sb("gT", [128, 24], F32)
        qkg = sb("qkg", [128, 4], F32)
        gout = sb("gout", [128, 1], F32)
        epst = sb("epst", [128, 1], F32)
        neglam = sb("neglam", [128, 1], F32)
        rstd = sb("rstd", [128, 512], F32)
        lnv = sb("lnv", [128, 512], F32)
        tmpA = [sb("tmpA%d" % i, [128, 512], F32) for i in range(2)]
        ARENA = 84 * 1024
        arena = sb("arena", [128, ARENA // 2], BF16)
        ps = es.enter_context(nc.psum_tensor("ps", [128, 8, 512], F32))
        psb = ps[:, 3, :].bitcast(BF16)

        def carve(off, shape, dt):
            n = int(np.prod(shape[1:]))
            if dt == F32:
                assert off % 4 == 0
                v = arena[:, off // 2: off // 2 + 2 * n].bitcast(F32)
            else:
                v = arena[:, off // 2: off // 2 + n]
            if len(shape) == 2:
                return v
            names = "abcd"[: len(shape) - 1]
            pat = "p (" + " ".join(names) + ") -> p " + " ".join(names)
            kw = {names[i]: shape[i + 1] for i in range(len(shape) - 1)}
            return v.rearrange(pat, **kw)

        PS = lambda b: ("ps", b)
        PSB = ("ps", 3)

        P.dma("sp", lambda e: e.dma_start(out=ident[:], in_=ident_d), writes=["ident"])
        P.dma("pool", lambda e: e.dma_start(out=ones_b[:], in_=ones_d), writes=["ones_b"])
        P.dma("pool", lambda e: e.dma_start(out=bones_b[:], in_=bones_d), writes=["bones_b"])
        P.dma("pool", lambda e: e.dma_start(out=rmat_b[:], in_=rmat_d), writes=["rmat_b"])
        P.dma("pool", lambda e: e.dma_start(out=ident_b[:], in_=ident_d), writes=["ident_b"])
        P.dma("sp", lambda e: e.dma_start(out=gT[:], in_=gT_d), writes=["gT"])
        P.dma("sp", lambda e: e.dma_start(out=qkg[:], in_=qkg_d), writes=["qkg"])
        P.dma("sp", lambda e: e.dma_start(out=gout[:], in_=gout_d), writes=["gout"])
        P.dve(lambda e: e.memset(epst[:], EPS), writes=["epst"])

        xin = [carve(8192 + i * 4096, [128, 1024], F32) for i in range(2)]
        cc_sb = carve(16384, [128, 1024], F32)
        sc_sb = carve(20480, [128, 1024], F32)
        lam_sb = carve(24576, [128, 256], F32)
        lam_t = carve(25600, [128, 16], F32)
        ones_row = carve(25664, [128, 128], F32)
        PA = 64512
        scT = carve(PA, [128, 8, 2], BF16)
        mod_blk = carve(PA + 256, [128, 512], F32)
        wada_ring = [carve(PA + 256 + 2048 + i * 8192, [128, 8, 512], BF16) for i in range(2)]
        badaT = sb("badaT", [128, 72], F32)

        P.dma("sp", lambda e: e.dma_start(out=cc_sb[0:2, :], in_=cc_d), writes=["cc"])
        P.dma("sp", lambda e: e.dma_start(out=badaT[:], in_=badaT_d), writes=["badaT"])
        P.dma("sp", lambda e: e.dma_start(out=lam_sb[0:1, :], in_=lamv_d), writes=["lamv"])
        P.act(lambda e: e.activation(sc_sb[0:2, :], cc_sb[0:2, :], AF.Silu), reads=["cc"], writes=["sc"])
        for kc in range(KC):
            P.pe(lambda e, kc=kc: e.matmul(ps[:, 0, 2 * kc:2 * kc + 2], sc_sb[0:2, kc * 128:(kc + 1) * 128],
                                           ident[0:2, 0:2], start=True, stop=True),
                 reads=["sc", "ident"], writes=[PS(0)])
        P.dve(lambda e: e.tensor_copy(scT.rearrange("p a b -> p (a b)"), ps[:, 0, 0:16]),
              reads=[PS(0)], writes=["scT"])

        def wada_load(n):
            wr = wada_ring[n % 2]
            P.dma("pool", lambda e: e.dma_start(out=wr.rearrange("p a b -> p (a b)"), in_=wada_d[n]),
                  writes=[("wada", n % 2)])

        def wada_block(n, acc_bank, tr_bank):
            wr = wada_ring[n % 2]
            for kc in range(KC):
                P.pe(lambda e, kc=kc: e.matmul(ps[0:2, acc_bank, :], scT[:, kc, :], wr[:, kc, :],
                                               start=(kc == 0), stop=(kc == KC - 1)),
                     reads=["scT", ("wada", n % 2)], writes=[PS(acc_bank)])
            P.dve(lambda e: e.tensor_copy(mod_blk[0:2, :], ps[0:2, acc_bank, :]),
                  reads=[PS(acc_bank)], writes=["modb"])
            for q in range(4):
                i = 4 * n + q
                P.pe(lambda e, i=i, q=q: e.matmul(ps[:, tr_bank, 2 * i:2 * i + 2], mod_blk[0:2, q * 128:(q + 1) * 128],
                                                  ident[0:2, 0:2], start=True, stop=True),
                     reads=["modb", "ident"], writes=[PS(tr_bank)])

        def mods_finish(i_lo, i_hi, tr_bank, subs):
            for v in range(2):
                P.dve(lambda e, v=v: e.tensor_tensor(
                    out=modT[:, i_lo:i_hi, v], in0=ps[:, tr_bank, 2 * i_lo:2 * i_hi].rearrange("p (a b) -> p a b", b=2)[:, :, v],
                    in1=badaT[:, i_lo:i_hi], op=ALU.add),
                    reads=[PS(tr_bank), "badaT"], writes=["modT"])
            for s_ in subs:
                for v in range(2):
                    P.dve(lambda e, s_=s_, v=v: e.scalar_tensor_tensor(
                        out=Asc[:, s_, :, v], in0=modT[:, (3 * s_ + 1) * 8:(3 * s_ + 2) * 8, v], scalar=1.0,
                        in1=gT[:, s_ * 8:(s_ + 1) * 8], op0=ALU.add, op1=ALU.mult),
                        reads=["modT", "gT"], writes=["Asc"])
                P.dve(lambda e, s_=s_: e.tensor_scalar(
                    out=Gsc[:, s_, :, :], in0=modT[:, (3 * s_ + 2) * 8:(3 * s_ + 3) * 8, :],
                    scalar1=(1.0 if s_ == 1 else 0.5), scalar2=None, op0=ALU.mult),
                    reads=["modT"], writes=["Gsc"])

        def x_tile(i):
            slot = i % 2
            src = x_d[i * 128:(i + 1) * 128, :] if i < 16 else ctx_d[(i - 16) * 128:(i - 15) * 128, :]
            P.dma("sp", lambda e: e.dma_start(out=xin[slot][:], in_=src), writes=["xin%d" % slot])
            for half in range(2):
                bank = 4 + 2 * (i % 2) + half
                for q in range(4):
                    kc = half * 4 + q
                    P.pe(lambda e, kc=kc, bank=bank, q=q: e.transpose(
                        ps[:, bank, q * 128:(q + 1) * 128], xin[slot][:, kc * 128:(kc + 1) * 128], ident[:]),
                        reads=["xin%d" % slot, "ident"], writes=[PS(bank)])
                dst = hT[:, half * 4:half * 4 + 4, i * 128:(i + 1) * 128]
                srcp = ps[:, bank, :].rearrange("p (a b) -> p a b", a=4)
                if half == 0:
                    P.act(lambda e, dst=dst, srcp=srcp: e.activation(dst, srcp, AF.Copy),
                          reads=[PS(bank)], writes=[("hT", i)])
                else:
                    P.dve(lambda e, dst=dst, srcp=srcp: e.tensor_copy(dst, srcp),
                          reads=[PS(bank)], writes=[("hT", i)])

        wada_load(0)
        wada_load(1)
        for n in range(6):
            for i in range(3 * n, 3 * n + 3):
                x_tile(i)
            wada_block(n, 1 + (n % 2), 3)
            if n + 2 < 18:
                wada_load(n + 2)
        mods_finish(0, 24, 3, [0])
        Bsc = lambda s, kc, v: modT[:, 3 * s * 8 + kc, v:v + 1]
        P.dve(lambda e: e.memset(ones_row[0:1, :], 1.0), writes=["ones_row"])
        P.dve(lambda e: e.tensor_tensor(out=lam_sb[0:1, 0:64], in0=lam_sb[0:1, 0:64],
                                        in1=lam_sb[0:1, 64:128], op=ALU.mult), reads=["lamv"], writes=["lamv"])
        P.dve(lambda e: e.tensor_tensor(out=lam_sb[0:1, 128:192], in0=lam_sb[0:1, 128:192],
                                        in1=lam_sb[0:1, 192:256], op=ALU.mult), reads=["lamv"], writes=["lamv"])
        P.dve(lambda e: e.reduce_sum(out=lam_t[0:1, 0:1], in_=lam_sb[0:1, 0:64], axis=AX.X),
              reads=["lamv"], writes=["lam_t"])
        P.dve(lambda e: e.reduce_sum(out=lam_t[0:1, 1:2], in_=lam_sb[0:1, 128:192], axis=AX.X),
              reads=["lamv", "lam_t"], writes=["lam_t"])
        P.act(lambda e: e.activation(lam_t[0:1, 2:4], lam_t[0:1, 0:2], AF.Exp), reads=["lam_t"], writes=["lam_t"])
        P.dve(lambda e: e.scalar_tensor_tensor(out=lam_t[0:1, 4:5], in0=lam_t[0:1, 3:4], scalar=-LAM_INIT,
                                               in1=lam_t[0:1, 2:3], op0=ALU.add, op1=ALU.subtract),
              reads=["lam_t"], writes=["lam_t2"])
        P.pe(lambda e: e.matmul(ps[:, 0, 0:1], ones_row[0:1, :], lam_t[0:1, 4:5], start=True, stop=True),
             reads=["lam_t2", "ones_row"], writes=[PS(0)])
        P.dve(lambda e: e.tensor_copy(neglam[:], ps[:, 0, 0:1]), reads=[PS(0)], writes=["neglam"])

        def adaln_bg_items():
            items = []
            for n in range(6, 18):
                def item(n=n):
                    wada_block(n, 7, 6)
                    if n + 2 < 18:
                        wada_load(n + 2)
                    if n == 17:
                        mods_finish(24, 72, 6, [1, 2])
                        if dbg:
                            P.dma("sp", lambda e: e.dma_start(out=dbg_d["mod"], in_=modT.rearrange("p a b -> p (a b)")),
                                  reads=["modT"])
                items.append(item)
            return items

        def norm_phase(s, chunks, sq, bank=6, toff=8192, ci0=0):
            ntmp = [carve(toff + i * 2048, [128, 512], F32) for i in range(4)]
            for ci_, (t0, L, v) in enumerate(chunks):
                ci = ci_ + ci0
                tiles = [("hT", i) for i in range(t0 // 128, (t0 + L) // 128)]
                for kc in range(KC):
                    if kc % 2 == 0:
                        P.act(lambda e, kc=kc, t0=t0, L=L: e.activation(sq[:, kc, 0:L], hT[:, kc, t0:t0 + L], AF.Square),
                              reads=tiles, writes=[("sq", kc)])
                    else:
                        P.dve(lambda e, kc=kc, t0=t0, L=L: e.tensor_tensor(out=sq[:, kc, 0:L], in0=hT[:, kc, t0:t0 + L],
                                                                            in1=hT[:, kc, t0:t0 + L], op=ALU.mult),
                              reads=tiles, writes=[("sq", kc)])
                for kc in range(KC):
                    P.pe(lambda e, kc=kc, L=L: e.matmul(ps[:, bank, 0:L], ones_b[:], sq[:, kc, 0:L],
                                                        start=(kc == 0), stop=(kc == KC - 1)),
                         reads=[("sq", kc), "ones_b"], writes=[PS(bank)])
                P.act(lambda e, L=L: e.activation(lnv[:, 0:L], ps[:, bank, 0:L], AF.Ln, bias=epst[:, 0:1], scale=1.0 / D),
                      reads=[PS(bank), "epst"], writes=["lnv"])
                P.act(lambda e, L=L: e.activation(rstd[:, 0:L], lnv[:, 0:L], AF.Exp, scale=-0.5),
                      reads=["lnv"], writes=["rstd"])
                for kc in range(KC):
                    tb = ntmp[kc % 4]
                    P.dve(lambda e, kc=kc, t0=t0, L=L, tb=tb, v=v: e.scalar_tensor_tensor(
                        out=tb[:, 0:L], in0=hT[:, kc, t0:t0 + L], scalar=Asc[:, s, kc, v:v + 1], in1=rstd[:, 0:L],
                        op0=ALU.mult, op1=ALU.mult),
                        reads=tiles + ["rstd", "Asc"], writes=[("ntmp", kc % 4)])
                    P.act(lambda e, kc=kc, t0=t0, L=L, tb=tb, v=v: e.activation(
                        xnT[:, kc, t0:t0 + L], tb[:, 0:L], AF.Identity, bias=Bsc(s, kc, v), scale=1.0),
                        reads=[("ntmp", kc % 4), "modT"], writes=[("xn", ci)])

        def ffn_phase(s, chunks, wgu, wd, aoff, final=None, bgitems=None):
            actT = carve(aoff, [128, 6, T], BF16)
            wgu_ring = [carve(aoff + 27648 + i * 4096, [128, 8, 256], BF16) for i in range(4)]
            wd_ring = [carve(aoff + 27648 + 16384 + i * 2048, [128, 1024], BF16) for i in range(8)]
            sg = [carve(aoff + 27648 + 16384 + 16384 + i * 2048, [128, 512], F32) for i in range(2)]
            nci = len(chunks)
            cnt = [0]
            dcnt = [0]

            def load_wgu(j):
                P.dma("pool", lambda e, j=j: e.dma_start(out=wgu_ring[j % 4].rearrange("p a b -> p (a b)"), in_=wgu[j]),
                      writes=[("wgu", j % 4)])

            def load_wd(j):
                P.dma("pool", lambda e, j=j: e.dma_start(out=wd_ring[j % 8][:], in_=wd[j * 128:(j + 1) * 128, :]),
                      writes=[("wd", j % 8)])

            for j in range(3):
                load_wgu(j)
            for j in range(6):
                load_wd(j)
            for (j0, j1) in FGROUPS:
                for j in range(j0, j1):
                    if j + 3 < NJ:
                        load_wgu(j + 3)
                    if bgitems:
                        bgitems.pop(0)()
                    w = wgu_ring[j % 4]
                    for ci, (t0, L, v) in enumerate(chunks):
                        bg = 2 * (cnt[0] % 2)
                        bu = bg + 1
                        cnt[0] += 1
                        for kc in range(KC):
                            P.pe(lambda e, w=w, kc=kc, t0=t0, L=L, bg=bg: e.matmul(
                                ps[:, bg, 0:L], w[:, kc, 0:128], xnT[:, kc, t0:t0 + L],
                                start=(kc == 0), stop=(kc == KC - 1)),
                                reads=[("wgu", j % 4), ("xn", ci)], writes=[PS(bg)])
                        for kc in range(KC):
                            P.pe(lambda e, w=w, kc=kc, t0=t0, L=L, bu=bu: e.matmul(
                                ps[:, bu, 0:L], w[:, kc, 128:256], xnT[:, kc, t0:t0 + L],
                                start=(kc == 0), stop=(kc == KC - 1)),
                                reads=[("wgu", j % 4), ("xn", ci)], writes=[PS(bu)])
                        sgb = sg[cnt[0] % 2]
                        P.act(lambda e, sgb=sgb, bg=bg, L=L: e.activation(sgb[:, 0:L], ps[:, bg, 0:L], AF.Silu),
                              reads=[PS(bg)], writes=[("sg", cnt[0] % 2)])
                        P.dve(lambda e, sgb=sgb, bu=bu, L=L, jj=j - j0, t0=t0: e.tensor_tensor(
                            out=actT[:, jj, t0:t0 + L], in0=sgb[:, 0:L], in1=ps[:, bu, 0:L], op=ALU.mult),
                            reads=[("sg", cnt[0] % 2), PS(bu)], writes=[("act", j - j0, ci)])
                ng = j1 - j0
                last_group = (j1 == NJ) and final is not None
                order = ([(m, ci) for ci in range(len(chunks)) for m in range(KC)] if last_group
                         else [(m, ci) for m in range(KC) for ci in range(len(chunks))])
                for (m, ci) in order:
                    if True:
                        t0, L, v = chunks[ci]
                        bd = 4 + (dcnt[0] % 2)
                        dcnt[0] += 1
                        tiles = [("hT", i) for i in range(t0 // 128, (t0 + L) // 128)]
                        for jj in range(ng):
                            j = j0 + jj
                            P.pe(lambda e, j=j, jj=jj, m=m, t0=t0, L=L, bd=bd, ng=ng: e.matmul(
                                ps[:, bd, 0:L], wd_ring[j % 8][:, m * 128:(m + 1) * 128], actT[:, jj, t0:t0 + L],
                                start=(jj == 0), stop=(jj == ng - 1)),
                                reads=[("wd", j % 8), ("act", jj, ci)], writes=[PS(bd)])
                        P.dve(lambda e, m=m, t0=t0, L=L, bd=bd, v=v: e.scalar_tensor_tensor(
                            out=hT[:, m, t0:t0 + L], in0=ps[:, bd, 0:L], scalar=Gsc[:, s, m, v:v + 1],
                            in1=hT[:, m, t0:t0 + L], op0=ALU.mult, op1=ALU.add),
                            reads=[PS(bd), "Gsc"] + tiles, writes=tiles)
                        if last_group and m == KC - 1:
                            final(ci)
                nxt = [g for g in FGROUPS if g[0] == j1]
                if nxt:
                    for j in range(max(nxt[0][0], 6), nxt[0][1]):
                        load_wd(j)

        def dump_hT():
            P.dma("sp", lambda e: e.dma_start(out=dbg_d["hT"], in_=hT.rearrange("p a b -> p (a b)")),
                  reads=[("hT", i) for i in range(18)])

        def write_out(tiles=range(16), banks=(0, 1, 2, 3), ooff=0):
            osb = [carve(ooff + i * 4096, [128, 1024], F32) for i in range(2)]
            for i in tiles:
                slot = i % 2
                for half in range(2):
                    bank = banks[(2 * (i % 2) + half) % len(banks)]
                    for q in range(4):
                        kc = half * 4 + q
                        P.pe(lambda e, kc=kc, bank=bank, q=q, i=i: e.transpose(
                            ps[:, bank, q * 128:(q + 1) * 128], hT[:, kc, i * 128:(i + 1) * 128], ident[:]),
                            reads=[("hT", i), "ident"], writes=[PS(bank)])
                    dst = osb[slot][:, half * 512:(half + 1) * 512]
                    if half == 0:
                        P.act(lambda e, dst=dst, bank=bank: e.activation(dst, ps[:, bank, :], AF.Copy),
                              reads=[PS(bank)], writes=[("osb", slot, half)])
                    else:
                        P.dve(lambda e, dst=dst, bank=bank: e.tensor_copy(dst, ps[:, bank, :]),
                              reads=[PS(bank)], writes=[("osb", slot, half)])
                P.dma("sp", lambda e, i=i, slot=slot: e.dma_start(out=out_d[i * 128:(i + 1) * 128, :], in_=osb[slot][:]),
                      reads=[("osb", slot, 0), ("osb", slot, 1)])

        P.barrier()
        sq1 = carve(0, [128, 8, 512], BF16)
        norm_phase(0, TCH, sq1)
        if dbg:
            P.dma("sp", lambda e: e.dma_start(out=dbg_d["xn"], in_=xnT.rearrange("p a b -> p (a b)")),
                  reads=[("xn", i) for i in range(5)])
        P.barrier()
        if stage >= 2:
            bg1 = adaln_bg_items()
            if stage > 2:
                sq2 = carve(PA, [128, 8, 512], BF16)
                ffn_phase(0, TCH, wgu_d[0], wd_d[0], 0, bgitems=bg1,
                          final=lambda ci: norm_phase(1, [TCH[ci]], sq2, toff=PA + 8192, ci0=ci))
            else:
                ffn_phase(0, TCH, wgu_d[0], wd_d[0], 0, bgitems=bg1)
            assert not bg1
        if stage <= 2:
            P.barrier()
            if dbg:
                dump_hT()
            write_out()
            P.emit(nc)
            return nc

        P.barrier()

        class Bump:
            def __init__(self):
                self.off = 0

            def __call__(self, shape, dt):
                n = int(np.prod(shape[1:])) * (4 if dt == F32 else 2)
                self.off = (self.off + 3) // 4 * 4
                v = carve(self.off, shape, dt)
                self.off += n
                assert self.off <= ARENA, self.off
                return v

        def chunk_of_tile(i):
            return i // 4 if i < 16 else 4

        LAT = TCH[:4]
        YB = 2
        SCR = 3


        def qk_items(slot, wsl_, wcol, dst, dname, chunks, gcol, rope, sqb_, qn, qnb, t1, ropec, ropes,
                     ybs=(2,), scrs=(3,), lnvs=None, rstds=None, grouped=False, roff=0):
            lnvs = lnvs or [(lnv, "lnv")]
            rstds = rstds or [(rstd, "rstd")]
            items = []
            groups = []
            for ci, (t0, L, v) in enumerate(chunks):
                ri = ci + roff
                sb_ = sqb_[ri % 2]
                sqk = ("sqq", ri % 2)
                roped = rope and v == 0
                yb = ybs[ri % len(ybs)]
                scr = scrs[ri % len(scrs)]
                lnv_, lnk = lnvs[ri % len(lnvs)]
                rstd_, rsk = rstds[ri % len(rstds)]

                def stage_a(ci=ci, t0=t0, L=L, sb_=sb_, yb=yb, sqk=sqk):
                    for kc in range(KC):
                        P.pe(lambda e, kc=kc: e.matmul(
                            ps[:, yb, 0:L], wsl_[:, kc, wcol:wcol + 128], xnT[:, kc, t0:t0 + L],
                            start=(kc == 0), stop=(kc == KC - 1)),
                            reads=[("win", slot), ("xn", ci)], writes=[PS(yb)])
                    P.act(lambda e: e.activation(sb_[:, 0:L], ps[:, yb, 0:L], AF.Square),
                          reads=[PS(yb)], writes=[sqk])

                def stage_b(ci=ci, t0=t0, L=L, sb_=sb_, roped=roped, yb=yb, scr=scr, lnv_=lnv_, lnk=lnk,
                            rstd_=rstd_, rsk=rsk, sqk=sqk):
                    P.pe(lambda e: e.matmul(ps[:, scr, 0:L], bones_b[:], sb_[:, 0:L], start=True, stop=True),
                         reads=[sqk, "bones_b"], writes=[PS(scr)])
                    P.act(lambda e: e.activation(lnv_[:, 0:L], ps[:, scr, 0:L], AF.Ln, bias=epst[:, 0:1], scale=1.0 / 64),
                          reads=[PS(scr), "epst"], writes=[lnk])
                    P.act(lambda e: e.activation(rstd_[:, 0:L], lnv_[:, 0:L], AF.Exp, scale=-0.5),
                          reads=[lnk], writes=[rsk])
                    if not roped:
                        P.dve(lambda e: e.scalar_tensor_tensor(
                            out=dst[:, t0:t0 + L], in0=ps[:, yb, 0:L], scalar=qkg[:, gcol:gcol + 1], in1=rstd_[:, 0:L],
                            op0=ALU.mult, op1=ALU.mult),
                            reads=[PS(yb), rsk, "qkg"], writes=[(dname, slot, ci)])
                    else:
                        P.dve(lambda e: e.scalar_tensor_tensor(
                            out=qn[:, 0:L], in0=ps[:, yb, 0:L], scalar=qkg[:, gcol:gcol + 1], in1=rstd_[:, 0:L],
                            op0=ALU.mult, op1=ALU.mult),
                            reads=[PS(yb), rsk, "qkg"], writes=["qn"])
                        P.dve(lambda e: e.tensor_copy(qnb[:, 0:L], qn[:, 0:L]), reads=["qn"], writes=["qnb"])
                        P.pool(lambda e: e.tensor_tensor(out=t1[:, 0:L], in0=qn[:, 0:L], in1=ropec[:, t0:t0 + L],
                                                         op=ALU.mult), reads=["qn", "ropec"], writes=["t1"])

                def stage_c(ci=ci, t0=t0, L=L, scr=scr):
                    P.pe(lambda e: e.matmul(ps[:, scr, 0:L], rmat_b[:], qnb[:, 0:L], start=True, stop=True),
                         reads=["qnb", "rmat_b"], writes=[PS(scr)])
                    P.dve(lambda e: e.tensor_tensor(out=tmpA[0][:, 0:L], in0=ps[:, scr, 0:L],
                                                    in1=ropes[:, t0:t0 + L], op=ALU.mult),
                          reads=[PS(scr), "ropes"], writes=[("tmpA", 0)])
                    P.pool(lambda e: e.tensor_tensor(out=dst[:, t0:t0 + L], in0=t1[:, 0:L],
                                                     in1=tmpA[0][:, 0:L], op=ALU.add),
                           reads=["t1", ("tmpA", 0)], writes=[(dname, slot, ci)])
                g = [stage_a, stage_b] + ([stage_c] if roped else [])
                groups.append(g)
                items += g
            return groups if grouped else items

        def emit_pipelined(groups):
            n = len(groups)
            for i in range(n + 1):
                if i < n:
                    groups[i][0]()
                if i >= 1:
                    for st in groups[i - 1][1:]:
                        st()

        def v_items(slot, wsl_, Vt, na):
            items = []
            for g in range(5):
                def item(g=g):
                    tl = list(range(4 * g, min(4 * g + 4, 18)))
                    for ti, i in enumerate(tl):
                        for kc in range(KC):
                            P.pe(lambda e, kc=kc, i=i, ti=ti: e.matmul(
                                ps[:, YB, ti * 128:(ti + 1) * 128], xnT[:, kc, i * 128:(i + 1) * 128],
                                wsl_[:, kc, 256:384], start=(kc == 0), stop=(kc == KC - 1)),
                                reads=[("win", slot), ("xn", chunk_of_tile(i))], writes=[PS(YB)])
                    n = len(tl)
                    if na:
                        dstv = Vt[:, 4 * g:4 * g + n, :, 0:64]
                        srcv = ps[:, YB, 0:n * 128].rearrange("p (a h d) -> p a h d", a=n, h=2)
                    else:
                        dstv = Vt[:, 4 * g:4 * g + n, 0:128]
                        srcv = ps[:, YB, 0:n * 128].rearrange("p (a d) -> p a d", a=n)
                    P.dve(lambda e: e.tensor_copy(dstv, srcv), reads=[PS(YB)], writes=[("V", slot, g)])
                items.append(item)
            return items

        def wout_items(slot, woutb_, oTt, banks=(2,), act_share=None):
            items = []
            for m in range(KC):
                for t in range(4):
                    def item(m=m, t=t, yb=banks[(m * 4 + t) % len(banks)]):
                        P.pe(lambda e: e.matmul(
                            ps[:, yb, :], woutb_[:, m * 128:(m + 1) * 128], oTt[:, t * 512:(t + 1) * 512],
                            start=True, stop=True),
                            reads=[("wout", slot), ("oT", slot, t)], writes=[PS(yb)])
                        tiles = [("hT", i) for i in range(4 * t, 4 * t + 4)]
                        k_ = m * 4 + t
                        use_dve = (k_ % 3 != 2) if act_share is None else (k_ % 3 >= act_share)
                        if act_share is None and len(banks) == 1:
                            use_dve = True
                        if use_dve:
                            P.dve(lambda e: e.scalar_tensor_tensor(
                                out=hT[:, m, t * 512:(t + 1) * 512], in0=ps[:, yb, :], scalar=Gsc[:, 1, m, 0:1],
                                in1=hT[:, m, t * 512:(t + 1) * 512], op0=ALU.mult, op1=ALU.add),
                                reads=[PS(yb), "Gsc"] + tiles, writes=tiles)
                        else:
                            P.act(lambda e: e.activation(tmpA[1][:, 0:512], ps[:, yb, :], AF.Copy, scale=Gsc[:, 1, m, 0:1]),
                                  reads=[PS(yb), "Gsc"], writes=[("tmpA", 1)])
                            P.pool(lambda e: e.tensor_tensor(out=hT[:, m, t * 512:(t + 1) * 512], in0=tmpA[1][:, 0:512],
                                                             in1=hT[:, m, t * 512:(t + 1) * 512], op=ALU.add),
                                   reads=[("tmpA", 1)] + tiles, writes=tiles)
                    items.append(item)
            return items

        def run_with_bg(nsteps_, step_fn, bg, tail=0):
            nb = len(bg)
            done = 0
            for i in range(nsteps_ + tail):
                want = min((nb * (i + 1) + nsteps_ - 1) // nsteps_, nb) if i < nsteps_ else done
                step_fn(i, want - done)
                if i < nsteps_:
                    while done < min(want, nb):
                        bg[done]()
                        done += 1
            while done < nb:
                bg[done]()
                done += 1

        A = Bump()
        ropec = A([128, S], F32)
        ropes = A([128, S], F32)
        wsl = [A([128, 8, 384], BF16) for _ in range(2)]
        qT = [A([128, S], BF16) for _ in range(2)]
        kT = [A([128, T], BF16) for _ in range(2)]
        Vd = [A([128, 18, 132], BF16) for _ in range(2)]
        oT = [A([128, S], BF16) for _ in range(2)]
        woutb = [A([128, 1024], BF16) for _ in range(2)]
        sqb = [A([128, 512], BF16) for _ in range(2)]
        qn = A([128, 512], F32)
        qnb = A([128, 512], BF16)
        t1 = A([128, 512], F32)
        pT = [A([128, 512], BF16) for _ in range(3)]
        o_f = A([128, 2, 128], F32)
        o_b = A([128, 2, 128], BF16)
        rz = A([128, 2, 2, 1], F32)
        rzl = A([128, 2, 1], F32)
        ssq = A([128, 2], F32)
        rs4 = A([128, 2], F32)
        gout8 = A([128, 1], F32)
        accs = A([128, 2, 260], F32)

        P.dma("sp", lambda e: e.dma_start(out=ropec, in_=ropec_d), writes=["ropec"])
        P.dma("sp", lambda e: e.dma_start(out=ropes, in_=ropes_d), writes=["ropes"])
        P.dve(lambda e: e.tensor_scalar(out=gout8, in0=gout[:], scalar1=1.0 - LAM_INIT, scalar2=None, op0=ALU.mult),
              reads=["gout"], writes=["gout8"])
        for sl in range(2):
            P.dve(lambda e, sl=sl: e.memset(Vd[sl][:, :, 128:129], 1.0), writes=[("Vones", sl)])

        n_diff = 4 if stage >= 3 else 0

        def diff_load(u):
            slot = u % 2
            P.dma("pool", lambda e: e.dma_start(out=wsl[slot].rearrange("p a b -> p (a b)"), in_=win_d[u]),
                  writes=[("win", slot)])

        def diff_load_wout(u):
            slot = u % 2
            P.dma("pool", lambda e: e.dma_start(out=woutb[slot], in_=wout_d[u]), writes=[("wout", slot)])

        def diff_proj_items(u):
            slot = u % 2
            return (qk_items(slot, wsl[slot], 128, kT[slot], "kT", TCH, 1, True, sqb, qn, qnb, t1, ropec, ropes)
                    + v_items(slot, wsl[slot], Vd[slot], False)
                    + qk_items(slot, wsl[slot], 0, qT[slot], "qT", LAT, 0, True, sqb, qn, qnb, t1, ropec, ropes))

        NFILL = 2

        def diff_attention(u, bg):
            slot = u % 2
            steps = [(qc, kt) for qc in range(8) for kt in range(18)]
            nst = len(steps)
            SPAIR = [4, 6]
            deferred = {}

            def emit_S(i):
                qc, kt = steps[i]
                b0 = SPAIR[i % 2]
                kci = chunk_of_tile(kt)
                for c in range(2):
                    P.pe(lambda e, c=c: e.matmul(
                        ps[:, b0 + c, 0:256], kT[slot][c * 64:(c + 1) * 64, kt * 128:(kt + 1) * 128],
                        qT[slot][c * 64:(c + 1) * 64, qc * 256:(qc + 1) * 256], start=True, stop=True),
                        reads=[("kT", slot, kci), ("qT", slot, qc // 2)], writes=[PS(b0 + c)])
                pi = i % 3
                P.act(lambda e: e.activation(pT[pi].rearrange("p (a b) -> p a b", a=2),
                                             ps[:, b0:b0 + 2, 0:256], AF.Exp, scale=0.125),
                      reads=[PS(b0), PS(b0 + 1)], writes=[("pT", pi)])

            def emit_PV(i, nfill=2):
                qc, kt = steps[i]
                ab = (0, 1)
                pi = i % 3
                if NFILL and kt != 0:
                    for f_ in range(nfill):
                        P.pe(lambda e, f_=f_: e.matmul(ps[:, ab[f_ % 2], 264:512], ident_b[:], xnT[:, f_, 0:248],
                                                       start=False, stop=False),
                             reads=["ident_b"], writes=[PS(ab[f_ % 2])])
                for c in range(2):
                    bk = ab[c]
                    for qs in range(2):
                        col = qs * 130
                        P.pe(lambda e, qs=qs, c=c, bk=bk, col=col: e.matmul(
                            ps[:, bk, col:col + 129], pT[pi][:, c * 256 + qs * 128:c * 256 + (qs + 1) * 128],
                            Vd[slot][:, kt, 0:129], start=(kt == 0 and qs == 0), stop=(kt == 17 and qs == 1)),
                            reads=[("pT", pi), ("V", slot, kt // 4), ("Vones", slot)], writes=[PS(bk)])
                if kt != 17:
                    return
                P.act(lambda e: e.activation(accs[:, 0, :], ps[:, ab[0], 0:260], AF.Copy),
                      reads=[PS(ab[0])], writes=[("accs", 0)])
                P.dve(lambda e: e.tensor_copy(accs[:, 1, :], ps[:, ab[1], 0:260]),
                      reads=[PS(ab[1])], writes=[("accs", 1)])
                for c in range(2):
                    accv = accs[:, c, :].rearrange("p (a b) -> p a b", a=2)
                    P.dve(lambda e, c=c, accv=accv: e.reciprocal(rz[:, c, 0:2, :], accv[:, :, 128:129]),
                          reads=[("accs", c)], writes=[("rz", c)])
                P.dve(lambda e: e.tensor_scalar(out=rzl[:, 0:2, :], in0=rz[:, 1, 0:2, :], scalar1=neglam[:, 0:1],
                                                scalar2=None, op0=ALU.mult),
                      reads=[("rz", 1), "neglam"], writes=["rzl"])
                for qs in range(2):
                    col = qs * 130
                    P.dve(lambda e, qs=qs, col=col: e.tensor_scalar(
                        out=o_f[:, qs, :], in0=accs[:, 0, col:col + 128], scalar1=rz[:, 0, qs, :], scalar2=None,
                        op0=ALU.mult),
                        reads=[("accs", 0), ("rz", 0)], writes=[("o_f", qs)])
                    P.dve(lambda e, qs=qs, col=col: e.scalar_tensor_tensor(
                        out=o_f[:, qs, :], in0=accs[:, 1, col:col + 128], scalar=rzl[:, qs, :],
                        in1=o_f[:, qs, :], op0=ALU.mult, op1=ALU.add),
                        reads=[("accs", 1), "rzl", ("o_f", qs)], writes=[("o_f", qs)])
                def part_a2():
                    P.dve(lambda e: e.memset(ssq, 0.0), writes=["ssq"])
                    for qs in range(2):
                        P.act(lambda e, qs=qs: e.activation(tmpA[1][:, 0:128], o_f[:, qs, :], AF.Square,
                                                            accum_out=ssq[:, qs:qs + 1]),
                              reads=[("o_f", qs), "ssq"], writes=["ssq", ("tmpA", 1)])
                    P.act(lambda e: e.activation(rs4[:, 0:2], ssq[:, 0:2], AF.Ln, bias=epst[:, 0:1], scale=1.0 / 128),
                          reads=["ssq", "epst"], writes=["rs4"])
                    P.act(lambda e: e.activation(rs4[:, 0:2], rs4[:, 0:2], AF.Exp, scale=-0.5), reads=["rs4"], writes=["rs4"])

                def part_a3():
                    for qs in range(2):
                        P.dve(lambda e, qs=qs: e.tensor_scalar(out=o_b[:, qs, :], in0=o_f[:, qs, :], scalar1=rs4[:, qs:qs + 1],
                                                               scalar2=None, op0=ALU.mult),
                              reads=[("o_f", qs), "rs4"], writes=[("o_b", qs)])

                def part_b(qc=qc):
                    for qs in range(2):
                        P.pe(lambda e, qs=qs: e.transpose(psb[:, qs * 128:(qs + 1) * 128], o_b[:, qs, :], ident_b[:]),
                             reads=[("o_b", qs), "ident_b"], writes=[PSB])
                    P.dve(lambda e: e.tensor_scalar(out=oT[slot][:, qc * 256:(qc + 1) * 256], in0=psb[:, 0:256],
                                                    scalar1=gout8[:, 0:1], scalar2=None, op0=ALU.mult),
                          reads=[PSB, "gout8"], writes=[("oT", slot, qc // 2)])
                deferred.setdefault(i + 3, []).append(part_a2)
                deferred.setdefault(i + 6, []).append(part_a3)
                deferred.setdefault(i + 10, []).append(part_b)

            def step(i, nbg=0):
                if i < nst:
                    emit_S(i)
                if 1 <= i <= nst:
                    emit_PV(i - 1, 1 if nbg else 2)
                for fn_ in deferred.pop(i - 1, []):
                    fn_()

            run_with_bg(nst, step, bg, tail=14)
            assert not deferred

        def dbg_dump(u, slot, q_, k_, v_, vpat):
            P.dma("sp", lambda e: e.dma_start(out=dbg_d["q"][u], in_=q_), reads=[("qT", slot, ci) for ci in range(4)])
            P.dma("sp", lambda e: e.dma_start(out=dbg_d["k"][u], in_=k_), reads=[("kT", slot, ci) for ci in range(5)])
            P.dma("sp", lambda e: e.dma_start(out=dbg_d["v"][u], in_=v_.rearrange(vpat)),
                  reads=[("V", slot, g) for g in range(5)] + [("Vones", slot)])

        if n_diff:
            diff_load(0)
            diff_load(1)
            kw0 = dict(ybs=(2, 4, 6), scrs=(3, 5, 7), grouped=True)
            gk0 = qk_items(0, wsl[0], 128, kT[0], "kT", TCH, 1, True, sqb, qn, qnb, t1, ropec, ropes, **kw0)
            gq0 = qk_items(0, wsl[0], 0, qT[0], "qT", LAT, 0, True, sqb, qn, qnb, t1, ropec, ropes, roff=5, **kw0)
            emit_pipelined(gk0 + gq0)
            for it in v_items(0, wsl[0], Vd[0], False):
                it()
        for u in range(n_diff):
            slot = u % 2
            bg = []
            if u + 1 < n_diff:
                bg += diff_proj_items(u + 1)
            if u >= 1:
                diff_load_wout(u - 1)
                bg += wout_items((u - 1) % 2, woutb[(u - 1) % 2], oT[(u - 1) % 2])
            if dbg:
                dbg_dump(u, slot, qT[slot], kT[slot], Vd[slot], "p a b -> p (a b)")
            diff_attention(u, bg)
            if u + 2 < n_diff:
                diff_load(u + 2)
            if dbg:
                P.dma("sp", lambda e, u=u, slot=slot: e.dma_start(out=dbg_d["o"][u], in_=oT[slot]),
                      reads=[("oT", slot, t) for t in range(4)])
        if n_diff:
            diff_load_wout(n_diff - 1)
            for it in wout_items((n_diff - 1) % 2, woutb[(n_diff - 1) % 2], oT[(n_diff - 1) % 2], banks=(2, 4, 5, 6, 7)):
                it()

        P.barrier()
        A2 = Bump()
        Utab = A2([128, 2, 12, 128], F32)
        Umask = A2([128, 12, 128], F32)
        wsl_n = [A2([128, 8, 384], BF16) for _ in range(2)]
        qT_n = [A2([128, S], BF16) for _ in range(2)]
        kT_n = [A2([128, T], BF16) for _ in range(2)]
        Vn = [A2([128, 18, 2, 66], BF16) for _ in range(2)]
        oT_n = [A2([128, S], BF16) for _ in range(2)]
        woutb_n = [A2([128, 1024], BF16) for _ in range(2)]
        sqb_n = [A2([128, 512], BF16) for _ in range(2)]
        sbn = [A2([128, 640], F32) for _ in range(2)]
        pTn = [A2([128, 7, 128], BF16) for _ in range(2)]
        o_tok = [A2([128, 128], BF16) for _ in range(2)]
        rzn = A2([128, 2], F32)
        P.dma("sp", lambda e: e.dma_start(out=Umask.rearrange("p a b -> p (a b)"), in_=nam_d), writes=["Umask"])
        for sl in range(2):
            P.dve(lambda e, sl=sl: e.memset(Vn[sl][:, :, :, 64:65], 1.0), writes=[("Vones", sl)])
        n_na = 4 if stage >= 4 else 0

        def na_load(pr):
            u = 4 + pr
            slot = pr % 2
            P.dma("pool", lambda e: e.dma_start(out=wsl_n[slot].rearrange("p a b -> p (a b)"), in_=win_d[u]),
                  writes=[("win", slot)])

        def na_load_wout(pr):
            u = 4 + pr
            slot = pr % 2
            P.dma("pool", lambda e: e.dma_start(out=woutb_n[slot], in_=wout_d[u]), writes=[("wout", slot)])

        def na_table(pr):
            P.dma("sp", lambda e: e.dma_start(out=Utab.rearrange("p a b c -> p (a b c)"), in_=nab_d[pr]),
                  writes=["Utab"])
            for hh in range(2):
                P.dve(lambda e, hh=hh: e.tensor_tensor(out=Utab[:, hh, :, :], in0=Utab[:, hh, :, :], in1=Umask,
                                                       op=ALU.add), reads=["Utab", "Umask"], writes=["Utab"])

        def na_qk_groups(pr):
            slot = pr % 2
            kw = dict(ybs=(2, 4, 6), scrs=(3, 5, 7), lnvs=[(lnv, "lnv"), (tmpA[0], ("tmpA", 0))],
                      rstds=[(rstd, "rstd"), (tmpA[1], ("tmpA", 1))], grouped=True)
            gk = qk_items(slot, wsl_n[slot], 128, kT_n[slot], "kT", TCH, 3, False, sqb_n, None, None, None, None, None, **kw)
            gq = qk_items(slot, wsl_n[slot], 0, qT_n[slot], "qT", LAT, 2, False, sqb_n, None, None, None, None, None,
                          roff=5, **kw)
            return gk + gq

        def na_v_items(pr):
            slot = pr % 2
            return v_items(slot, wsl_n[slot], Vn[slot], True)

        def na_attention(pr, bg):
            slot = pr % 2
            nsteps = [(qt, hh) for qt in range(16) for hh in range(2)]
            NSETS = [(4, 5), (6, 7)]
            ndef = {}

            def na_info(qt):
                generic = 2 <= qt <= 13
                if generic:
                    kts = list(range(qt - 2, qt + 3))
                    e0 = 0
                else:
                    kts = list(range(0, 4)) if qt < 2 else list(range(12, 16))
                    e0 = 5 + (kts[0] - qt + 3)
                return generic, kts, e0

            def na_S(i):
                qt, hh = nsteps[i]
                generic, kts, e0 = na_info(qt)
                nl = len(kts)
                bx, by = NSETS[i % 2]
                nb_ = i % 2
                hs = slice(hh * 64, (hh + 1) * 64)
                for s_, kt in enumerate(kts):
                    if s_ < 4:
                        dstp, bkk = ps[:, bx, s_ * 128:(s_ + 1) * 128], bx
                    else:
                        dstp, bkk = ps[:, by, 0:128], by
                    P.pe(lambda e, dstp=dstp, kt=kt: e.matmul(
                        dstp, kT_n[slot][hs, kt * 128:(kt + 1) * 128], qT_n[slot][hs, qt * 128:(qt + 1) * 128],
                        start=True, stop=True),
                        reads=[("kT", slot, kt // 4), ("qT", slot, qt // 4)], writes=[PS(bkk)])
                for s_ in range(2):
                    P.pe(lambda e, s_=s_: e.matmul(
                        ps[:, by, 128 + s_ * 128:256 + s_ * 128], kT_n[slot][hs, S + s_ * 128:S + (s_ + 1) * 128],
                        qT_n[slot][hs, qt * 128:(qt + 1) * 128], start=True, stop=True),
                        reads=[("kT", slot, 4), ("qT", slot, qt // 4)], writes=[PS(by)])
                P.dve(lambda e: e.scalar_tensor_tensor(
                    out=sbn[nb_][:, 0:512], in0=ps[:, bx, :], scalar=0.125,
                    in1=Utab[:, hh, e0:e0 + 4, :].rearrange("p a b -> p (a b)"), op0=ALU.mult, op1=ALU.add),
                    reads=[PS(bx), "Utab"], writes=[("sbn", nb_)])
                if generic:
                    P.dve(lambda e: e.scalar_tensor_tensor(
                        out=sbn[nb_][:, 512:640], in0=ps[:, by, 0:128], scalar=0.125, in1=Utab[:, hh, 4, :],
                        op0=ALU.mult, op1=ALU.add),
                        reads=[PS(by), "Utab"], writes=[("sbn", nb_)])
                P.act(lambda e: e.activation(
                    pTn[nb_][:, 0:nl, :].rearrange("p a b -> p (a b)"), sbn[nb_][:, 0:nl * 128], AF.Exp),
                    reads=[("sbn", nb_)], writes=[("pTn", nb_)])
                P.act(lambda e: e.activation(
                    pTn[nb_][:, 5:7, :].rearrange("p a b -> p (a b)"), ps[:, by, 128:384], AF.Exp, scale=0.125),
                    reads=[PS(by)], writes=[("pTn", nb_)])

            def na_PV(i):
                qt, hh = nsteps[i]
                generic, kts, e0 = na_info(qt)
                nb_ = i % 2
                ob = qt % 2
                bka = i % 2
                srcs = [(s_, kt) for s_, kt in enumerate(kts)] + [(5, 16), (6, 17)]
                for j_, (s_, kt) in enumerate(srcs):
                    P.pe(lambda e, s_=s_, kt=kt, j_=j_, nn=len(srcs): e.matmul(
                        ps[:, bka, 0:65], pTn[nb_][:, s_, :], Vn[slot][:, kt, hh, 0:65],
                        start=(j_ == 0), stop=(j_ == nn - 1)),
                        reads=[("pTn", nb_), ("V", slot, kt // 4), ("Vones", slot)], writes=[PS(bka)])
                P.dve(lambda e: e.reciprocal(rzn[:, hh:hh + 1], ps[:, bka, 64:65]),
                      reads=[PS(bka)], writes=[("rzn", hh)])
                P.dve(lambda e: e.tensor_scalar(
                    out=o_tok[ob][:, hh * 64:(hh + 1) * 64], in0=ps[:, bka, 0:64], scalar1=rzn[:, hh:hh + 1],
                    scalar2=None, op0=ALU.mult),
                    reads=[PS(bka), ("rzn", hh)], writes=[("o_tok", ob, hh)])
                if hh == 1:
                    def part_b(qt=qt, ob=ob):
                        P.pe(lambda e: e.transpose(psb[:, 0:128], o_tok[ob], ident_b[:]),
                             reads=[("o_tok", ob, 0), ("o_tok", ob, 1), "ident_b"], writes=[PSB])
                        P.dve(lambda e: e.tensor_copy(oT_n[slot][:, qt * 128:(qt + 1) * 128], psb[:, 0:128]),
                              reads=[PSB], writes=[("oT", slot, qt // 4)])
                    ndef.setdefault(i + 2, []).append(part_b)

            nn_ = len(nsteps)

            def step(i, nbg=0):
                if i < nn_:
                    na_S(i)
                if 1 <= i <= nn_:
                    na_PV(i - 1)
                for fn_ in ndef.pop(i - 1, []):
                    fn_()

            run_with_bg(nn_, step, bg, tail=4)
            assert not ndef

        if n_na:
            na_load(0)
            na_load(1)
            emit_pipelined(na_qk_groups(0))
            for it in na_v_items(0):
                it()
        for pr in range(n_na):
            u = 4 + pr
            slot = pr % 2
            na_table(pr)
            bg = []
            if pr + 1 < n_na:
                bg += na_v_items(pr + 1)
            if pr >= 1:
                na_load_wout(pr - 1)
                bg += wout_items((pr - 1) % 2, woutb_n[(pr - 1) % 2], oT_n[(pr - 1) % 2], act_share=2)
            if dbg:
                dbg_dump(u, slot, qT_n[slot], kT_n[slot], Vn[slot], "p a b c -> p (a b c)")
            na_attention(pr, bg)
            if pr + 1 < n_na:
                emit_pipelined(na_qk_groups(pr + 1))
            if pr + 2 < n_na:
                na_load(pr + 2)
            if dbg:
                P.dma("sp", lambda e, u=u, slot=slot: e.dma_start(out=dbg_d["o"][u], in_=oT_n[slot]),
                      reads=[("oT", slot, t) for t in range(4)])
        if n_na:
            na_load_wout(n_na - 1)
            for it in wout_items((n_na - 1) % 2, woutb_n[(n_na - 1) % 2], oT_n[(n_na - 1) % 2], banks=(2, 4, 5, 6, 7)):
                it()

        if stage <= 4:
            P.barrier()
            if dbg:
                dump_hT()
            write_out()
            P.emit(nc)
            return nc

        P.barrier()
        norm_phase(2, LAT, sq1)
        P.barrier()
        if dbg:
            ffn_phase(2, LAT, wgu_d[1], wd_d[1], 0)
            P.barrier()
            dump_hT()
            write_out()
        else:
            ffn_phase(2, LAT, wgu_d[1], wd_d[1], 0,
                      final=lambda ci: write_out(tiles=range(4 * ci, 4 * ci + 4), banks=(6, 7), ooff=66048))
        P.emit(nc)
    return nc


def _rope_tables():
    t = np.arange(S, dtype=np.int32)
    row = (t // 64).astype(np.float32)
    col = (t % 64).astype(np.float32)
    inv_freq = (np.float32(10000.0) ** (-np.arange(16, dtype=np.float32) / np.float32(16))).astype(np.float32)
    ang_row = row[:, None] * inv_freq[None, :]
    ang_col = col[:, None] * inv_freq[None, :]
    cosT = np.zeros((128, S), np.float32)
    sinT = np.zeros((128, S), np.float32)
    rmat = np.zeros((128, 128), np.float32)
    for p in range(128):
        d = p % 64
        ang = ang_row if d < 32 else ang_col
        dd = d % 32
        i = dd % 16
        cosT[p] = np.cos(ang[:, i])
        if dd < 16:
            sinT[p] = -np.sin(ang[:, i])
            partner = p + 16
        else:
            sinT[p] = np.sin(ang[:, i])
            partner = p - 16
        rmat[partner, p] = 1.0
    return cosT, sinT, rmat


def _na_tables(rpb):
    p = np.arange(128)
    a = (p >= 64).astype(np.int64)
    kc = p % 64
    qc = np.arange(64)
    colvalid = np.zeros((64, 64), bool)
    for q in range(64):
        cs = min(max(q - 8, 0), 48)
        colvalid[cs:cs + 16, q] = True
    dc = kc[:, None] - qc[None, :] + 15
    dc_c = np.clip(dc, 0, 30)
    rels = [(-2 + e, True) for e in range(5)] + [(-3 + e, False) for e in range(7)]
    gath = np.zeros((8, 128, 12, 2, 64), np.float32)
    mask = np.zeros((128, 12, 2, 64), np.float32)
    for e, (rel, generic) in enumerate(rels):
        for b in range(2):
            dr = 2 * rel - b + a
            ok_r = (dr >= -7) & (dr <= 7)
            if generic:
                ok_r &= (dr >= -4) & (dr <= 3)
            valid = ok_r[:, None] & colvalid[kc, :]
            dr_c = np.clip(dr + 7, 0, 14)
            vals = rpb[:, dr_c[:, None], dc_c]
            gath[:, :, e, b, :] = np.where(valid[None], vals, np.float32(0.0))
            mask[:, e, b, :] = np.where(valid, np.float32(0.0), np.float32(NEG))
    nab = gath.reshape(4, 2, 128, 12 * 128).transpose(0, 2, 1, 3).reshape(4, 128, 2 * 12 * 128)
    nam = mask.reshape(128, 12 * 128)
    return np.ascontiguousarray(nab), np.ascontiguousarray(nam)


def _prep_shared(inp):
    f = lambda a: np.ascontiguousarray(np.asarray(a, dtype=np.float32))
    sh = {}
    wada = f(inp["w_ada"])[0]
    sh["wada"] = np.ascontiguousarray(wada.reshape(8, 128, 18, 512).transpose(2, 1, 0, 3).reshape(18, 128, 4096))
    sh["badaT"] = np.ascontiguousarray(f(inp["b_ada"])[0].reshape(72, 128).T)
    g = np.stack([f(inp["norm1"])[0], f(inp["norm2"])[0], f(inp["norm3"])[0]], 0)
    sh["gT"] = np.ascontiguousarray(g.reshape(3, 8, 128).transpose(2, 0, 1).reshape(128, 24))
    for i, nm in ((1, "ffn1"), (2, "ffn2")):
        wgu = f(inp[nm + "_w_gu"])[0]
        gcols = wgu[:, :FF].reshape(8, 128, NJ, 128)
        ucols = wgu[:, FF:].reshape(8, 128, NJ, 128)
        both = np.stack([gcols, ucols], axis=3)
        sh["wgu%d" % i] = np.ascontiguousarray(both.transpose(2, 1, 0, 3, 4).reshape(NJ, 128, 2048))
        sh["wd%d" % i] = f(inp[nm + "_w_down"])[0]
    win = f(inp["w_in"])[0].reshape(8, 128, 3072)
    units = []
    for u in range(8):
        if u < 4:
            cols = [u * 128, 512 + u * 128, 1024 + u * 128]
        else:
            cols = [1536 + (u - 4) * 128, 2048 + (u - 4) * 128, 2560 + (u - 4) * 128]
        blk = np.concatenate([win[:, :, c0:c0 + 128] for c0 in cols], axis=2)
        units.append(blk.transpose(1, 0, 2).reshape(128, 3072))
    sh["win"] = np.ascontiguousarray(np.stack(units, 0))
    sh["wout"] = np.ascontiguousarray(f(inp["w_out"])[0].reshape(8, 128, 1024))
    qkg = np.stack([np.tile(f(inp["diff_q_norm"])[0], 2), np.tile(f(inp["diff_k_norm"])[0], 2),
                    np.tile(f(inp["na_q_norm"])[0], 2), np.tile(f(inp["na_k_norm"])[0], 2)], 1)
    sh["qkg"] = np.ascontiguousarray(qkg)
    sh["gout"] = np.ascontiguousarray(f(inp["diff_out_norm"])[0].reshape(128, 1))
    sh["lamv"] = np.ascontiguousarray(np.concatenate([f(inp["lam_q1"])[0], f(inp["lam_k1"])[0],
                                                      f(inp["lam_q2"])[0], f(inp["lam_k2"])[0]]).reshape(1, 256))
    cosT, sinT, rmat = _rope_tables()
    sh["ropec"], sh["ropes"], sh["rmat"] = cosT, sinT, rmat
    sh["ident"] = np.eye(128, dtype=np.float32)
    sh["ones"] = np.ones((128, 128), np.float32)
    bo = np.zeros((128, 128), np.float32)
    bo[:64, :64] = 1.0
    bo[64:, 64:] = 1.0
    sh["bones"] = bo
    sh["nab"], sh["nam"] = _na_tables(f(inp["na_rpb"])[0])
    return sh


def make_in_maps(inp):
    sh = _prep_shared(inp)
    x = np.asarray(inp["x"], np.float32)
    ctx = np.asarray(inp["ctx"], np.float32)
    c = np.asarray(inp["c"], np.float32)
    c_ctx = np.asarray(inp["c_ctx"], np.float32)
    maps = []
    for b in range(8):
        m = dict(sh)
        m["x"] = np.ascontiguousarray(x[b])
        m["ctx"] = np.ascontiguousarray(ctx[b])
        m["cc"] = np.ascontiguousarray(np.stack([c[b], c_ctx], 0))
        maps.append(m)
    return maps


_NC_CACHE = {}


def kernel(**inputs):
    if "nc" not in _NC_CACHE:
        _NC_CACHE["nc"] = build()
    nc = _NC_CACHE["nc"]
    maps = make_in_maps(inputs)
    res = run_bass_kernel_spmd(nc, maps, core_ids=list(range(8)))
    return np.stack([np.asarray(r["out"], np.float32) for r in res.results], 0)
```

```python
import math
from contextlib import ExitStack

import numpy as np
import concourse.bass as bass
import concourse.mybir as mybir
from concourse.bass_utils import run_bass_kernel_spmd

F32 = mybir.dt.float32
BF16 = mybir.dt.bfloat16
ALU = mybir.AluOpType
AF = mybir.ActivationFunctionType
AX = mybir.AxisListType

DMA_RING = 8


class _Op:
    __slots__ = ("eng", "fn", "reads", "writes", "dma", "idx", "need", "signal", "seq",
                 "slot", "val", "prewait", "barrier")


class Prog:
    ENG = ("pe", "act", "dve", "pool", "sp")

    def __init__(self):
        self.ops = []

    def add(self, eng, fn, reads=(), writes=(), dma=False):
        op = _Op()
        op.eng, op.fn, op.dma = eng, fn, dma
        rs, ws = list(reads), list(writes)
        for r in list(rs):
            if isinstance(r, tuple) and r[0] == "ps" and r not in ws:
                ws.append(r)
        op.reads, op.writes = tuple(rs), tuple(ws)
        op.idx = len(self.ops)
        op.need = []
        op.signal = False
        op.seq = 0
        op.slot = op.val = op.prewait = None
        op.barrier = False
        self.ops.append(op)
        return op

    def pe(self, fn, reads=(), writes=()):
        return self.add("pe", fn, reads, writes)

    def act(self, fn, reads=(), writes=()):
        return self.add("act", fn, reads, writes)

    def dve(self, fn, reads=(), writes=()):
        return self.add("dve", fn, reads, writes)

    def pool(self, fn, reads=(), writes=()):
        return self.add("pool", fn, reads, writes)

    def dma(self, eng, fn, reads=(), writes=()):
        return self.add(eng, fn, reads, writes, dma=True)

    def barrier(self):
        op = self.add("sp", None)
        op.barrier = True
        return op

    def _analyze(self):
        last_w = {}
        readers = {}
        ops = self.ops
        last_comp = {}
        recent_dma = {e: [] for e in self.ENG}
        bar_deps = []
        for op in ops:
            if op.barrier:
                bar_deps = list(last_comp.values())
                for e in self.ENG:
                    bar_deps += recent_dma[e][-DMA_RING:]
                last_w.clear()
                readers.clear()
                continue
            raw = set()
            other = set()
            for r in op.reads:
                if r in last_w:
                    raw.add(last_w[r])
            for w in op.writes:
                if w in last_w:
                    other.add(last_w[w])
                for rd in readers.get(w, ()):
                    other.add(rd)
            raw.discard(op.idx)
            other.discard(op.idx)
            other -= raw
            need = []
            for p in sorted(raw | other):
                P = ops[p]
                if P.dma or op.dma:
                    need.append(p)
                elif P.eng == op.eng:
                    if P.eng == "pe":
                        continue
                    need.append(p)
                else:
                    need.append(p)
            for p in bar_deps:
                if p not in need and not (ops[p].eng == op.eng and not ops[p].dma and not op.dma
                                          and op.eng == "pe"):
                    need.append(p)
            need.sort()
            op.need = need
            for p in need:
                ops[p].signal = True
            for r in op.reads:
                readers.setdefault(r, []).append(op.idx)
            for w in op.writes:
                last_w[w] = op.idx
                readers[w] = []
            if op.dma:
                recent_dma[op.eng].append(op.idx)
            else:
                last_comp[op.eng] = op.idx
        cnt = {e: 0 for e in self.ENG}
        dcnt = {e: 0 for e in self.ENG}
        for op in ops:
            if op.barrier:
                continue
            if op.dma:
                j = dcnt[op.eng]
                dcnt[op.eng] += 1
                op.slot = j % DMA_RING
                op.val = 16 * (j // DMA_RING + 1)
                op.prewait = 16 * (j // DMA_RING) if j >= DMA_RING else None
            elif op.signal:
                cnt[op.eng] += 1
                op.seq = cnt[op.eng]
        self.counts = cnt
        self.dcounts = dcnt

    def emit(self, nc):
        self._analyze()
        ops = self.ops
        with ExitStack() as es:
            csem = {e: es.enter_context(nc.semaphore("c_" + e)) for e in self.ENG if e != "sp"}
            dsem = {e: [es.enter_context(nc.semaphore("d_%s%d" % (e, i))) for i in range(DMA_RING)]
                    for e in self.ENG if self.dcounts[e] > 0}
            block = es.enter_context(nc.Block())
            for c in self.counts.values():
                assert c < 60000, self.counts

            def run(engname, handle):
                known = {}
                last_dma = {}
                for op in ops:
                    if op.eng != engname or op.barrier:
                        continue
                    waits = []
                    for p in op.need:
                        Pp = ops[p]
                        if Pp.dma:
                            waits.append((dsem[Pp.eng][Pp.slot], Pp.val))
                        else:
                            waits.append((csem[Pp.eng], Pp.seq))
                    if op.dma and op.prewait is not None:
                        waits.append((dsem[op.eng][op.slot], op.prewait))
                    for s, v in waits:
                        k = id(s)
                        if known.get(k, 0) >= v:
                            continue
                        known[k] = v
                        handle.wait_ge(s, v)
                    inst = op.fn(handle)
                    if op.dma:
                        inst.then_inc(dsem[op.eng][op.slot], 16)
                        last_dma[op.slot] = op.val
                    elif op.signal:
                        inst.then_inc(csem[op.eng], 1)
                for slot, v in last_dma.items():
                    if known.get(id(dsem[engname][slot]), 0) < v:
                        handle.wait_ge(dsem[engname][slot], v)

            @block.tensor
            def _(e):
                run("pe", e)

            @block.scalar
            def _(e):
                run("act", e)

            @block.vector
            def _(e):
                run("dve", e)

            @block.gpsimd
            def _(e):
                run("pool", e)

            @block.sync
            def _(e):
                run("sp", e)


D = 1024
S = 2048
C = 256
T = S + C
KC = 8
FF = 2816
NJ = 22
NMOD = 9
EPS = 1e-6
LAM_INIT = 0.8 - 0.6 * math.exp(0.0)
TCH = [(0, 512, 0), (512, 512, 0), (1024, 512, 0), (1536, 512, 0), (2048, 256, 1)]
FGROUPS = [(0, 6), (6, 12), (12, 18), (18, 22)]
NEG = -30000.0


def build(stage=99, dbg=False):
    nc = bass.Bass("TRN2", target_bir_lowering=False)

    def din(name, shape, dt=F32):
        return nc.dram_tensor(name, list(shape), dt, kind="ExternalInput").ap()

    x_d = din("x", [S, D])
    ctx_d = din("ctx", [C, D])
    cc_d = din("cc", [2, D])
    wada_d = din("wada", [18, 128, 4096])
    badaT_d = din("badaT", [128, 72])
    gT_d = din("gT", [128, 24])
    wgu_d = [din("wgu1", [NJ, 128, 2048]), din("wgu2", [NJ, 128, 2048])]
    wd_d = [din("wd1", [FF, D]), din("wd2", [FF, D])]
    win_d = din("win", [8, 128, 3072])
    wout_d = din("wout", [8, 128, 1024])
    qkg_d = din("qkg", [128, 4])
    gout_d = din("gout", [128, 1])
    lamv_d = din("lamv", [1, 256])
    ropec_d = din("ropec", [128, S])
    ropes_d = din("ropes", [128, S])
    rmat_d = din("rmat", [128, 128])
    ident_d = din("ident", [128, 128])
    ones_d = din("ones", [128, 128])
    bones_d = din("bones", [128, 128])
    nab_d = din("nab", [4, 128, 2 * 12 * 128])
    nam_d = din("nam", [128, 12 * 128])
    out_d = nc.dram_tensor("out", [S, D], F32, kind="ExternalOutput").ap()
    dbg_d = {}
    if dbg:
        dbg_d["hT"] = nc.dram_tensor("dbg_hT", [128, KC * T], F32, kind="ExternalOutput").ap()
        dbg_d["mod"] = nc.dram_tensor("dbg_mod", [128, 144], F32, kind="ExternalOutput").ap()
        dbg_d["xn"] = nc.dram_tensor("dbg_xn", [128, KC * T], BF16, kind="ExternalOutput").ap()
        dbg_d["q"] = nc.dram_tensor("dbg_q", [8, 128, S], BF16, kind="ExternalOutput").ap()
        dbg_d["k"] = nc.dram_tensor("dbg_k", [8, 128, T], BF16, kind="ExternalOutput").ap()
        dbg_d["v"] = nc.dram_tensor("dbg_v", [8, 128, 18 * 132], BF16, kind="ExternalOutput").ap()
        dbg_d["o"] = nc.dram_tensor("dbg_o", [8, 128, S], BF16, kind="ExternalOutput").ap()

    P = Prog()
    with ExitStack() as es:
        def sb(name, shape, dt):
            return es.enter_context(nc.sbuf_tensor("s_" + name, list(shape), dt))

        hT = sb("hT", [128, KC, T], F32)
        xnT = sb("xnT", [128, KC, T], BF16)
        ident = sb("ident", [128, 128], F32)
        ones_b = sb("ones_b", [128, 128], BF16)
        bones_b = sb("bones_b", [128, 128], BF16)
        rmat_b = sb("rmat_b", [128, 128], BF16)
        ident_b = sb("ident_b", [128, 128], BF16)
        modT = sb("modT", [128, 72, 2], F32)
        Asc = sb("Asc", [128, 3, KC, 2], F32)
        Gsc = sb("Gsc", [128, 3, KC, 2], F32)
        gT = sb("gT", [128, 24], F32)
        qkg = sb("qkg", [128, 4], F32)
        gout = sb("gout", [128, 1], F32)
        epst = sb("epst", [128, 1], F32)
        neglam = sb("neglam", [128, 1], F32)
        rstd = sb("rstd", [128, 512], F32)
        lnv = sb("lnv", [128, 512], F32)
        tmpA = [sb("tmpA%d" % i, [128, 512], F32) for i in range(2)]
        ARENA = 84 * 1024
        arena = sb("arena", [128, ARENA // 2], BF16)
        ps = es.enter_context(nc.psum_tensor("ps", [128, 8, 512], F32))
        psb = ps[:, 3, :].bitcast(BF16)

        def carve(off, shape, dt):
            n = int(np.prod(shape[1:]))
            if dt == F32:
                assert off % 4 == 0
                v = arena[:, off // 2: off // 2 + 2 * n].bitcast(F32)
            else:
                v = arena[:, off // 2: off // 2 + n]
            if len(shape) == 2:
                return v
            names = "abcd"[: len(shape) - 1]
            pat = "p (" + " ".join(names) + ") -> p " + " ".join(names)
            kw = {names[i]: shape[i + 1] for i in range(len(shape) - 1)}
            return v.rearrange(pat, **kw)

        PS = lambda b: ("ps", b)
        PSB = ("ps", 3)

        P.dma("sp", lambda e: e.dma_start(out=ident[:], in_=ident_d), writes=["ident"])
        P.dma("pool", lambda e: e.dma_start(out=ones_b[:], in_=ones_d), writes=["ones_b"])
        P.dma("pool", lambda e: e.dma_start(out=bones_b[:], in_=bones_d), writes=["bones_b"])
        P.dma("pool", lambda e: e.dma_start(out=rmat_b[:], in_=rmat_d), writes=["rmat_b"])
        P.dma("pool", lambda e: e.dma_start(out=ident_b[:], in_=ident_d), writes=["ident_b"])
        P.dma("sp", lambda e: e.dma_start(out=gT[:], in_=gT_d), writes=["gT"])
        P.dma("sp", lambda e: e.dma_start(out=qkg[:], in_=qkg_d), writes=["qkg"])
        P.dma("sp", lambda e: e.dma_start(out=gout[:], in_=gout_d), writes=["gout"])
        P.dve(lambda e: e.memset(epst[:], EPS), writes=["epst"])

        xin = [carve(8192 + i * 4096, [128, 1024], F32) for i in range(2)]
        cc_sb = carve(16384, [128, 1024], F32)
        sc_sb = carve(20480, [128, 1024], F32)
        lam_sb = carve(24576, [128, 256], F32)
        lam_t = carve(25600, [128, 16], F32)
        ones_row = carve(25664, [128, 128], F32)
        PA = 64512
        scT = carve(PA, [128, 8, 2], BF16)
        mod_blk = carve(PA + 256, [128, 512], F32)
        wada_ring = [carve(PA + 256 + 2048 + i * 8192, [128, 8, 512], BF16) for i in range(2)]
        badaT = sb("badaT", [128, 72], F32)

        P.dma("sp", lambda e: e.dma_start(out=cc_sb[0:2, :], in_=cc_d), writes=["cc"])
        P.dma("sp", lambda e: e.dma_start(out=badaT[:], in_=badaT_d), writes=["badaT"])
        P.dma("sp", lambda e: e.dma_start(out=lam_sb[0:1, :], in_=lamv_d), writes=["lamv"])
        P.act(lambda e: e.activation(sc_sb[0:2, :], cc_sb[0:2, :], AF.Silu), reads=["cc"], writes=["sc"])
        for kc in range(KC):
            P.pe(lambda e, kc=kc: e.matmul(ps[:, 0, 2 * kc:2 * kc + 2], sc_sb[0:2, kc * 128:(kc + 1) * 128],
                                           ident[0:2, 0:2], start=True, stop=True),
                 reads=["sc", "ident"], writes=[PS(0)])
        P.dve(lambda e: e.tensor_copy(scT.rearrange("p a b -> p (a b)"), ps[:, 0, 0:16]),
              reads=[PS(0)], writes=["scT"])

        def wada_load(n):
            wr = wada_ring[n % 2]
            P.dma("pool", lambda e: e.dma_start(out=wr.rearrange("p a b -> p (a b)"), in_=wada_d[n]),
                  writes=[("wada", n % 2)])

        def wada_block(n, acc_bank, tr_bank):
            wr = wada_ring[n % 2]
            for kc in range(KC):
                P.pe(lambda e, kc=kc: e.matmul(ps[0:2, acc_bank, :], scT[:, kc, :], wr[:, kc, :],
                                               start=(kc == 0), stop=(kc == KC - 1)),
                     reads=["scT", ("wada", n % 2)], writes=[PS(acc_bank)])
            P.dve(lambda e: e.tensor_copy(mod_blk[0:2, :], ps[0:2, acc_bank, :]),
                  reads=[PS(acc_bank)], writes=["modb"])
            for q in range(4):
                i = 4 * n + q
                P.pe(lambda e, i=i, q=q: e.matmul(ps[:, tr_bank, 2 * i:2 * i + 2], mod_blk[0:2, q * 128:(q + 1) * 128],
                                                  ident[0:2, 0:2], start=True, stop=True),
                     reads=["modb", "ident"], writes=[PS(tr_bank)])

        def mods_finish(i_lo, i_hi, tr_bank, subs, do_a=True, do_g=True):
            for v in range(2):
                P.dve(lambda e, v=v: e.tensor_tensor(
                    out=modT[:, i_lo:i_hi, v], in0=ps[:, tr_bank, 2 * i_lo:2 * i_hi].rearrange("p (a b) -> p a b", b=2)[:, :, v],
                    in1=badaT[:, i_lo:i_hi], op=ALU.add),
                    reads=[PS(tr_bank), "badaT"], writes=["modT"])
            for s_ in subs:
                for v in range(2):
                    if do_a:
                        P.dve(lambda e, s_=s_, v=v: e.scalar_tensor_tensor(
                            out=Asc[:, s_, :, v], in0=modT[:, (3 * s_ + 1) * 8:(3 * s_ + 2) * 8, v], scalar=1.0,
                            in1=gT[:, s_ * 8:(s_ + 1) * 8], op0=ALU.add, op1=ALU.mult),
                            reads=["modT", "gT"], writes=["Asc"])
                if do_g:
                    P.dve(lambda e, s_=s_: e.tensor_scalar(
                        out=Gsc[:, s_, :, :], in0=modT[:, (3 * s_ + 2) * 8:(3 * s_ + 3) * 8, :],
                        scalar1=(1.0 if s_ == 1 else 0.5), scalar2=None, op0=ALU.mult),
                        reads=["modT"], writes=["Gsc"])

        def x_tile(i):
            slot = i % 2
            src = x_d[i * 128:(i + 1) * 128, :] if i < 16 else ctx_d[(i - 16) * 128:(i - 15) * 128, :]
            P.dma("sp", lambda e: e.dma_start(out=xin[slot][:], in_=src), writes=["xin%d" % slot])
            for half in range(2):
                bank = 4 + 2 * (i % 2) + half
                for q in range(4):
                    kc = half * 4 + q
                    P.pe(lambda e, kc=kc, bank=bank, q=q: e.transpose(
                        ps[:, bank, q * 128:(q + 1) * 128], xin[slot][:, kc * 128:(kc + 1) * 128], ident[:]),
                        reads=["xin%d" % slot, "ident"], writes=[PS(bank)])
                dst = hT[:, half * 4:half * 4 + 4, i * 128:(i + 1) * 128]
                srcp = ps[:, bank, :].rearrange("p (a b) -> p a b", a=4)
                if half == 0:
                    P.act(lambda e, dst=dst, srcp=srcp: e.activation(dst, srcp, AF.Copy),
                          reads=[PS(bank)], writes=[("hT", i)])
                else:
                    P.dve(lambda e, dst=dst, srcp=srcp: e.tensor_copy(dst, srcp),
                          reads=[PS(bank)], writes=[("hT", i)])

        wada_load(0)
        wada_load(1)
        xsplit = [0, 5, 10, 14, 18]
        for n in range(4):
            for i in range(xsplit[n], xsplit[n + 1]):
                x_tile(i)
            wada_block(n, 1 + (n % 2), 3)
            if n + 2 < 18:
                wada_load(n + 2)
        mods_finish(0, 16, 3, [0], do_g=False)
        Bsc = lambda s, kc, v: modT[:, 3 * s * 8 + kc, v:v + 1]
        P.dve(lambda e: e.memset(ones_row[0:1, :], 1.0), writes=["ones_row"])
        P.dve(lambda e: e.tensor_tensor(out=lam_sb[0:1, 0:64], in0=lam_sb[0:1, 0:64],
                                        in1=lam_sb[0:1, 64:128], op=ALU.mult), reads=["lamv"], writes=["lamv"])
        P.dve(lambda e: e.tensor_tensor(out=lam_sb[0:1, 128:192], in0=lam_sb[0:1, 128:192],
                                        in1=lam_sb[0:1, 192:256], op=ALU.mult), reads=["lamv"], writes=["lamv"])
        P.dve(lambda e: e.reduce_sum(out=lam_t[0:1, 0:1], in_=lam_sb[0:1, 0:64], axis=AX.X),
              reads=["lamv"], writes=["lam_t"])
        P.dve(lambda e: e.reduce_sum(out=lam_t[0:1, 1:2], in_=lam_sb[0:1, 128:192], axis=AX.X),
              reads=["lamv", "lam_t"], writes=["lam_t"])
        P.act(lambda e: e.activation(lam_t[0:1, 2:4], lam_t[0:1, 0:2], AF.Exp), reads=["lam_t"], writes=["lam_t"])
        P.dve(lambda e: e.scalar_tensor_tensor(out=lam_t[0:1, 4:5], in0=lam_t[0:1, 3:4], scalar=-LAM_INIT,
                                               in1=lam_t[0:1, 2:3], op0=ALU.add, op1=ALU.subtract),
              reads=["lam_t"], writes=["lam_t2"])
        P.pe(lambda e: e.matmul(ps[:, 0, 0:1], ones_row[0:1, :], lam_t[0:1, 4:5], start=True, stop=True),
             reads=["lam_t2", "ones_row"], writes=[PS(0)])
        P.dve(lambda e: e.tensor_copy(neglam[:], ps[:, 0, 0:1]), reads=[PS(0)], writes=["neglam"])

        def adaln_bg_items():
            items = []
            for n in range(4, 18):
                def item(n=n):
                    wada_block(n, 7, 6)
                    if n + 2 < 18:
                        wada_load(n + 2)
                    if n == 5:
                        mods_finish(16, 24, 6, [0], do_a=False)
                    if n == 17:
                        mods_finish(24, 72, 6, [1, 2])
                        if dbg:
                            P.dma("sp", lambda e: e.dma_start(out=dbg_d["mod"], in_=modT.rearrange("p a b -> p (a b)")),
                                  reads=["modT"])
                items.append(item)
            return items

        def norm_phase(s, chunks, sq, bank=6, toff=8192, ci0=0):
            ntmp = [carve(toff + i * 2048, [128, 512], F32) for i in range(4)]
            for ci_, (t0, L, v) in enumerate(chunks):
                ci = ci_ + ci0
                tiles = [("hT", i) for i in range(t0 // 128, (t0 + L) // 128)]
                for kc in range(KC):
                    if kc % 2 == 0:
                        P.act(lambda e, kc=kc, t0=t0, L=L: e.activation(sq[:, kc, 0:L], hT[:, kc, t0:t0 + L], AF.Square),
                              reads=tiles, writes=[("sq", kc)])
                    else:
                        P.dve(lambda e, kc=kc, t0=t0, L=L: e.tensor_tensor(out=sq[:, kc, 0:L], in0=hT[:, kc, t0:t0 + L],
                                                                            in1=hT[:, kc, t0:t0 + L], op=ALU.mult),
                              reads=tiles, writes=[("sq", kc)])
                for kc in range(KC):
                    P.pe(lambda e, kc=kc, L=L: e.matmul(ps[:, bank, 0:L], ones_b[:], sq[:, kc, 0:L],
                                                        start=(kc == 0), stop=(kc == KC - 1)),
                         reads=[("sq", kc), "ones_b"], writes=[PS(bank)])
                P.act(lambda e, L=L: e.activation(lnv[:, 0:L], ps[:, bank, 0:L], AF.Ln, bias=epst[:, 0:1], scale=1.0 / D),
                      reads=[PS(bank), "epst"], writes=["lnv"])
                P.act(lambda e, L=L: e.activation(rstd[:, 0:L], lnv[:, 0:L], AF.Exp, scale=-0.5),
                      reads=["lnv"], writes=["rstd"])
                for kc in range(KC):
                    tb = ntmp[kc % 4]
                    P.dve(lambda e, kc=kc, t0=t0, L=L, tb=tb, v=v: e.scalar_tensor_tensor(
                        out=tb[:, 0:L], in0=hT[:, kc, t0:t0 + L], scalar=Asc[:, s, kc, v:v + 1], in1=rstd[:, 0:L],
                        op0=ALU.mult, op1=ALU.mult),
                        reads=tiles + ["rstd", "Asc"], writes=[("ntmp", kc % 4)])
                    P.act(lambda e, kc=kc, t0=t0, L=L, tb=tb, v=v: e.activation(
                        xnT[:, kc, t0:t0 + L], tb[:, 0:L], AF.Identity, bias=Bsc(s, kc, v), scale=1.0),
                        reads=[("ntmp", kc % 4), "modT"], writes=[("xn", ci)])

        def ffn_phase(s, chunks, wgu, wd, aoff, final=None, bgitems=None):
            actT = carve(aoff, [128, 6, T], BF16)
            wgu_ring = [carve(aoff + 27648 + i * 4096, [128, 8, 256], BF16) for i in range(4)]
            wd_ring = [carve(aoff + 27648 + 16384 + i * 2048, [128, 1024], BF16) for i in range(8)]
            sg = [carve(aoff + 27648 + 16384 + 16384 + i * 2048, [128, 512], F32) for i in range(2)]
            nci = len(chunks)
            cnt = [0]
            dcnt = [0]

            def load_wgu(j):
                P.dma("pool", lambda e, j=j: e.dma_start(out=wgu_ring[j % 4].rearrange("p a b -> p (a b)"), in_=wgu[j]),
                      writes=[("wgu", j % 4)])

            def load_wd(j):
                P.dma("pool", lambda e, j=j: e.dma_start(out=wd_ring[j % 8][:], in_=wd[j * 128:(j + 1) * 128, :]),
                      writes=[("wd", j % 8)])

            for j in range(3):
                load_wgu(j)
            for j in range(6):
                load_wd(j)
            for (j0, j1) in FGROUPS:
                for j in range(j0, j1):
                    if j + 3 < NJ:
                        load_wgu(j + 3)
                    if bgitems:
                        bgitems.pop(0)()
                    w = wgu_ring[j % 4]
                    for ci, (t0, L, v) in enumerate(chunks):
                        bg = 2 * (cnt[0] % 2)
                        bu = bg + 1
                        cnt[0] += 1
                        for kc in range(KC):
                            P.pe(lambda e, w=w, kc=kc, t0=t0, L=L, bg=bg: e.matmul(
                                ps[:, bg, 0:L], w[:, kc, 0:128], xnT[:, kc, t0:t0 + L],
                                start=(kc == 0), stop=(kc == KC - 1)),
                                reads=[("wgu", j % 4), ("xn", ci)], writes=[PS(bg)])
                        for kc in range(KC):
                            P.pe(lambda e, w=w, kc=kc, t0=t0, L=L, bu=bu: e.matmul(
                                ps[:, bu, 0:L], w[:, kc, 128:256], xnT[:, kc, t0:t0 + L],
                                start=(kc == 0), stop=(kc == KC - 1)),
                                reads=[("wgu", j % 4), ("xn", ci)], writes=[PS(bu)])
                        sgb = sg[cnt[0] % 2]
                        P.act(lambda e, sgb=sgb, bg=bg, L=L: e.activation(sgb[:, 0:L], ps[:, bg, 0:L], AF.Silu),
                              reads=[PS(bg)], writes=[("sg", cnt[0] % 2)])
                        P.dve(lambda e, sgb=sgb, bu=bu, L=L, jj=j - j0, t0=t0: e.tensor_tensor(
                            out=actT[:, jj, t0:t0 + L], in0=sgb[:, 0:L], in1=ps[:, bu, 0:L], op=ALU.mult),
                            reads=[("sg", cnt[0] % 2), PS(bu)], writes=[("act", j - j0, ci)])
                ng = j1 - j0
                last_group = (j1 == NJ) and final is not None
                order = ([(m, ci) for ci in range(len(chunks)) for m in range(KC)] if last_group
                         else [(m, ci) for m in range(KC) for ci in range(len(chunks))])
                for (m, ci) in order:
                    if True:
                        t0, L, v = chunks[ci]
                        bd = 4 + (dcnt[0] % 2)
                        dcnt[0] += 1
                        tiles = [("hT", i) for i in range(t0 // 128, (t0 + L) // 128)]
                        for jj in range(ng):
                            j = j0 + jj
                            P.pe(lambda e, j=j, jj=jj, m=m, t0=t0, L=L, bd=bd, ng=ng: e.matmul(
                                ps[:, bd, 0:L], wd_ring[j % 8][:, m * 128:(m + 1) * 128], actT[:, jj, t0:t0 + L],
                                start=(jj == 0), stop=(jj == ng - 1)),
                                reads=[("wd", j % 8), ("act", jj, ci)], writes=[PS(bd)])
                        P.dve(lambda e, m=m, t0=t0, L=L, bd=bd, v=v: e.scalar_tensor_tensor(
                            out=hT[:, m, t0:t0 + L], in0=ps[:, bd, 0:L], scalar=Gsc[:, s, m, v:v + 1],
                            in1=hT[:, m, t0:t0 + L], op0=ALU.mult, op1=ALU.add),
                            reads=[PS(bd), "Gsc"] + tiles, writes=tiles)
                        if last_group and m == KC - 1:
                            final(ci)
                nxt = [g for g in FGROUPS if g[0] == j1]
                if nxt:
                    for j in range(max(nxt[0][0], 6), nxt[0][1]):
                        load_wd(j)

        def dump_hT():
            P.dma("sp", lambda e: e.dma_start(out=dbg_d["hT"], in_=hT.rearrange("p a b -> p (a b)")),
                  reads=[("hT", i) for i in range(18)])

        def write_out(tiles=range(16), banks=(0, 1, 2, 3), ooff=0):
            osb = [carve(ooff + i * 4096, [128, 1024], F32) for i in range(2)]
            for i in tiles:
                slot = i % 2
                for half in range(2):
                    bank = banks[(2 * (i % 2) + half) % len(banks)]
                    for q in range(4):
                        kc = half * 4 + q
                        P.pe(lambda e, kc=kc, bank=bank, q=q, i=i: e.transpose(
                            ps[:, bank, q * 128:(q + 1) * 128], hT[:, kc, i * 128:(i + 1) * 128], ident[:]),
                            reads=[("hT", i), "ident"], writes=[PS(bank)])
                    dst = osb[slot][:, half * 512:(half + 1) * 512]
                    if half == 0:
                        P.act(lambda e, dst=dst, bank=bank: e.activation(dst, ps[:, bank, :], AF.Copy),
                              reads=[PS(bank)], writes=[("osb", slot, half)])
                    else:
                        P.dve(lambda e, dst=dst, bank=bank: e.tensor_copy(dst, ps[:, bank, :]),
                              reads=[PS(bank)], writes=[("osb", slot, half)])
                P.dma("sp", lambda e, i=i, slot=slot: e.dma_start(out=out_d[i * 128:(i + 1) * 128, :], in_=osb[slot][:]),
                      reads=[("osb", slot, 0), ("osb", slot, 1)])

        P.barrier()
        sq1 = carve(0, [128, 8, 512], BF16)
        norm_phase(0, TCH, sq1)
        if dbg:
            P.dma("sp", lambda e: e.dma_start(out=dbg_d["xn"], in_=xnT.rearrange("p a b -> p (a b)")),
                  reads=[("xn", i) for i in range(5)])
        P.barrier()
        if stage >= 2:
            bg1 = adaln_bg_items()
            if stage > 2:
                sq2 = carve(PA, [128, 8, 512], BF16)
                ffn_phase(0, TCH, wgu_d[0], wd_d[0], 0, bgitems=bg1,
                          final=lambda ci: norm_phase(1, [TCH[ci]], sq2, toff=PA + 8192, ci0=ci))
            else:
                ffn_phase(0, TCH, wgu_d[0], wd_d[0], 0, bgitems=bg1)
            assert not bg1
        if stage <= 2:
            P.barrier()
            if dbg:
                dump_hT()
            write_out()
            P.emit(nc)
            return nc

        P.barrier()

        class Bump:
            def __init__(self):
                self.off = 0

            def __call__(self, shape, dt):
                n = int(np.prod(shape[1:])) * (4 if dt == F32 else 2)
                self.off = (self.off + 3) // 4 * 4
                v = carve(self.off, shape, dt)
                self.off += n
                assert self.off <= ARENA, self.off
                return v

        def chunk_of_tile(i):
            return i // 4 if i < 16 else 4

        LAT = TCH[:4]
        YB = 2
        SCR = 3


        def qk_items(slot, wsl_, wcol, dst, dname, chunks, gcol, rope, sqb_, qn, qnb, t1, ropec, ropes,
                     ybs=(2,), scrs=(3,), lnvs=None, rstds=None, grouped=False, roff=0):
            lnvs = lnvs or [(lnv, "lnv")]
            rstds = rstds or [(rstd, "rstd")]
            items = []
            groups = []
            for ci, (t0, L, v) in enumerate(chunks):
                ri = ci + roff
                sb_ = sqb_[ri % 2]
                sqk = ("sqq", ri % 2)
                roped = rope and v == 0
                yb = ybs[ri % len(ybs)]
                scr = scrs[ri % len(scrs)]
                lnv_, lnk = lnvs[ri % len(lnvs)]
                rstd_, rsk = rstds[ri % len(rstds)]

                def stage_a(ci=ci, t0=t0, L=L, sb_=sb_, yb=yb, sqk=sqk):
                    for kc in range(KC):
                        P.pe(lambda e, kc=kc: e.matmul(
                            ps[:, yb, 0:L], wsl_[:, kc, wcol:wcol + 128], xnT[:, kc, t0:t0 + L],
                            start=(kc == 0), stop=(kc == KC - 1)),
                            reads=[("win", slot), ("xn", ci)], writes=[PS(yb)])
                    P.act(lambda e: e.activation(sb_[:, 0:L], ps[:, yb, 0:L], AF.Square),
                          reads=[PS(yb)], writes=[sqk])

                def stage_b(ci=ci, t0=t0, L=L, sb_=sb_, roped=roped, yb=yb, scr=scr, lnv_=lnv_, lnk=lnk,
                            rstd_=rstd_, rsk=rsk, sqk=sqk):
                    P.pe(lambda e: e.matmul(ps[:, scr, 0:L], bones_b[:], sb_[:, 0:L], start=True, stop=True),
                         reads=[sqk, "bones_b"], writes=[PS(scr)])
                    P.act(lambda e: e.activation(lnv_[:, 0:L], ps[:, scr, 0:L], AF.Ln, bias=epst[:, 0:1], scale=1.0 / 64),
                          reads=[PS(scr), "epst"], writes=[lnk])
                    P.act(lambda e: e.activation(rstd_[:, 0:L], lnv_[:, 0:L], AF.Exp, scale=-0.5),
                          reads=[lnk], writes=[rsk])
                    if not roped:
                        P.dve(lambda e: e.scalar_tensor_tensor(
                            out=dst[:, t0:t0 + L], in0=ps[:, yb, 0:L], scalar=qkg[:, gcol:gcol + 1], in1=rstd_[:, 0:L],
                            op0=ALU.mult, op1=ALU.mult),
                            reads=[PS(yb), rsk, "qkg"], writes=[(dname, slot, ci)])
                    else:
                        P.dve(lambda e: e.scalar_tensor_tensor(
                            out=qn[:, 0:L], in0=ps[:, yb, 0:L], scalar=qkg[:, gcol:gcol + 1], in1=rstd_[:, 0:L],
                            op0=ALU.mult, op1=ALU.mult),
                            reads=[PS(yb), rsk, "qkg"], writes=["qn"])
                        P.dve(lambda e: e.tensor_copy(qnb[:, 0:L], qn[:, 0:L]), reads=["qn"], writes=["qnb"])
                        P.pool(lambda e: e.tensor_tensor(out=t1[:, 0:L], in0=qn[:, 0:L], in1=ropec[:, t0:t0 + L],
                                                         op=ALU.mult), reads=["qn", "ropec"], writes=["t1"])

                def stage_c(ci=ci, t0=t0, L=L, scr=scr):
                    P.pe(lambda e: e.matmul(ps[:, scr, 0:L], rmat_b[:], qnb[:, 0:L], start=True, stop=True),
                         reads=["qnb", "rmat_b"], writes=[PS(scr)])
                    P.dve(lambda e: e.tensor_tensor(out=tmpA[0][:, 0:L], in0=ps[:, scr, 0:L],
                                                    in1=ropes[:, t0:t0 + L], op=ALU.mult),
                          reads=[PS(scr), "ropes"], writes=[("tmpA", 0)])
                    P.pool(lambda e: e.tensor_tensor(out=dst[:, t0:t0 + L], in0=t1[:, 0:L],
                                                     in1=tmpA[0][:, 0:L], op=ALU.add),
                           reads=["t1", ("tmpA", 0)], writes=[(dname, slot, ci)])
                g = [stage_a, stage_b] + ([stage_c] if roped else [])
                groups.append(g)
                items += g
            return groups if grouped else items

        def emit_pipelined(groups):
            n = len(groups)
            for i in range(n + 1):
                if i < n:
                    groups[i][0]()
                if i >= 1:
                    for st in groups[i - 1][1:]:
                        st()

        def v_items(slot, wsl_, Vt, na):
            items = []
            for g in range(5):
                def item(g=g):
                    tl = list(range(4 * g, min(4 * g + 4, 18)))
                    for ti, i in enumerate(tl):
                        for kc in range(KC):
                            P.pe(lambda e, kc=kc, i=i, ti=ti: e.matmul(
                                ps[:, YB, ti * 128:(ti + 1) * 128], xnT[:, kc, i * 128:(i + 1) * 128],
                                wsl_[:, kc, 256:384], start=(kc == 0), stop=(kc == KC - 1)),
                                reads=[("win", slot), ("xn", chunk_of_tile(i))], writes=[PS(YB)])
                    n = len(tl)
                    if na:
                        dstv = Vt[:, 4 * g:4 * g + n, :, 0:64]
                        srcv = ps[:, YB, 0:n * 128].rearrange("p (a h d) -> p a h d", a=n, h=2)
                    else:
                        dstv = Vt[:, 4 * g:4 * g + n, 0:128]
                        srcv = ps[:, YB, 0:n * 128].rearrange("p (a d) -> p a d", a=n)
                    P.dve(lambda e: e.tensor_copy(dstv, srcv), reads=[PS(YB)], writes=[("V", slot, g)])
                items.append(item)
            return items

        def wout_items(slot, woutb_, oTt, banks=(2,)):
            items = []
            for m in range(KC):
                for t in range(4):
                    def item(m=m, t=t, yb=banks[(m * 4 + t) % len(banks)]):
                        P.pe(lambda e: e.matmul(
                            ps[:, yb, :], woutb_[:, m * 128:(m + 1) * 128], oTt[:, t * 512:(t + 1) * 512],
                            start=True, stop=True),
                            reads=[("wout", slot), ("oT", slot, t)], writes=[PS(yb)])
                        tiles = [("hT", i) for i in range(4 * t, 4 * t + 4)]
                        eng = P.dve if (m * 4 + t) % 3 != 2 or len(banks) == 1 else P.pool
                        if len(banks) == 1 or (m * 4 + t) % 3 != 2:
                            P.dve(lambda e: e.scalar_tensor_tensor(
                                out=hT[:, m, t * 512:(t + 1) * 512], in0=ps[:, yb, :], scalar=Gsc[:, 1, m, 0:1],
                                in1=hT[:, m, t * 512:(t + 1) * 512], op0=ALU.mult, op1=ALU.add),
                                reads=[PS(yb), "Gsc"] + tiles, writes=tiles)
                        else:
                            P.act(lambda e: e.activation(tmpA[1][:, 0:512], ps[:, yb, :], AF.Copy, scale=Gsc[:, 1, m, 0:1]),
                                  reads=[PS(yb), "Gsc"], writes=[("tmpA", 1)])
                            P.pool(lambda e: e.tensor_tensor(out=hT[:, m, t * 512:(t + 1) * 512], in0=tmpA[1][:, 0:512],
                                                             in1=hT[:, m, t * 512:(t + 1) * 512], op=ALU.add),
                                   reads=[("tmpA", 1)] + tiles, writes=tiles)
                    items.append(item)
            return items

        def run_with_bg(nsteps_, step_fn, bg, tail=0):
            nb = len(bg)
            done = 0
            for i in range(nsteps_ + tail):
                want = min((nb * (i + 1) + nsteps_ - 1) // nsteps_, nb) if i < nsteps_ else done
                step_fn(i, want - done)
                if i < nsteps_:
                    while done < min(want, nb):
                        bg[done]()
                        done += 1
            while done < nb:
                bg[done]()
                done += 1

        A = Bump()
        ropec = A([128, S], F32)
        ropes = A([128, S], F32)
        wsl = [A([128, 8, 384], BF16) for _ in range(2)]
        qT = [A([128, S], BF16) for _ in range(2)]
        kT = [A([128, T], BF16) for _ in range(2)]
        Vd = [A([128, 18, 132], BF16) for _ in range(2)]
        oT = [A([128, S], BF16) for _ in range(2)]
        woutb = [A([128, 1024], BF16) for _ in range(2)]
        sqb = [A([128, 512], BF16) for _ in range(2)]
        qn = A([128, 512], F32)
        qnb = A([128, 512], BF16)
        t1 = A([128, 512], F32)
        pT = [A([128, 512], BF16) for _ in range(3)]
        o_f = A([128, 2, 128], F32)
        o_b = A([128, 2, 128], BF16)
        rz = A([128, 2, 2, 1], F32)
        rzl = A([128, 2, 1], F32)
        ssq = A([128, 2], F32)
        rs4 = A([128, 2], F32)
        gout8 = A([128, 1], F32)
        accs = A([128, 2, 260], F32)

        P.dma("sp", lambda e: e.dma_start(out=ropec, in_=ropec_d), writes=["ropec"])
        P.dma("sp", lambda e: e.dma_start(out=ropes, in_=ropes_d), writes=["ropes"])
        P.dve(lambda e: e.tensor_scalar(out=gout8, in0=gout[:], scalar1=1.0 - LAM_INIT, scalar2=None, op0=ALU.mult),
              reads=["gout"], writes=["gout8"])
        for sl in range(2):
            P.dve(lambda e, sl=sl: e.memset(Vd[sl][:, :, 128:129], 1.0), writes=[("Vones", sl)])

        n_diff = 4 if stage >= 3 else 0

        def diff_load(u):
            slot = u % 2
            P.dma("pool", lambda e: e.dma_start(out=wsl[slot].rearrange("p a b -> p (a b)"), in_=win_d[u]),
                  writes=[("win", slot)])

        def diff_load_wout(u):
            slot = u % 2
            P.dma("pool", lambda e: e.dma_start(out=woutb[slot], in_=wout_d[u]), writes=[("wout", slot)])

        def diff_proj_items(u):
            slot = u % 2
            return (qk_items(slot, wsl[slot], 128, kT[slot], "kT", TCH, 1, True, sqb, qn, qnb, t1, ropec, ropes)
                    + v_items(slot, wsl[slot], Vd[slot], False)
                    + qk_items(slot, wsl[slot], 0, qT[slot], "qT", LAT, 0, True, sqb, qn, qnb, t1, ropec, ropes))

        NFILL = 2

        def diff_attention(u, bg):
            slot = u % 2
            steps = [(qc, kt) for qc in range(8) for kt in range(18)]
            nst = len(steps)
            SPAIR = [4, 6]
            deferred = {}

            def emit_S(i):
                qc, kt = steps[i]
                b0 = SPAIR[i % 2]
                kci = chunk_of_tile(kt)
                for c in range(2):
                    P.pe(lambda e, c=c: e.matmul(
                        ps[:, b0 + c, 0:256], kT[slot][c * 64:(c + 1) * 64, kt * 128:(kt + 1) * 128],
                        qT[slot][c * 64:(c + 1) * 64, qc * 256:(qc + 1) * 256], start=True, stop=True),
                        reads=[("kT", slot, kci), ("qT", slot, qc // 2)], writes=[PS(b0 + c)])
                pi = i % 3
                P.act(lambda e: e.activation(pT[pi].rearrange("p (a b) -> p a b", a=2),
                                             ps[:, b0:b0 + 2, 0:256], AF.Exp, scale=0.125),
                      reads=[PS(b0), PS(b0 + 1)], writes=[("pT", pi)])

            def emit_PV(i, nfill=2):
                qc, kt = steps[i]
                ab = (0, 1)
                pi = i % 3
                if NFILL and kt != 0:
                    for f_ in range(nfill):
                        P.pe(lambda e, f_=f_: e.matmul(ps[:, ab[f_ % 2], 264:512], ident_b[:], xnT[:, f_, 0:248],
                                                       start=False, stop=False),
                             reads=["ident_b"], writes=[PS(ab[f_ % 2])])
                for c in range(2):
                    bk = ab[c]
                    for qs in range(2):
                        col = qs * 130
                        P.pe(lambda e, qs=qs, c=c, bk=bk, col=col: e.matmul(
                            ps[:, bk, col:col + 129], pT[pi][:, c * 256 + qs * 128:c * 256 + (qs + 1) * 128],
                            Vd[slot][:, kt, 0:129], start=(kt == 0 and qs == 0), stop=(kt == 17 and qs == 1)),
                            reads=[("pT", pi), ("V", slot, kt // 4), ("Vones", slot)], writes=[PS(bk)])
                if kt != 17:
                    return
                P.act(lambda e: e.activation(accs[:, 0, :], ps[:, ab[0], 0:260], AF.Copy),
                      reads=[PS(ab[0])], writes=[("accs", 0)])
                P.dve(lambda e: e.tensor_copy(accs[:, 1, :], ps[:, ab[1], 0:260]),
                      reads=[PS(ab[1])], writes=[("accs", 1)])
                for c in range(2):
                    accv = accs[:, c, :].rearrange("p (a b) -> p a b", a=2)
                    P.dve(lambda e, c=c, accv=accv: e.reciprocal(rz[:, c, 0:2, :], accv[:, :, 128:129]),
                          reads=[("accs", c)], writes=[("rz", c)])
                P.dve(lambda e: e.tensor_scalar(out=rzl[:, 0:2, :], in0=rz[:, 1, 0:2, :], scalar1=neglam[:, 0:1],
                                                scalar2=None, op0=ALU.mult),
                      reads=[("rz", 1), "neglam"], writes=["rzl"])
                for qs in range(2):
                    col = qs * 130
                    P.dve(lambda e, qs=qs, col=col: e.tensor_scalar(
                        out=o_f[:, qs, :], in0=accs[:, 0, col:col + 128], scalar1=rz[:, 0, qs, :], scalar2=None,
                        op0=ALU.mult),
                        reads=[("accs", 0), ("rz", 0)], writes=[("o_f", qs)])
                    P.dve(lambda e, qs=qs, col=col: e.scalar_tensor_tensor(
                        out=o_f[:, qs, :], in0=accs[:, 1, col:col + 128], scalar=rzl[:, qs, :],
                        in1=o_f[:, qs, :], op0=ALU.mult, op1=ALU.add),
                        reads=[("accs", 1), "rzl", ("o_f", qs)], writes=[("o_f", qs)])
                def part_a2():
                    P.dve(lambda e: e.memset(ssq, 0.0), writes=["ssq"])
                    for qs in range(2):
                        P.act(lambda e, qs=qs: e.activation(tmpA[1][:, 0:128], o_f[:, qs, :], AF.Square,
                                                            accum_out=ssq[:, qs:qs + 1]),
                              reads=[("o_f", qs), "ssq"], writes=["ssq", ("tmpA", 1)])
                    P.act(lambda e: e.activation(rs4[:, 0:2], ssq[:, 0:2], AF.Ln, bias=epst[:, 0:1], scale=1.0 / 128),
                          reads=["ssq", "epst"], writes=["rs4"])
                    P.act(lambda e: e.activation(rs4[:, 0:2], rs4[:, 0:2], AF.Exp, scale=-0.5), reads=["rs4"], writes=["rs4"])

                def part_a3():
                    for qs in range(2):
                        P.dve(lambda e, qs=qs: e.tensor_scalar(out=o_b[:, qs, :], in0=o_f[:, qs, :], scalar1=rs4[:, qs:qs + 1],
                                                               scalar2=None, op0=ALU.mult),
                              reads=[("o_f", qs), "rs4"], writes=[("o_b", qs)])

                def part_b(qc=qc):
                    for qs in range(2):
                        P.pe(lambda e, qs=qs: e.transpose(psb[:, qs * 128:(qs + 1) * 128], o_b[:, qs, :], ident_b[:]),
                             reads=[("o_b", qs), "ident_b"], writes=[PSB])
                    P.dve(lambda e: e.tensor_scalar(out=oT[slot][:, qc * 256:(qc + 1) * 256], in0=psb[:, 0:256],
                                                    scalar1=gout8[:, 0:1], scalar2=None, op0=ALU.mult),
                          reads=[PSB, "gout8"], writes=[("oT", slot, qc // 2)])
                deferred.setdefault(i + 3, []).append(part_a2)
                deferred.setdefault(i + 6, []).append(part_a3)
                deferred.setdefault(i + 10, []).append(part_b)

            def step(i, nbg=0):
                if i < nst:
                    emit_S(i)
                if 1 <= i <= nst:
                    emit_PV(i - 1, 1 if nbg else 2)
                for fn_ in deferred.pop(i - 1, []):
                    fn_()

            run_with_bg(nst, step, bg, tail=14)
            assert not deferred

        def dbg_dump(u, slot, q_, k_, v_, vpat):
            P.dma("sp", lambda e: e.dma_start(out=dbg_d["q"][u], in_=q_), reads=[("qT", slot, ci) for ci in range(4)])
            P.dma("sp", lambda e: e.dma_start(out=dbg_d["k"][u], in_=k_), reads=[("kT", slot, ci) for ci in range(5)])
            P.dma("sp", lambda e: e.dma_start(out=dbg_d["v"][u], in_=v_.rearrange(vpat)),
                  reads=[("V", slot, g) for g in range(5)] + [("Vones", slot)])

        if n_diff:
            diff_load(0)
            diff_load(1)
            kw0 = dict(ybs=(2, 4, 6), scrs=(3, 5, 7), grouped=True)
            gk0 = qk_items(0, wsl[0], 128, kT[0], "kT", TCH, 1, True, sqb, qn, qnb, t1, ropec, ropes, **kw0)
            gq0 = qk_items(0, wsl[0], 0, qT[0], "qT", LAT, 0, True, sqb, qn, qnb, t1, ropec, ropes, roff=5, **kw0)
            emit_pipelined(gk0 + gq0)
            for it in v_items(0, wsl[0], Vd[0], False):
                it()
        for u in range(n_diff):
            slot = u % 2
            bg = []
            if u + 1 < n_diff:
                bg += diff_proj_items(u + 1)
            if u >= 1:
                diff_load_wout(u - 1)
                bg += wout_items((u - 1) % 2, woutb[(u - 1) % 2], oT[(u - 1) % 2])
            if dbg:
                dbg_dump(u, slot, qT[slot], kT[slot], Vd[slot], "p a b -> p (a b)")
            diff_attention(u, bg)
            if u + 2 < n_diff:
                diff_load(u + 2)
            if dbg:
                P.dma("sp", lambda e, u=u, slot=slot: e.dma_start(out=dbg_d["o"][u], in_=oT[slot]),
                      reads=[("oT", slot, t) for t in range(4)])
        if n_diff:
            diff_load_wout(n_diff - 1)
            for it in wout_items((n_diff - 1) % 2, woutb[(n_diff - 1) % 2], oT[(n_diff - 1) % 2], banks=(2, 4, 5, 6, 7)):
                it()

        P.barrier()
        A2 = Bump()
        Utab = A2([128, 2, 12, 128], F32)
        Umask = A2([128, 12, 128], F32)
        wsl_n = [A2([128, 8, 384], BF16) for _ in range(2)]
        qT_n = [A2([128, S], BF16) for _ in range(2)]
        kT_n = [A2([128, T], BF16) for _ in range(2)]
        Vn = [A2([128, 18, 2, 66], BF16) for _ in range(2)]
        oT_n = [A2([128, S], BF16) for _ in range(2)]
        woutb_n = [A2([128, 1024], BF16) for _ in range(2)]
        sqb_n = [A2([128, 512], BF16) for _ in range(2)]
        sbn = [A2([128, 640], F32) for _ in range(2)]
        pTn = [A2([128, 7, 128], BF16) for _ in range(2)]
        o_tok = [A2([128, 128], BF16) for _ in range(2)]
        rzn = A2([128, 2], F32)
        P.dma("sp", lambda e: e.dma_start(out=Umask.rearrange("p a b -> p (a b)"), in_=nam_d), writes=["Umask"])
        for sl in range(2):
            P.dve(lambda e, sl=sl: e.memset(Vn[sl][:, :, :, 64:65], 1.0), writes=[("Vones", sl)])
        n_na = 4 if stage >= 4 else 0

        def na_load(pr):
            u = 4 + pr
            slot = pr % 2
            P.dma("pool", lambda e: e.dma_start(out=wsl_n[slot].rearrange("p a b -> p (a b)"), in_=win_d[u]),
                  writes=[("win", slot)])

        def na_load_wout(pr):
            u = 4 + pr
            slot = pr % 2
            P.dma("pool", lambda e: e.dma_start(out=woutb_n[slot], in_=wout_d[u]), writes=[("wout", slot)])

        def na_table(pr):
            P.dma("sp", lambda e: e.dma_start(out=Utab.rearrange("p a b c -> p (a b c)"), in_=nab_d[pr]),
                  writes=["Utab"])
            for hh in range(2):
                P.dve(lambda e, hh=hh: e.tensor_tensor(out=Utab[:, hh, :, :], in0=Utab[:, hh, :, :], in1=Umask,
                                                       op=ALU.add), reads=["Utab", "Umask"], writes=["Utab"])

        def na_qk_groups(pr):
            slot = pr % 2
            kw = dict(ybs=(2, 4, 6), scrs=(3, 5, 7), lnvs=[(lnv, "lnv"), (tmpA[0], ("tmpA", 0))],
                      rstds=[(rstd, "rstd"), (tmpA[1], ("tmpA", 1))], grouped=True)
            gk = qk_items(slot, wsl_n[slot], 128, kT_n[slot], "kT", TCH, 3, False, sqb_n, None, None, None, None, None, **kw)
            gq = qk_items(slot, wsl_n[slot], 0, qT_n[slot], "qT", LAT, 2, False, sqb_n, None, None, None, None, None,
                          roff=5, **kw)
            return gk + gq

        def na_v_items(pr):
            slot = pr % 2
            return v_items(slot, wsl_n[slot], Vn[slot], True)

        def na_attention(pr, bg):
            slot = pr % 2
            nsteps = [(qt, hh) for qt in range(16) for hh in range(2)]
            NSETS = [(4, 5), (6, 7)]
            ndef = {}

            def na_info(qt):
                generic = 2 <= qt <= 13
                if generic:
                    kts = list(range(qt - 2, qt + 3))
                    e0 = 0
                else:
                    kts = list(range(0, 4)) if qt < 2 else list(range(12, 16))
                    e0 = 5 + (kts[0] - qt + 3)
                return generic, kts, e0

            def na_S(i):
                qt, hh = nsteps[i]
                generic, kts, e0 = na_info(qt)
                nl = len(kts)
                bx, by = NSETS[i % 2]
                nb_ = i % 2
                hs = slice(hh * 64, (hh + 1) * 64)
                for s_, kt in enumerate(kts):
                    if s_ < 4:
                        dstp, bkk = ps[:, bx, s_ * 128:(s_ + 1) * 128], bx
                    else:
                        dstp, bkk = ps[:, by, 0:128], by
                    P.pe(lambda e, dstp=dstp, kt=kt: e.matmul(
                        dstp, kT_n[slot][hs, kt * 128:(kt + 1) * 128], qT_n[slot][hs, qt * 128:(qt + 1) * 128],
                        start=True, stop=True),
                        reads=[("kT", slot, kt // 4), ("qT", slot, qt // 4)], writes=[PS(bkk)])
                for s_ in range(2):
                    P.pe(lambda e, s_=s_: e.matmul(
                        ps[:, by, 128 + s_ * 128:256 + s_ * 128], kT_n[slot][hs, S + s_ * 128:S + (s_ + 1) * 128],
                        qT_n[slot][hs, qt * 128:(qt + 1) * 128], start=True, stop=True),
                        reads=[("kT", slot, 4), ("qT", slot, qt // 4)], writes=[PS(by)])
                P.dve(lambda e: e.scalar_tensor_tensor(
                    out=sbn[nb_][:, 0:512], in0=ps[:, bx, :], scalar=0.125,
                    in1=Utab[:, hh, e0:e0 + 4, :].rearrange("p a b -> p (a b)"), op0=ALU.mult, op1=ALU.add),
                    reads=[PS(bx), "Utab"], writes=[("sbn", nb_)])
                if generic:
                    P.dve(lambda e: e.scalar_tensor_tensor(
                        out=sbn[nb_][:, 512:640], in0=ps[:, by, 0:128], scalar=0.125, in1=Utab[:, hh, 4, :],
                        op0=ALU.mult, op1=ALU.add),
                        reads=[PS(by), "Utab"], writes=[("sbn", nb_)])
                P.act(lambda e: e.activation(
                    pTn[nb_][:, 0:nl, :].rearrange("p a b -> p (a b)"), sbn[nb_][:, 0:nl * 128], AF.Exp),
                    reads=[("sbn", nb_)], writes=[("pTn", nb_)])
                P.act(lambda e: e.activation(
                    pTn[nb_][:, 5:7, :].rearrange("p a b -> p (a b)"), ps[:, by, 128:384], AF.Exp, scale=0.125),
                    reads=[PS(by)], writes=[("pTn", nb_)])

            def na_PV(i):
                qt, hh = nsteps[i]
                generic, kts, e0 = na_info(qt)
                nb_ = i % 2
                ob = qt % 2
                bka = i % 2
                srcs = [(s_, kt) for s_, kt in enumerate(kts)] + [(5, 16), (6, 17)]
                for j_, (s_, kt) in enumerate(srcs):
                    P.pe(lambda e, s_=s_, kt=kt, j_=j_, nn=len(srcs): e.matmul(
                        ps[:, bka, 0:65], pTn[nb_][:, s_, :], Vn[slot][:, kt, hh, 0:65],
                        start=(j_ == 0), stop=(j_ == nn - 1)),
                        reads=[("pTn", nb_), ("V", slot, kt // 4), ("Vones", slot)], writes=[PS(bka)])
                P.dve(lambda e: e.reciprocal(rzn[:, hh:hh + 1], ps[:, bka, 64:65]),
                      reads=[PS(bka)], writes=[("rzn", hh)])
                P.dve(lambda e: e.tensor_scalar(
                    out=o_tok[ob][:, hh * 64:(hh + 1) * 64], in0=ps[:, bka, 0:64], scalar1=rzn[:, hh:hh + 1],
                    scalar2=None, op0=ALU.mult),
                    reads=[PS(bka), ("rzn", hh)], writes=[("o_tok", ob, hh)])
                if hh == 1:
                    def part_b(qt=qt, ob=ob):
                        P.pe(lambda e: e.transpose(psb[:, 0:128], o_tok[ob], ident_b[:]),
                             reads=[("o_tok", ob, 0), ("o_tok", ob, 1), "ident_b"], writes=[PSB])
                        P.dve(lambda e: e.tensor_copy(oT_n[slot][:, qt * 128:(qt + 1) * 128], psb[:, 0:128]),
                              reads=[PSB], writes=[("oT", slot, qt // 4)])
                    ndef.setdefault(i + 2, []).append(part_b)

            nn_ = len(nsteps)

            def step(i, nbg=0):
                if i < nn_:
                    na_S(i)
                if 1 <= i <= nn_:
                    na_PV(i - 1)
                for fn_ in ndef.pop(i - 1, []):
                    fn_()

            run_with_bg(nn_, step, bg, tail=4)
            assert not ndef

        if n_na:
            na_load(0)
            na_load(1)
            emit_pipelined(na_qk_groups(0))
            for it in na_v_items(0):
                it()
        for pr in range(n_na):
            u = 4 + pr
            slot = pr % 2
            na_table(pr)
            bg = []
            if pr + 1 < n_na:
                bg += na_v_items(pr + 1)
            if pr >= 1:
                na_load_wout(pr - 1)
                bg += wout_items((pr - 1) % 2, woutb_n[(pr - 1) % 2], oT_n[(pr - 1) % 2])
            if dbg:
                dbg_dump(u, slot, qT_n[slot], kT_n[slot], Vn[slot], "p a b c -> p (a b c)")
            na_attention(pr, bg)
            if pr + 1 < n_na:
                emit_pipelined(na_qk_groups(pr + 1))
            if pr + 2 < n_na:
                na_load(pr + 2)
            if dbg:
                P.dma("sp", lambda e, u=u, slot=slot: e.dma_start(out=dbg_d["o"][u], in_=oT_n[slot]),
                      reads=[("oT", slot, t) for t in range(4)])
        if n_na:
            na_load_wout(n_na - 1)
            for it in wout_items((n_na - 1) % 2, woutb_n[(n_na - 1) % 2], oT_n[(n_na - 1) % 2], banks=(2, 4, 5, 6, 7)):
                it()

        if stage <= 4:
            P.barrier()
            if dbg:
                dump_hT()
            write_out()
            P.emit(nc)
            return nc

        P.barrier()
        norm_phase(2, LAT, sq1)
        P.barrier()
        if dbg:
            ffn_phase(2, LAT, wgu_d[1], wd_d[1], 0)
            P.barrier()
            dump_hT()
            write_out()
        else:
            ffn_phase(2, LAT, wgu_d[1], wd_d[1], 0,
                      final=lambda ci: write_out(tiles=range(4 * ci, 4 * ci + 4), banks=(6, 7), ooff=66048))
        P.emit(nc)
    return nc


def _rope_tables():
    t = np.arange(S, dtype=np.int32)
    row = (t // 64).astype(np.float32)
    col = (t % 64).astype(np.float32)
    inv_freq = (np.float32(10000.0) ** (-np.arange(16, dtype=np.float32) / np.float32(16))).astype(np.float32)
    ang_row = row[:, None] * inv_freq[None, :]
    ang_col = col[:, None] * inv_freq[None, :]
    cosT = np.zeros((128, S), np.float32)
    sinT = np.zeros((128, S), np.float32)
    rmat = np.zeros((128, 128), np.float32)
    for p in range(128):
        d = p % 64
        ang = ang_row if d < 32 else ang_col
        dd = d % 32
        i = dd % 16
        cosT[p] = np.cos(ang[:, i])
        if dd < 16:
            sinT[p] = -np.sin(ang[:, i])
            partner = p + 16
        else:
            sinT[p] = np.sin(ang[:, i])
            partner = p - 16
        rmat[partner, p] = 1.0
    return cosT, sinT, rmat


def _na_tables(rpb):
    p = np.arange(128)
    a = (p >= 64).astype(np.int64)
    kc = p % 64
    qc = np.arange(64)
    colvalid = np.zeros((64, 64), bool)
    for q in range(64):
        cs = min(max(q - 8, 0), 48)
        colvalid[cs:cs + 16, q] = True
    dc = kc[:, None] - qc[None, :] + 15
    dc_c = np.clip(dc, 0, 30)
    rels = [(-2 + e, True) for e in range(5)] + [(-3 + e, False) for e in range(7)]
    gath = np.zeros((8, 128, 12, 2, 64), np.float32)
    mask = np.zeros((128, 12, 2, 64), np.float32)
    for e, (rel, generic) in enumerate(rels):
        for b in range(2):
            dr = 2 * rel - b + a
            ok_r = (dr >= -7) & (dr <= 7)
            if generic:
                ok_r &= (dr >= -4) & (dr <= 3)
            valid = ok_r[:, None] & colvalid[kc, :]
            dr_c = np.clip(dr + 7, 0, 14)
            vals = rpb[:, dr_c[:, None], dc_c]
            gath[:, :, e, b, :] = np.where(valid[None], vals, np.float32(0.0))
            mask[:, e, b, :] = np.where(valid, np.float32(0.0), np.float32(NEG))
    nab = gath.reshape(4, 2, 128, 12 * 128).transpose(0, 2, 1, 3).reshape(4, 128, 2 * 12 * 128)
    nam = mask.reshape(128, 12 * 128)
    return np.ascontiguousarray(nab), np.ascontiguousarray(nam)


def _prep_shared(inp):
    f = lambda a: np.ascontiguousarray(np.asarray(a, dtype=np.float32))
    sh = {}
    wada = f(inp["w_ada"])[0]
    sh["wada"] = np.ascontiguousarray(wada.reshape(8, 128, 18, 512).transpose(2, 1, 0, 3).reshape(18, 128, 4096))
    sh["badaT"] = np.ascontiguousarray(f(inp["b_ada"])[0].reshape(72, 128).T)
    g = np.stack([f(inp["norm1"])[0], f(inp["norm2"])[0], f(inp["norm3"])[0]], 0)
    sh["gT"] = np.ascontiguousarray(g.reshape(3, 8, 128).transpose(2, 0, 1).reshape(128, 24))
    for i, nm in ((1, "ffn1"), (2, "ffn2")):
        wgu = f(inp[nm + "_w_gu"])[0]
        gcols = wgu[:, :FF].reshape(8, 128, NJ, 128)
        ucols = wgu[:, FF:].reshape(8, 128, NJ, 128)
        both = np.stack([gcols, ucols], axis=3)
        sh["wgu%d" % i] = np.ascontiguousarray(both.transpose(2, 1, 0, 3, 4).reshape(NJ, 128, 2048))
        sh["wd%d" % i] = f(inp[nm + "_w_down"])[0]
    win = f(inp["w_in"])[0].reshape(8, 128, 3072)
    units = []
    for u in range(8):
        if u < 4:
            cols = [u * 128, 512 + u * 128, 1024 + u * 128]
        else:
            cols = [1536 + (u - 4) * 128, 2048 + (u - 4) * 128, 2560 + (u - 4) * 128]
        blk = np.concatenate([win[:, :, c0:c0 + 128] for c0 in cols], axis=2)
        units.append(blk.transpose(1, 0, 2).reshape(128, 3072))
    sh["win"] = np.ascontiguousarray(np.stack(units, 0))
    sh["wout"] = np.ascontiguousarray(f(inp["w_out"])[0].reshape(8, 128, 1024))
    qkg = np.stack([np.tile(f(inp["diff_q_norm"])[0], 2), np.tile(f(inp["diff_k_norm"])[0], 2),
                    np.tile(f(inp["na_q_norm"])[0], 2), np.tile(f(inp["na_k_norm"])[0], 2)], 1)
    sh["qkg"] = np.ascontiguousarray(qkg)
    sh["gout"] = np.ascontiguousarray(f(inp["diff_out_norm"])[0].reshape(128, 1))
    sh["lamv"] = np.ascontiguousarray(np.concatenate([f(inp["lam_q1"])[0], f(inp["lam_k1"])[0],
                                                      f(inp["lam_q2"])[0], f(inp["lam_k2"])[0]]).reshape(1, 256))
    cosT, sinT, rmat = _rope_tables()
    sh["ropec"], sh["ropes"], sh["rmat"] = cosT, sinT, rmat
    sh["ident"] = np.eye(128, dtype=np.float32)
    sh["ones"] = np.ones((128, 128), np.float32)
    bo = np.zeros((128, 128), np.float32)
    bo[:64, :64] = 1.0
    bo[64:, 64:] = 1.0
    sh["bones"] = bo
    sh["nab"], sh["nam"] = _na_tables(f(inp["na_rpb"])[0])
    return sh


def make_in_maps(inp):
    sh = _prep_shared(inp)
    x = np.asarray(inp["x"], np.float32)
    ctx = np.asarray(inp["ctx"], np.float32)
    c = np.asarray(inp["c"], np.float32)
    c_ctx = np.asarray(inp["c_ctx"], np.float32)
    maps = []
    for b in range(8):
        m = dict(sh)
        m["x"] = np.ascontiguousarray(x[b])
        m["ctx"] = np.ascontiguousarray(ctx[b])
        m["cc"] = np.ascontiguousarray(np.stack([c[b], c_ctx], 0))
        maps.append(m)
    return maps


_NC_CACHE = {}


def kernel(**inputs):
    if "nc" not in _NC_CACHE:
        _NC_CACHE["nc"] = build()
    nc = _NC_CACHE["nc"]
    maps = make_in_maps(inputs)
    res = run_bass_kernel_spmd(nc, maps, core_ids=list(range(8)))
    return np.stack([np.asarray(r["out"], np.float32) for r in res.results], 0)
```

```python
import math
from contextlib import ExitStack

import numpy as np
import concourse.bass as bass
import concourse.mybir as mybir
from concourse.bass_utils import run_bass_kernel_spmd

F32 = mybir.dt.float32
BF16 = mybir.dt.bfloat16
ALU = mybir.AluOpType
AF = mybir.ActivationFunctionType
AX = mybir.AxisListType

DMA_RING = 8


class _Op:
    __slots__ = ("eng", "fn", "reads", "writes", "dma", "idx", "need", "signal", "seq",
                 "slot", "val", "prewait", "barrier")


class Prog:
    ENG = ("pe", "act", "dve", "pool", "sp")

    def __init__(self):
        self.ops = []

    def add(self, eng, fn, reads=(), writes=(), dma=False):
        op = _Op()
        op.eng, op.fn, op.dma = eng, fn, dma
        rs, ws = list(reads), list(writes)
        for r in list(rs):
            if isinstance(r, tuple) and r[0] == "ps" and r not in ws:
                ws.append(r)
        op.reads, op.writes = tuple(rs), tuple(ws)
        op.idx = len(self.ops)
        op.need = []
        op.signal = False
        op.seq = 0
        op.slot = op.val = op.prewait = None
        op.barrier = False
        self.ops.append(op)
        return op

    def pe(self, fn, reads=(), writes=()):
        return self.add("pe", fn, reads, writes)

    def act(self, fn, reads=(), writes=()):
        return self.add("act", fn, reads, writes)

    def dve(self, fn, reads=(), writes=()):
        return self.add("dve", fn, reads, writes)

    def pool(self, fn, reads=(), writes=()):
        return self.add("pool", fn, reads, writes)

    def dma(self, eng, fn, reads=(), writes=()):
        return self.add(eng, fn, reads, writes, dma=True)

    def barrier(self):
        op = self.add("sp", None)
        op.barrier = True
        return op

    def _analyze(self):
        last_w = {}
        readers = {}
        ops = self.ops
        last_comp = {}
        recent_dma = {e: [] for e in self.ENG}
        bar_deps = []
        for op in ops:
            if op.barrier:
                bar_deps = list(last_comp.values())
                for e in self.ENG:
                    bar_deps += recent_dma[e][-DMA_RING:]
                last_w.clear()
                readers.clear()
                continue
            raw = set()
            other = set()
            for r in op.reads:
                if r in last_w:
                    raw.add(last_w[r])
            for w in op.writes:
                if w in last_w:
                    other.add(last_w[w])
                for rd in readers.get(w, ()):
                    other.add(rd)
            raw.discard(op.idx)
            other.discard(op.idx)
            other -= raw
            need = []
            for p in sorted(raw | other):
                P = ops[p]
                if P.dma or op.dma:
                    need.append(p)
                elif P.eng == op.eng:
                    if P.eng == "pe":
                        continue
                    need.append(p)
                else:
                    need.append(p)
            for p in bar_deps:
                if p not in need and not (ops[p].eng == op.eng and not ops[p].dma and not op.dma
                                          and op.eng == "pe"):
                    need.append(p)
            need.sort()
            op.need = need
            for p in need:
                ops[p].signal = True
            for r in op.reads:
                readers.setdefault(r, []).append(op.idx)
            for w in op.writes:
                last_w[w] = op.idx
                readers[w] = []
            if op.dma:
                recent_dma[op.eng].append(op.idx)
            else:
                last_comp[op.eng] = op.idx
        cnt = {e: 0 for e in self.ENG}
        dcnt = {e: 0 for e in self.ENG}
        for op in ops:
            if op.barrier:
                continue
            if op.dma:
                j = dcnt[op.eng]
                dcnt[op.eng] += 1
                op.slot = j % DMA_RING
                op.val = 16 * (j // DMA_RING + 1)
                op.prewait = 16 * (j // DMA_RING) if j >= DMA_RING else None
            elif op.signal:
                cnt[op.eng] += 1
                op.seq = cnt[op.eng]
        self.counts = cnt
        self.dcounts = dcnt

    def emit(self, nc):
        self._analyze()
        ops = self.ops
        with ExitStack() as es:
            csem = {e: es.enter_context(nc.semaphore("c_" + e)) for e in self.ENG if e != "sp"}
            dsem = {e: [es.enter_context(nc.semaphore("d_%s%d" % (e, i))) for i in range(DMA_RING)]
                    for e in self.ENG if self.dcounts[e] > 0}
            block = es.enter_context(nc.Block())
            for c in self.counts.values():
                assert c < 60000, self.counts

            def run(engname, handle):
                known = {}
                last_dma = {}
                for op in ops:
                    if op.eng != engname or op.barrier:
                        continue
                    waits = []
                    for p in op.need:
                        Pp = ops[p]
                        if Pp.dma:
                            waits.append((dsem[Pp.eng][Pp.slot], Pp.val))
                        else:
                            waits.append((csem[Pp.eng], Pp.seq))
                    if op.dma and op.prewait is not None:
                        waits.append((dsem[op.eng][op.slot], op.prewait))
                    for s, v in waits:
                        k = id(s)
                        if known.get(k, 0) >= v:
                            continue
                        known[k] = v
                        handle.wait_ge(s, v)
                    inst = op.fn(handle)
                    if op.dma:
                        inst.then_inc(dsem[op.eng][op.slot], 16)
                        last_dma[op.slot] = op.val
                    elif op.signal:
                        inst.then_inc(csem[op.eng], 1)
                for slot, v in last_dma.items():
                    if known.get(id(dsem[engname][slot]), 0) < v:
                        handle.wait_ge(dsem[engname][slot], v)

            @block.tensor
            def _(e):
                run("pe", e)

            @block.scalar
            def _(e):
                run("act", e)

            @block.vector
            def _(e):
                run("dve", e)

            @block.gpsimd
            def _(e):
                run("pool", e)

            @block.sync
            def _(e):
                run("sp", e)


D = 1024
S = 2048
C = 256
T = S + C
KC = 8
FF = 2816
NJ = 22
NMOD = 9
EPS = 1e-6
LAM_INIT = 0.8 - 0.6 * math.exp(0.0)
TCH = [(0, 512, 0), (512, 512, 0), (1024, 512, 0), (1536, 512, 0), (2048, 256, 1)]
FGROUPS = [(0, 6), (6, 12), (12, 18), (18, 22)]
NEG = -30000.0


def build(stage=99, dbg=False):
    nc = bass.Bass("TRN2", target_bir_lowering=False)

    def din(name, shape, dt=F32):
        return nc.dram_tensor(name, list(shape), dt, kind="ExternalInput").ap()

    x_d = din("x", [S, D])
    ctx_d = din("ctx", [C, D])
    cc_d = din("cc", [2, D])
    wada_d = din("wada", [18, 128, 4096])
    badaT_d = din("badaT", [128, 72])
    gT_d = din("gT", [128, 24])
    wgu_d = [din("wgu1", [NJ, 128, 2048]), din("wgu2", [NJ, 128, 2048])]
    wd_d = [din("wd1", [FF, D]), din("wd2", [FF, D])]
    win_d = din("win", [8, 128, 3072])
    wout_d = din("wout", [8, 128, 1024])
    qkg_d = din("qkg", [128, 4])
    gout_d = din("gout", [128, 1])
    lamv_d = din("lamv", [1, 256])
    ropec_d = din("ropec", [128, S])
    ropes_d = din("ropes", [128, S])
    rmat_d = din("rmat", [128, 128])
    ident_d = din("ident", [128, 128])
    ones_d = din("ones", [128, 128])
    bones_d = din("bones", [128, 128])
    nab_d = din("nab", [4, 128, 2 * 12 * 128])
    nam_d = din("nam", [128, 12 * 128])
    out_d = nc.dram_tensor("out", [S, D], F32, kind="ExternalOutput").ap()
    dbg_d = {}
    if dbg:
        dbg_d["hT"] = nc.dram_tensor("dbg_hT", [128, KC * T], F32, kind="ExternalOutput").ap()
        dbg_d["mod"] = nc.dram_tensor("dbg_mod", [128, 144], F32, kind="ExternalOutput").ap()
        dbg_d["xn"] = nc.dram_tensor("dbg_xn", [128, KC * T], BF16, kind="ExternalOutput").ap()
        dbg_d["q"] = nc.dram_tensor("dbg_q", [8, 128, S], BF16, kind="ExternalOutput").ap()
        dbg_d["k"] = nc.dram_tensor("dbg_k", [8, 128, T], BF16, kind="ExternalOutput").ap()
        dbg_d["v"] = nc.dram_tensor("dbg_v", [8, 128, 18 * 132], BF16, kind="ExternalOutput").ap()
        dbg_d["o"] = nc.dram_tensor("dbg_o", [8, 128, S], BF16, kind="ExternalOutput").ap()

    P = Prog()
    with ExitStack() as es:
        def sb(name, shape, dt):
            return es.enter_context(nc.sbuf_tensor("s_" + name, list(shape), dt))

        hT = sb("hT", [128, KC, T], F32)
        xnT = sb("xnT", [128, KC, T], BF16)
        ident = sb("ident", [128, 128], F32)
        ones_b = sb("ones_b", [128, 128], BF16)
        bones_b = sb("bones_b", [128, 128], BF16)
        rmat_b = sb("rmat_b", [128, 128], BF16)
        ident_b = sb("ident_b", [128, 128], BF16)
        modT = sb("modT", [128, 72, 2], F32)
        Asc = sb("Asc", [128, 3, KC, 2], F32)
        Gsc = sb("Gsc", [128, 3, KC, 2], F32)
        gT = sb("gT", [128, 24], F32)
        qkg = sb("qkg", [128, 4], F32)
        gout = sb("gout", [128, 1], F32)
        epst = sb("epst", [128, 1], F32)
        neglam = sb("neglam", [128, 1], F32)
        rstd = sb("rstd", [128, 512], F32)
        lnv = sb("lnv", [128, 512], F32)
        tmpA = [sb("tmpA%d" % i, [128, 512], F32) for i in range(2)]
        ARENA = 84 * 1024
        arena = sb("arena", [128, ARENA // 2], BF16)
        ps = es.enter_context(nc.psum_tensor("ps", [128, 8, 512], F32))
        psb = ps[:, 3, :].bitcast(BF16)

        def carve(off, shape, dt):
            n = int(np.prod(shape[1:]))
            if dt == F32:
                assert off % 4 == 0
                v = arena[:, off // 2: off // 2 + 2 * n].bitcast(F32)
            else:
                v = arena[:, off // 2: off // 2 + n]
            if len(shape) == 2:
                return v
            names = "abcd"[: len(shape) - 1]
            pat = "p (" + " ".join(names) + ") -> p " + " ".join(names)
            kw = {names[i]: shape[i + 1] for i in range(len(shape) - 1)}
            return v.rearrange(pat, **kw)

        PS = lambda b: ("ps", b)
        PSB = ("ps", 3)

        P.dma("sp", lambda e: e.dma_start(out=ident[:], in_=ident_d), writes=["ident"])
        P.dma("pool", lambda e: e.dma_start(out=ones_b[:], in_=ones_d), writes=["ones_b"])
        P.dma("pool", lambda e: e.dma_start(out=bones_b[:], in_=bones_d), writes=["bones_b"])
        P.dma("pool", lambda e: e.dma_start(out=rmat_b[:], in_=rmat_d), writes=["rmat_b"])
        P.dma("pool", lambda e: e.dma_start(out=ident_b[:], in_=ident_d), writes=["ident_b"])
        P.dma("sp", lambda e: e.dma_start(out=gT[:], in_=gT_d), writes=["gT"])
        P.dma("sp", lambda e: e.dma_start(out=qkg[:], in_=qkg_d), writes=["qkg"])
        P.dma("sp", lambda e: e.dma_start(out=gout[:], in_=gout_d), writes=["gout"])
        P.dve(lambda e: e.memset(epst[:], EPS), writes=["epst"])

        def norm_stats(t0, L, sq, bank, rstd_, rkey):
            tiles = [("hT", i) for i in range(t0 // 128, (t0 + L) // 128)]
            for kc in range(KC):
                if kc % 2 == 0:
                    P.act(lambda e, kc=kc: e.activation(sq[:, kc, 0:L], hT[:, kc, t0:t0 + L], AF.Square),
                          reads=tiles, writes=[("sq", kc)])
                else:
                    P.dve(lambda e, kc=kc: e.tensor_tensor(out=sq[:, kc, 0:L], in0=hT[:, kc, t0:t0 + L],
                                                           in1=hT[:, kc, t0:t0 + L], op=ALU.mult),
                          reads=tiles, writes=[("sq", kc)])
            for kc in range(KC):
                P.pe(lambda e, kc=kc: e.matmul(ps[:, bank, 0:L], ones_b[:], sq[:, kc, 0:L],
                                               start=(kc == 0), stop=(kc == KC - 1)),
                     reads=[("sq", kc), "ones_b"], writes=[PS(bank)])
            P.act(lambda e: e.activation(lnv[:, 0:L], ps[:, bank, 0:L], AF.Ln, bias=epst[:, 0:1], scale=1.0 / D),
                  reads=[PS(bank), "epst"], writes=["lnv"])
            P.act(lambda e: e.activation(rstd_[:, 0:L], lnv[:, 0:L], AF.Exp, scale=-0.5),
                  reads=["lnv"], writes=[rkey])

        def norm_apply(s, ci, t0, L, v, ntmp, rstd_, rkey):
            tiles = [("hT", i) for i in range(t0 // 128, (t0 + L) // 128)]
            for kc in range(KC):
                tb = ntmp[kc % 4]
                P.dve(lambda e, kc=kc, tb=tb: e.scalar_tensor_tensor(
                    out=tb[:, 0:L], in0=hT[:, kc, t0:t0 + L], scalar=Asc[:, s, kc, v:v + 1], in1=rstd_[:, 0:L],
                    op0=ALU.mult, op1=ALU.mult),
                    reads=tiles + [rkey, "Asc"], writes=[("ntmp", kc % 4)])
                P.act(lambda e, kc=kc, tb=tb: e.activation(
                    xnT[:, kc, t0:t0 + L], tb[:, 0:L], AF.Identity, bias=Bsc(s, kc, v), scale=1.0),
                    reads=[("ntmp", kc % 4), "modT"], writes=[("xn", ci)])

        def norm_phase(s, chunks, sq, bank=6, toff=8192, ci0=0):
            ntmp = [carve(toff + i * 2048, [128, 512], F32) for i in range(4)]
            for ci_, (t0, L, v) in enumerate(chunks):
                norm_stats(t0, L, sq, bank, rstd, "rstd")
                norm_apply(s, ci_ + ci0, t0, L, v, ntmp, rstd, "rstd")

        xin = [carve(8192 + i * 4096, [128, 1024], F32) for i in range(2)]
        cc_sb = carve(16384, [128, 1024], F32)
        sc_sb = carve(20480, [128, 1024], F32)
        lam_sb = carve(24576, [128, 256], F32)
        lam_t = carve(25600, [128, 16], F32)
        ones_row = carve(25664, [128, 128], F32)
        PA = 64512
        scT = carve(PA, [128, 8, 2], BF16)
        mod_blk = carve(PA + 256, [128, 512], F32)
        wada_ring = [carve(PA + 256 + 2048 + i * 8192, [128, 8, 512], BF16) for i in range(2)]
        badaT = sb("badaT", [128, 72], F32)

        P.dma("sp", lambda e: e.dma_start(out=cc_sb[0:2, :], in_=cc_d), writes=["cc"])
        P.dma("sp", lambda e: e.dma_start(out=badaT[:], in_=badaT_d), writes=["badaT"])
        P.dma("sp", lambda e: e.dma_start(out=lam_sb[0:1, :], in_=lamv_d), writes=["lamv"])
        P.act(lambda e: e.activation(sc_sb[0:2, :], cc_sb[0:2, :], AF.Silu), reads=["cc"], writes=["sc"])
        for kc in range(KC):
            P.pe(lambda e, kc=kc: e.matmul(ps[:, 0, 2 * kc:2 * kc + 2], sc_sb[0:2, kc * 128:(kc + 1) * 128],
                                           ident[0:2, 0:2], start=True, stop=True),
                 reads=["sc", "ident"], writes=[PS(0)])
        P.dve(lambda e: e.tensor_copy(scT.rearrange("p a b -> p (a b)"), ps[:, 0, 0:16]),
              reads=[PS(0)], writes=["scT"])

        def wada_load(n):
            wr = wada_ring[n % 2]
            P.dma("pool", lambda e: e.dma_start(out=wr.rearrange("p a b -> p (a b)"), in_=wada_d[n]),
                  writes=[("wada", n % 2)])

        def wada_block(n, acc_bank, tr_bank):
            wr = wada_ring[n % 2]
            for kc in range(KC):
                P.pe(lambda e, kc=kc: e.matmul(ps[0:2, acc_bank, :], scT[:, kc, :], wr[:, kc, :],
                                               start=(kc == 0), stop=(kc == KC - 1)),
                     reads=["scT", ("wada", n % 2)], writes=[PS(acc_bank)])
            P.dve(lambda e: e.tensor_copy(mod_blk[0:2, :], ps[0:2, acc_bank, :]),
                  reads=[PS(acc_bank)], writes=["modb"])
            for q in range(4):
                i = 4 * n + q
                P.pe(lambda e, i=i, q=q: e.matmul(ps[:, tr_bank, 2 * i:2 * i + 2], mod_blk[0:2, q * 128:(q + 1) * 128],
                                                  ident[0:2, 0:2], start=True, stop=True),
                     reads=["modb", "ident"], writes=[PS(tr_bank)])

        def mods_finish(i_lo, i_hi, tr_bank, subs, do_a=True, do_g=True):
            for v in range(2):
                P.dve(lambda e, v=v: e.tensor_tensor(
                    out=modT[:, i_lo:i_hi, v], in0=ps[:, tr_bank, 2 * i_lo:2 * i_hi].rearrange("p (a b) -> p a b", b=2)[:, :, v],
                    in1=badaT[:, i_lo:i_hi], op=ALU.add),
                    reads=[PS(tr_bank), "badaT"], writes=["modT"])
            for s_ in subs:
                for v in range(2):
                    if do_a:
                        P.dve(lambda e, s_=s_, v=v: e.scalar_tensor_tensor(
                            out=Asc[:, s_, :, v], in0=modT[:, (3 * s_ + 1) * 8:(3 * s_ + 2) * 8, v], scalar=1.0,
                            in1=gT[:, s_ * 8:(s_ + 1) * 8], op0=ALU.add, op1=ALU.mult),
                            reads=["modT", "gT"], writes=["Asc"])
                if do_g:
                    P.dve(lambda e, s_=s_: e.tensor_scalar(
                        out=Gsc[:, s_, :, :], in0=modT[:, (3 * s_ + 2) * 8:(3 * s_ + 3) * 8, :],
                        scalar1=(1.0 if s_ == 1 else 0.5), scalar2=None, op0=ALU.mult),
                        reads=["modT"], writes=["Gsc"])

        def x_tile(i):
            slot = i % 2
            src = x_d[i * 128:(i + 1) * 128, :] if i < 16 else ctx_d[(i - 16) * 128:(i - 15) * 128, :]
            P.dma("sp", lambda e: e.dma_start(out=xin[slot][:], in_=src), writes=["xin%d" % slot])
            for half in range(2):
                bank = 4 + 2 * (i % 2) + half
                for q in range(4):
                    kc = half * 4 + q
                    P.pe(lambda e, kc=kc, bank=bank, q=q: e.transpose(
                        ps[:, bank, q * 128:(q + 1) * 128], xin[slot][:, kc * 128:(kc + 1) * 128], ident[:]),
                        reads=["xin%d" % slot, "ident"], writes=[PS(bank)])
                dst = hT[:, half * 4:half * 4 + 4, i * 128:(i + 1) * 128]
                srcp = ps[:, bank, :].rearrange("p (a b) -> p a b", a=4)
                if half == 0:
                    P.act(lambda e, dst=dst, srcp=srcp: e.activation(dst, srcp, AF.Copy),
                          reads=[PS(bank)], writes=[("hT", i)])
                else:
                    P.dve(lambda e, dst=dst, srcp=srcp: e.tensor_copy(dst, srcp),
                          reads=[PS(bank)], writes=[("hT", i)])

        wada_load(0)
        wada_load(1)
        xsplit = [0, 5, 10, 14, 18]
        sq1 = carve(0, [128, 8, 512], BF16)
        rstd5 = [carve(32768 + i * 2048, [128, 512], F32) for i in range(5)]
        chunk_end = {3: 0, 7: 1, 11: 2, 15: 3, 17: 4}
        for n in range(4):
            for i in range(xsplit[n], xsplit[n + 1]):
                x_tile(i)
                if i in chunk_end:
                    ci = chunk_end[i]
                    norm_stats(TCH[ci][0], TCH[ci][1], sq1, 0, rstd5[ci], ("rstd5", ci))
            wada_block(n, 1 + (n % 2), 3)
            if n + 2 < 18:
                wada_load(n + 2)
        mods_finish(0, 16, 3, [0], do_g=False)
        Bsc = lambda s, kc, v: modT[:, 3 * s * 8 + kc, v:v + 1]
        P.dve(lambda e: e.memset(ones_row[0:1, :], 1.0), writes=["ones_row"])
        P.dve(lambda e: e.tensor_tensor(out=lam_sb[0:1, 0:64], in0=lam_sb[0:1, 0:64],
                                        in1=lam_sb[0:1, 64:128], op=ALU.mult), reads=["lamv"], writes=["lamv"])
        P.dve(lambda e: e.tensor_tensor(out=lam_sb[0:1, 128:192], in0=lam_sb[0:1, 128:192],
                                        in1=lam_sb[0:1, 192:256], op=ALU.mult), reads=["lamv"], writes=["lamv"])
        P.dve(lambda e: e.reduce_sum(out=lam_t[0:1, 0:1], in_=lam_sb[0:1, 0:64], axis=AX.X),
              reads=["lamv"], writes=["lam_t"])
        P.dve(lambda e: e.reduce_sum(out=lam_t[0:1, 1:2], in_=lam_sb[0:1, 128:192], axis=AX.X),
              reads=["lamv", "lam_t"], writes=["lam_t"])
        P.act(lambda e: e.activation(lam_t[0:1, 2:4], lam_t[0:1, 0:2], AF.Exp), reads=["lam_t"], writes=["lam_t"])
        P.dve(lambda e: e.scalar_tensor_tensor(out=lam_t[0:1, 4:5], in0=lam_t[0:1, 3:4], scalar=-LAM_INIT,
                                               in1=lam_t[0:1, 2:3], op0=ALU.add, op1=ALU.subtract),
              reads=["lam_t"], writes=["lam_t2"])
        P.pe(lambda e: e.matmul(ps[:, 0, 0:1], ones_row[0:1, :], lam_t[0:1, 4:5], start=True, stop=True),
             reads=["lam_t2", "ones_row"], writes=[PS(0)])
        P.dve(lambda e: e.tensor_copy(neglam[:], ps[:, 0, 0:1]), reads=[PS(0)], writes=["neglam"])

        def adaln_bg_items():
            items = []
            for n in range(4, 18):
                def item(n=n):
                    wada_block(n, 7, 6)
                    if n + 2 < 18:
                        wada_load(n + 2)
                    if n == 5:
                        mods_finish(16, 24, 6, [0], do_a=False)
                    if n == 17:
                        mods_finish(24, 72, 6, [1, 2])
                        if dbg:
                            P.dma("sp", lambda e: e.dma_start(out=dbg_d["mod"], in_=modT.rearrange("p a b -> p (a b)")),
                                  reads=["modT"])
                items.append(item)
            return items

        def ffn_phase(s, chunks, wgu, wd, aoff, final=None, bgitems=None):
            actT = carve(aoff, [128, 6, T], BF16)
            wgu_ring = [carve(aoff + 27648 + i * 4096, [128, 8, 256], BF16) for i in range(4)]
            wd_ring = [carve(aoff + 27648 + 16384 + i * 2048, [128, 1024], BF16) for i in range(8)]
            sg = [carve(aoff + 27648 + 16384 + 16384 + i * 2048, [128, 512], F32) for i in range(2)]
            nci = len(chunks)
            cnt = [0]
            dcnt = [0]

            def load_wgu(j):
                P.dma("pool", lambda e, j=j: e.dma_start(out=wgu_ring[j % 4].rearrange("p a b -> p (a b)"), in_=wgu[j]),
                      writes=[("wgu", j % 4)])

            def load_wd(j):
                P.dma("pool", lambda e, j=j: e.dma_start(out=wd_ring[j % 8][:], in_=wd[j * 128:(j + 1) * 128, :]),
                      writes=[("wd", j % 8)])

            for j in range(3):
                load_wgu(j)
            for j in range(6):
                load_wd(j)
            for (j0, j1) in FGROUPS:
                for j in range(j0, j1):
                    if j + 3 < NJ:
                        load_wgu(j + 3)
                    if bgitems:
                        bgitems.pop(0)()
                    w = wgu_ring[j % 4]
                    for ci, (t0, L, v) in enumerate(chunks):
                        bg = 2 * (cnt[0] % 2)
                        bu = bg + 1
                        cnt[0] += 1
                        for kc in range(KC):
                            P.pe(lambda e, w=w, kc=kc, t0=t0, L=L, bg=bg: e.matmul(
                                ps[:, bg, 0:L], w[:, kc, 0:128], xnT[:, kc, t0:t0 + L],
                                start=(kc == 0), stop=(kc == KC - 1)),
                                reads=[("wgu", j % 4), ("xn", ci)], writes=[PS(bg)])
                        for kc in range(KC):
                            P.pe(lambda e, w=w, kc=kc, t0=t0, L=L, bu=bu: e.matmul(
                                ps[:, bu, 0:L], w[:, kc, 128:256], xnT[:, kc, t0:t0 + L],
                                start=(kc == 0), stop=(kc == KC - 1)),
                                reads=[("wgu", j % 4), ("xn", ci)], writes=[PS(bu)])
                        sgb = sg[cnt[0] % 2]
                        P.act(lambda e, sgb=sgb, bg=bg, L=L: e.activation(sgb[:, 0:L], ps[:, bg, 0:L], AF.Silu),
                              reads=[PS(bg)], writes=[("sg", cnt[0] % 2)])
                        P.dve(lambda e, sgb=sgb, bu=bu, L=L, jj=j - j0, t0=t0: e.tensor_tensor(
                            out=actT[:, jj, t0:t0 + L], in0=sgb[:, 0:L], in1=ps[:, bu, 0:L], op=ALU.mult),
                            reads=[("sg", cnt[0] % 2), PS(bu)], writes=[("act", j - j0, ci)])
                ng = j1 - j0
                last_group = (j1 == NJ) and final is not None
                order = ([(m, ci) for ci in range(len(chunks)) for m in range(KC)] if last_group
                         else [(m, ci) for m in range(KC) for ci in range(len(chunks))])
                for (m, ci) in order:
                    if True:
                        t0, L, v = chunks[ci]
                        bd = 4 + (dcnt[0] % 2)
                        dcnt[0] += 1
                        tiles = [("hT", i) for i in range(t0 // 128, (t0 + L) // 128)]
                        for jj in range(ng):
                            j = j0 + jj
                            P.pe(lambda e, j=j, jj=jj, m=m, t0=t0, L=L, bd=bd, ng=ng: e.matmul(
                                ps[:, bd, 0:L], wd_ring[j % 8][:, m * 128:(m + 1) * 128], actT[:, jj, t0:t0 + L],
                                start=(jj == 0), stop=(jj == ng - 1)),
                                reads=[("wd", j % 8), ("act", jj, ci)], writes=[PS(bd)])
                        P.dve(lambda e, m=m, t0=t0, L=L, bd=bd, v=v: e.scalar_tensor_tensor(
                            out=hT[:, m, t0:t0 + L], in0=ps[:, bd, 0:L], scalar=Gsc[:, s, m, v:v + 1],
                            in1=hT[:, m, t0:t0 + L], op0=ALU.mult, op1=ALU.add),
                            reads=[PS(bd), "Gsc"] + tiles, writes=tiles)
                        if last_group and m == KC - 1:
                            final(ci)
                nxt = [g for g in FGROUPS if g[0] == j1]
                if nxt:
                    for j in range(max(nxt[0][0], 6), nxt[0][1]):
                        load_wd(j)

        def dump_hT():
            P.dma("sp", lambda e: e.dma_start(out=dbg_d["hT"], in_=hT.rearrange("p a b -> p (a b)")),
                  reads=[("hT", i) for i in range(18)])

        def write_out(tiles=range(16), banks=(0, 1, 2, 3), ooff=0):
            osb = [carve(ooff + i * 4096, [128, 1024], F32) for i in range(2)]
            for i in tiles:
                slot = i % 2
                for half in range(2):
                    bank = banks[(2 * (i % 2) + half) % len(banks)]
                    for q in range(4):
                        kc = half * 4 + q
                        P.pe(lambda e, kc=kc, bank=bank, q=q, i=i: e.transpose(
                            ps[:, bank, q * 128:(q + 1) * 128], hT[:, kc, i * 128:(i + 1) * 128], ident[:]),
                            reads=[("hT", i), "ident"], writes=[PS(bank)])
                    dst = osb[slot][:, half * 512:(half + 1) * 512]
                    if half == 0:
                        P.act(lambda e, dst=dst, bank=bank: e.activation(dst, ps[:, bank, :], AF.Copy),
                              reads=[PS(bank)], writes=[("osb", slot, half)])
                    else:
                        P.dve(lambda e, dst=dst, bank=bank: e.tensor_copy(dst, ps[:, bank, :]),
                              reads=[PS(bank)], writes=[("osb", slot, half)])
                P.dma("sp", lambda e, i=i, slot=slot: e.dma_start(out=out_d[i * 128:(i + 1) * 128, :], in_=osb[slot][:]),
                      reads=[("osb", slot, 0), ("osb", slot, 1)])

        ntmp1 = [carve(43008 + i * 2048, [128, 512], F32) for i in range(4)]
        for ci, (t0, L, v) in enumerate(TCH):
            norm_apply(0, ci, t0, L, v, ntmp1, rstd5[ci], ("rstd5", ci))
        if dbg:
            P.dma("sp", lambda e: e.dma_start(out=dbg_d["xn"], in_=xnT.rearrange("p a b -> p (a b)")),
                  reads=[("xn", i) for i in range(5)])
        P.barrier()
        if stage >= 2:
            bg1 = adaln_bg_items()
            if stage > 2:
                sq2 = carve(PA, [128, 8, 512], BF16)
                ffn_phase(0, TCH, wgu_d[0], wd_d[0], 0, bgitems=bg1,
                          final=lambda ci: norm_phase(1, [TCH[ci]], sq2, toff=PA + 8192, ci0=ci))
            else:
                ffn_phase(0, TCH, wgu_d[0], wd_d[0], 0, bgitems=bg1)
            assert not bg1
        if stage <= 2:
            P.barrier()
            if dbg:
                dump_hT()
            write_out()
            P.emit(nc)
            return nc

        P.barrier()

        class Bump:
            def __init__(self):
                self.off = 0

            def __call__(self, shape, dt):
                n = int(np.prod(shape[1:])) * (4 if dt == F32 else 2)
                self.off = (self.off + 3) // 4 * 4
                v = carve(self.off, shape, dt)
                self.off += n
                assert self.off <= ARENA, self.off
                return v

        def chunk_of_tile(i):
            return i // 4 if i < 16 else 4

        LAT = TCH[:4]
        YB = 2
        SCR = 3


        def qk_items(slot, wsl_, wcol, dst, dname, chunks, gcol, rope, sqb_, qn, qnb, t1, ropec, ropes,
                     ybs=(2,), scrs=(3,), lnvs=None, rstds=None, grouped=False, roff=0):
            lnvs = lnvs or [(lnv, "lnv")]
            rstds = rstds or [(rstd, "rstd")]
            items = []
            groups = []
            for ci, (t0, L, v) in enumerate(chunks):
                ri = ci + roff
                sb_ = sqb_[ri % 2]
                sqk = ("sqq", ri % 2)
                roped = rope and v == 0
                yb = ybs[ri % len(ybs)]
                scr = scrs[ri % len(scrs)]
                lnv_, lnk = lnvs[ri % len(lnvs)]
                rstd_, rsk = rstds[ri % len(rstds)]

                def stage_a(ci=ci, t0=t0, L=L, sb_=sb_, yb=yb, sqk=sqk):
                    for kc in range(KC):
                        P.pe(lambda e, kc=kc: e.matmul(
                            ps[:, yb, 0:L], wsl_[:, kc, wcol:wcol + 128], xnT[:, kc, t0:t0 + L],
                            start=(kc == 0), stop=(kc == KC - 1)),
                            reads=[("win", slot), ("xn", ci)], writes=[PS(yb)])
                    P.act(lambda e: e.activation(sb_[:, 0:L], ps[:, yb, 0:L], AF.Square),
                          reads=[PS(yb)], writes=[sqk])

                def stage_b(ci=ci, t0=t0, L=L, sb_=sb_, roped=roped, yb=yb, scr=scr, lnv_=lnv_, lnk=lnk,
                            rstd_=rstd_, rsk=rsk, sqk=sqk):
                    P.pe(lambda e: e.matmul(ps[:, scr, 0:L], bones_b[:], sb_[:, 0:L], start=True, stop=True),
                         reads=[sqk, "bones_b"], writes=[PS(scr)])
                    P.act(lambda e: e.activation(lnv_[:, 0:L], ps[:, scr, 0:L], AF.Ln, bias=epst[:, 0:1], scale=1.0 / 64),
                          reads=[PS(scr), "epst"], writes=[lnk])
                    P.act(lambda e: e.activation(rstd_[:, 0:L], lnv_[:, 0:L], AF.Exp, scale=-0.5),
                          reads=[lnk], writes=[rsk])
                    if not roped:
                        P.dve(lambda e: e.scalar_tensor_tensor(
                            out=dst[:, t0:t0 + L], in0=ps[:, yb, 0:L], scalar=qkg[:, gcol:gcol + 1], in1=rstd_[:, 0:L],
                            op0=ALU.mult, op1=ALU.mult),
                            reads=[PS(yb), rsk, "qkg"], writes=[(dname, slot, ci)])
                    else:
                        P.dve(lambda e: e.scalar_tensor_tensor(
                            out=qn[:, 0:L], in0=ps[:, yb, 0:L], scalar=qkg[:, gcol:gcol + 1], in1=rstd_[:, 0:L],
                            op0=ALU.mult, op1=ALU.mult),
                            reads=[PS(yb), rsk, "qkg"], writes=["qn"])
                        P.dve(lambda e: e.tensor_copy(qnb[:, 0:L], qn[:, 0:L]), reads=["qn"], writes=["qnb"])
                        P.pool(lambda e: e.tensor_tensor(out=t1[:, 0:L], in0=qn[:, 0:L], in1=ropec[:, t0:t0 + L],
                                                         op=ALU.mult), reads=["qn", "ropec"], writes=["t1"])

                def stage_c(ci=ci, t0=t0, L=L, scr=scr):
                    P.pe(lambda e: e.matmul(ps[:, scr, 0:L], rmat_b[:], qnb[:, 0:L], start=True, stop=True),
                         reads=["qnb", "rmat_b"], writes=[PS(scr)])
                    P.dve(lambda e: e.tensor_tensor(out=tmpA[0][:, 0:L], in0=ps[:, scr, 0:L],
                                                    in1=ropes[:, t0:t0 + L], op=ALU.mult),
                          reads=[PS(scr), "ropes"], writes=[("tmpA", 0)])
                    P.pool(lambda e: e.tensor_tensor(out=dst[:, t0:t0 + L], in0=t1[:, 0:L],
                                                     in1=tmpA[0][:, 0:L], op=ALU.add),
                           reads=["t1", ("tmpA", 0)], writes=[(dname, slot, ci)])
                g = [stage_a, stage_b] + ([stage_c] if roped else [])
                groups.append(g)
                items += g
            return groups if grouped else items

        def emit_pipelined(groups):
            n = len(groups)
            for i in range(n + 1):
                if i < n:
                    groups[i][0]()
                if i >= 1:
                    for st in groups[i - 1][1:]:
                        st()

        def v_items(slot, wsl_, Vt, na):
            items = []
            for g in range(5):
                def item(g=g):
                    tl = list(range(4 * g, min(4 * g + 4, 18)))
                    for ti, i in enumerate(tl):
                        for kc in range(KC):
                            P.pe(lambda e, kc=kc, i=i, ti=ti: e.matmul(
                                ps[:, YB, ti * 128:(ti + 1) * 128], xnT[:, kc, i * 128:(i + 1) * 128],
                                wsl_[:, kc, 256:384], start=(kc == 0), stop=(kc == KC - 1)),
                                reads=[("win", slot), ("xn", chunk_of_tile(i))], writes=[PS(YB)])
                    n = len(tl)
                    if na:
                        dstv = Vt[:, 4 * g:4 * g + n, :, 0:64]
                        srcv = ps[:, YB, 0:n * 128].rearrange("p (a h d) -> p a h d", a=n, h=2)
                    else:
                        dstv = Vt[:, 4 * g:4 * g + n, 0:128]
                        srcv = ps[:, YB, 0:n * 128].rearrange("p (a d) -> p a d", a=n)
                    P.dve(lambda e: e.tensor_copy(dstv, srcv), reads=[PS(YB)], writes=[("V", slot, g)])
                items.append(item)
            return items

        def wout_items(slot, woutb_, oTt, banks=(2,)):
            items = []
            for m in range(KC):
                for t in range(4):
                    def item(m=m, t=t, yb=banks[(m * 4 + t) % len(banks)]):
                        P.pe(lambda e: e.matmul(
                            ps[:, yb, :], woutb_[:, m * 128:(m + 1) * 128], oTt[:, t * 512:(t + 1) * 512],
                            start=True, stop=True),
                            reads=[("wout", slot), ("oT", slot, t)], writes=[PS(yb)])
                        tiles = [("hT", i) for i in range(4 * t, 4 * t + 4)]
                        eng = P.dve if (m * 4 + t) % 3 != 2 or len(banks) == 1 else P.pool
                        if len(banks) == 1 or (m * 4 + t) % 3 != 2:
                            P.dve(lambda e: e.scalar_tensor_tensor(
                                out=hT[:, m, t * 512:(t + 1) * 512], in0=ps[:, yb, :], scalar=Gsc[:, 1, m, 0:1],
                                in1=hT[:, m, t * 512:(t + 1) * 512], op0=ALU.mult, op1=ALU.add),
                                reads=[PS(yb), "Gsc"] + tiles, writes=tiles)
                        else:
                            P.act(lambda e: e.activation(tmpA[1][:, 0:512], ps[:, yb, :], AF.Copy, scale=Gsc[:, 1, m, 0:1]),
                                  reads=[PS(yb), "Gsc"], writes=[("tmpA", 1)])
                            P.pool(lambda e: e.tensor_tensor(out=hT[:, m, t * 512:(t + 1) * 512], in0=tmpA[1][:, 0:512],
                                                             in1=hT[:, m, t * 512:(t + 1) * 512], op=ALU.add),
                                   reads=[("tmpA", 1)] + tiles, writes=tiles)
                    items.append(item)
            return items

        def run_with_bg(nsteps_, step_fn, bg, tail=0):
            nb = len(bg)
            done = 0
            for i in range(nsteps_ + tail):
                want = min((nb * (i + 1) + nsteps_ - 1) // nsteps_, nb) if i < nsteps_ else done
                step_fn(i, want - done)
                if i < nsteps_:
                    while done < min(want, nb):
                        bg[done]()
                        done += 1
            while done < nb:
                bg[done]()
                done += 1

        A = Bump()
        ropec = A([128, S], F32)
        ropes = A([128, S], F32)
        wsl = [A([128, 8, 384], BF16) for _ in range(2)]
        qT = [A([128, S], BF16) for _ in range(2)]
        kT = [A([128, T], BF16) for _ in range(2)]
        Vd = [A([128, 18, 132], BF16) for _ in range(2)]
        oT = [A([128, S], BF16) for _ in range(2)]
        woutb = [A([128, 1024], BF16) for _ in range(2)]
        sqb = [A([128, 512], BF16) for _ in range(2)]
        qn = A([128, 512], F32)
        qnb = A([128, 512], BF16)
        t1 = A([128, 512], F32)
        pT = [A([128, 512], BF16) for _ in range(3)]
        o_f = A([128, 2, 128], F32)
        o_b = A([128, 2, 128], BF16)
        rz = A([128, 2, 2, 1], F32)
        rzl = A([128, 2, 1], F32)
        ssq = A([128, 2], F32)
        rs4 = A([128, 2], F32)
        gout8 = A([128, 1], F32)
        accs = A([128, 2, 260], F32)

        P.dma("sp", lambda e: e.dma_start(out=ropec, in_=ropec_d), writes=["ropec"])
        P.dma("sp", lambda e: e.dma_start(out=ropes, in_=ropes_d), writes=["ropes"])
        P.dve(lambda e: e.tensor_scalar(out=gout8, in0=gout[:], scalar1=1.0 - LAM_INIT, scalar2=None, op0=ALU.mult),
              reads=["gout"], writes=["gout8"])
        for sl in range(2):
            P.dve(lambda e, sl=sl: e.memset(Vd[sl][:, :, 128:129], 1.0), writes=[("Vones", sl)])

        n_diff = 4 if stage >= 3 else 0

        def diff_load(u):
            slot = u % 2
            P.dma("pool", lambda e: e.dma_start(out=wsl[slot].rearrange("p a b -> p (a b)"), in_=win_d[u]),
                  writes=[("win", slot)])

        def diff_load_wout(u):
            slot = u % 2
            P.dma("pool", lambda e: e.dma_start(out=woutb[slot], in_=wout_d[u]), writes=[("wout", slot)])

        def diff_proj_items(u):
            slot = u % 2
            return (qk_items(slot, wsl[slot], 128, kT[slot], "kT", TCH, 1, True, sqb, qn, qnb, t1, ropec, ropes)
                    + v_items(slot, wsl[slot], Vd[slot], False)
                    + qk_items(slot, wsl[slot], 0, qT[slot], "qT", LAT, 0, True, sqb, qn, qnb, t1, ropec, ropes))

        NFILL = 2

        def diff_attention(u, bg):
            slot = u % 2
            steps = [(qc, kt) for qc in range(8) for kt in range(18)]
            nst = len(steps)
            SPAIR = [4, 6]
            deferred = {}

            def emit_S(i):
                qc, kt = steps[i]
                b0 = SPAIR[i % 2]
                kci = chunk_of_tile(kt)
                for c in range(2):
                    P.pe(lambda e, c=c: e.matmul(
                        ps[:, b0 + c, 0:256], kT[slot][c * 64:(c + 1) * 64, kt * 128:(kt + 1) * 128],
                        qT[slot][c * 64:(c + 1) * 64, qc * 256:(qc + 1) * 256], start=True, stop=True),
                        reads=[("kT", slot, kci), ("qT", slot, qc // 2)], writes=[PS(b0 + c)])
                pi = i % 3
                P.act(lambda e: e.activation(pT[pi].rearrange("p (a b) -> p a b", a=2),
                                             ps[:, b0:b0 + 2, 0:256], AF.Exp, scale=0.125),
                      reads=[PS(b0), PS(b0 + 1)], writes=[("pT", pi)])

            def emit_PV(i, nfill=2):
                qc, kt = steps[i]
                ab = (0, 1)
                pi = i % 3
                if NFILL and kt != 0:
                    for f_ in range(nfill):
                        P.pe(lambda e, f_=f_: e.matmul(ps[:, ab[f_ % 2], 264:512], ident_b[:], xnT[:, f_, 0:248],
                                                       start=False, stop=False),
                             reads=["ident_b"], writes=[PS(ab[f_ % 2])])
                for c in range(2):
                    bk = ab[c]
                    for qs in range(2):
                        col = qs * 130
                        P.pe(lambda e, qs=qs, c=c, bk=bk, col=col: e.matmul(
                            ps[:, bk, col:col + 129], pT[pi][:, c * 256 + qs * 128:c * 256 + (qs + 1) * 128],
                            Vd[slot][:, kt, 0:129], start=(kt == 0 and qs == 0), stop=(kt == 17 and qs == 1)),
                            reads=[("pT", pi), ("V", slot, kt // 4), ("Vones", slot)], writes=[PS(bk)])
                if kt != 17:
                    return
                P.act(lambda e: e.activation(accs[:, 0, :], ps[:, ab[0], 0:260], AF.Copy),
                      reads=[PS(ab[0])], writes=[("accs", 0)])
                P.dve(lambda e: e.tensor_copy(accs[:, 1, :], ps[:, ab[1], 0:260]),
                      reads=[PS(ab[1])], writes=[("accs", 1)])
                for c in range(2):
                    accv = accs[:, c, :].rearrange("p (a b) -> p a b", a=2)
                    P.dve(lambda e, c=c, accv=accv: e.reciprocal(rz[:, c, 0:2, :], accv[:, :, 128:129]),
                          reads=[("accs", c)], writes=[("rz", c)])
                P.dve(lambda e: e.tensor_scalar(out=rzl[:, 0:2, :], in0=rz[:, 1, 0:2, :], scalar1=neglam[:, 0:1],
                                                scalar2=None, op0=ALU.mult),
                      reads=[("rz", 1), "neglam"], writes=["rzl"])
                for qs in range(2):
                    col = qs * 130
                    P.dve(lambda e, qs=qs, col=col: e.tensor_scalar(
                        out=o_f[:, qs, :], in0=accs[:, 0, col:col + 128], scalar1=rz[:, 0, qs, :], scalar2=None,
                        op0=ALU.mult),
                        reads=[("accs", 0), ("rz", 0)], writes=[("o_f", qs)])
                    P.dve(lambda e, qs=qs, col=col: e.scalar_tensor_tensor(
                        out=o_f[:, qs, :], in0=accs[:, 1, col:col + 128], scalar=rzl[:, qs, :],
                        in1=o_f[:, qs, :], op0=ALU.mult, op1=ALU.add),
                        reads=[("accs", 1), "rzl", ("o_f", qs)], writes=[("o_f", qs)])
                def part_a2():
                    P.dve(lambda e: e.memset(ssq, 0.0), writes=["ssq"])
                    for qs in range(2):
                        P.act(lambda e, qs=qs: e.activation(tmpA[1][:, 0:128], o_f[:, qs, :], AF.Square,
                                                            accum_out=ssq[:, qs:qs + 1]),
                              reads=[("o_f", qs), "ssq"], writes=["ssq", ("tmpA", 1)])
                    P.act(lambda e: e.activation(rs4[:, 0:2], ssq[:, 0:2], AF.Ln, bias=epst[:, 0:1], scale=1.0 / 128),
                          reads=["ssq", "epst"], writes=["rs4"])
                    P.act(lambda e: e.activation(rs4[:, 0:2], rs4[:, 0:2], AF.Exp, scale=-0.5), reads=["rs4"], writes=["rs4"])

                def part_a3():
                    for qs in range(2):
                        P.dve(lambda e, qs=qs: e.tensor_scalar(out=o_b[:, qs, :], in0=o_f[:, qs, :], scalar1=rs4[:, qs:qs + 1],
                                                               scalar2=None, op0=ALU.mult),
                              reads=[("o_f", qs), "rs4"], writes=[("o_b", qs)])

                def part_b(qc=qc):
                    for qs in range(2):
                        P.pe(lambda e, qs=qs: e.transpose(psb[:, qs * 128:(qs + 1) * 128], o_b[:, qs, :], ident_b[:]),
                             reads=[("o_b", qs), "ident_b"], writes=[PSB])
                    P.dve(lambda e: e.tensor_scalar(out=oT[slot][:, qc * 256:(qc + 1) * 256], in0=psb[:, 0:256],
                                                    scalar1=gout8[:, 0:1], scalar2=None, op0=ALU.mult),
                          reads=[PSB, "gout8"], writes=[("oT", slot, qc // 2)])
                deferred.setdefault(i + 3, []).append(part_a2)
                deferred.setdefault(i + 6, []).append(part_a3)
                deferred.setdefault(i + 10, []).append(part_b)

            def step(i, nbg=0):
                if i < nst:
                    emit_S(i)
                if 1 <= i <= nst:
                    emit_PV(i - 1, 1 if nbg else 2)
                for fn_ in deferred.pop(i - 1, []):
                    fn_()

            run_with_bg(nst, step, bg, tail=14)
            assert not deferred

        def dbg_dump(u, slot, q_, k_, v_, vpat):
            P.dma("sp", lambda e: e.dma_start(out=dbg_d["q"][u], in_=q_), reads=[("qT", slot, ci) for ci in range(4)])
            P.dma("sp", lambda e: e.dma_start(out=dbg_d["k"][u], in_=k_), reads=[("kT", slot, ci) for ci in range(5)])
            P.dma("sp", lambda e: e.dma_start(out=dbg_d["v"][u], in_=v_.rearrange(vpat)),
                  reads=[("V", slot, g) for g in range(5)] + [("Vones", slot)])

        if n_diff:
            diff_load(0)
            diff_load(1)
            kw0 = dict(ybs=(2, 4, 6), scrs=(3, 5, 7), grouped=True)
            gk0 = qk_items(0, wsl[0], 128, kT[0], "kT", TCH, 1, True, sqb, qn, qnb, t1, ropec, ropes, **kw0)
            gq0 = qk_items(0, wsl[0], 0, qT[0], "qT", LAT, 0, True, sqb, qn, qnb, t1, ropec, ropes, roff=5, **kw0)
            emit_pipelined(gk0 + gq0)
            for it in v_items(0, wsl[0], Vd[0], False):
                it()
        for u in range(n_diff):
            slot = u % 2
            bg = []
            if u + 1 < n_diff:
                bg += diff_proj_items(u + 1)
            if u >= 1:
                diff_load_wout(u - 1)
                bg += wout_items((u - 1) % 2, woutb[(u - 1) % 2], oT[(u - 1) % 2])
            if dbg:
                dbg_dump(u, slot, qT[slot], kT[slot], Vd[slot], "p a b -> p (a b)")
            diff_attention(u, bg)
            if u + 2 < n_diff:
                diff_load(u + 2)
            if dbg:
                P.dma("sp", lambda e, u=u, slot=slot: e.dma_start(out=dbg_d["o"][u], in_=oT[slot]),
                      reads=[("oT", slot, t) for t in range(4)])
        if n_diff:
            diff_load_wout(n_diff - 1)
            for it in wout_items((n_diff - 1) % 2, woutb[(n_diff - 1) % 2], oT[(n_diff - 1) % 2], banks=(2, 4, 5, 6, 7)):
                it()

        P.barrier()
        A2 = Bump()
        Utab = A2([128, 2, 12, 128], F32)
        Umask = A2([128, 12, 128], F32)
        wsl_n = [A2([128, 8, 384], BF16) for _ in range(2)]
        qT_n = [A2([128, S], BF16) for _ in range(2)]
        kT_n = [A2([128, T], BF16) for _ in range(2)]
        Vn = [A2([128, 18, 2, 66], BF16) for _ in range(2)]
        oT_n = [A2([128, S], BF16) for _ in range(2)]
        woutb_n = [A2([128, 1024], BF16) for _ in range(2)]
        sqb_n = [A2([128, 512], BF16) for _ in range(2)]
        sbn = [A2([128, 640], F32) for _ in range(2)]
        pTn = [A2([128, 7, 128], BF16) for _ in range(2)]
        o_tok = [A2([128, 128], BF16) for _ in range(2)]
        rzn = A2([128, 2], F32)
        P.dma("sp", lambda e: e.dma_start(out=Umask.rearrange("p a b -> p (a b)"), in_=nam_d), writes=["Umask"])
        for sl in range(2):
            P.dve(lambda e, sl=sl: e.memset(Vn[sl][:, :, :, 64:65], 1.0), writes=[("Vones", sl)])
        n_na = 4 if stage >= 4 else 0

        def na_load(pr):
            u = 4 + pr
            slot = pr % 2
            P.dma("pool", lambda e: e.dma_start(out=wsl_n[slot].rearrange("p a b -> p (a b)"), in_=win_d[u]),
                  writes=[("win", slot)])

        def na_load_wout(pr):
            u = 4 + pr
            slot = pr % 2
            P.dma("pool", lambda e: e.dma_start(out=woutb_n[slot], in_=wout_d[u]), writes=[("wout", slot)])

        def na_table(pr):
            P.dma("sp", lambda e: e.dma_start(out=Utab.rearrange("p a b c -> p (a b c)"), in_=nab_d[pr]),
                  writes=["Utab"])
            for hh in range(2):
                P.dve(lambda e, hh=hh: e.tensor_tensor(out=Utab[:, hh, :, :], in0=Utab[:, hh, :, :], in1=Umask,
                                                       op=ALU.add), reads=["Utab", "Umask"], writes=["Utab"])

        def na_qk_groups(pr):
            slot = pr % 2
            kw = dict(ybs=(2, 4, 6), scrs=(3, 5, 7), lnvs=[(lnv, "lnv"), (tmpA[0], ("tmpA", 0))],
                      rstds=[(rstd, "rstd"), (tmpA[1], ("tmpA", 1))], grouped=True)
            gk = qk_items(slot, wsl_n[slot], 128, kT_n[slot], "kT", TCH, 3, False, sqb_n, None, None, None, None, None, **kw)
            gq = qk_items(slot, wsl_n[slot], 0, qT_n[slot], "qT", LAT, 2, False, sqb_n, None, None, None, None, None,
                          roff=5, **kw)
            return gk + gq

        def na_v_items(pr):
            slot = pr % 2
            return v_items(slot, wsl_n[slot], Vn[slot], True)

        def na_attention(pr, bg):
            slot = pr % 2
            nsteps = [(qt, hh) for qt in range(16) for hh in range(2)]
            NSETS = [(4, 5), (6, 7)]
            ndef = {}

            def na_info(qt):
                generic = 2 <= qt <= 13
                if generic:
                    kts = list(range(qt - 2, qt + 3))
                    e0 = 0
                else:
                    kts = list(range(0, 4)) if qt < 2 else list(range(12, 16))
                    e0 = 5 + (kts[0] - qt + 3)
                return generic, kts, e0

            def na_S(i):
                qt, hh = nsteps[i]
                generic, kts, e0 = na_info(qt)
                nl = len(kts)
                bx, by = NSETS[i % 2]
                nb_ = i % 2
                hs = slice(hh * 64, (hh + 1) * 64)
                for s_, kt in enumerate(kts):
                    if s_ < 4:
                        dstp, bkk = ps[:, bx, s_ * 128:(s_ + 1) * 128], bx
                    else:
                        dstp, bkk = ps[:, by, 0:128], by
                    P.pe(lambda e, dstp=dstp, kt=kt: e.matmul(
                        dstp, kT_n[slot][hs, kt * 128:(kt + 1) * 128], qT_n[slot][hs, qt * 128:(qt + 1) * 128],
                        start=True, stop=True),
                        reads=[("kT", slot, kt // 4), ("qT", slot, qt // 4)], writes=[PS(bkk)])
                for s_ in range(2):
                    P.pe(lambda e, s_=s_: e.matmul(
                        ps[:, by, 128 + s_ * 128:256 + s_ * 128], kT_n[slot][hs, S + s_ * 128:S + (s_ + 1) * 128],
                        qT_n[slot][hs, qt * 128:(qt + 1) * 128], start=True, stop=True),
                        reads=[("kT", slot, 4), ("qT", slot, qt // 4)], writes=[PS(by)])
                P.dve(lambda e: e.scalar_tensor_tensor(
                    out=sbn[nb_][:, 0:512], in0=ps[:, bx, :], scalar=0.125,
                    in1=Utab[:, hh, e0:e0 + 4, :].rearrange("p a b -> p (a b)"), op0=ALU.mult, op1=ALU.add),
                    reads=[PS(bx), "Utab"], writes=[("sbn", nb_)])
                if generic:
                    P.dve(lambda e: e.scalar_tensor_tensor(
                        out=sbn[nb_][:, 512:640], in0=ps[:, by, 0:128], scalar=0.125, in1=Utab[:, hh, 4, :],
                        op0=ALU.mult, op1=ALU.add),
                        reads=[PS(by), "Utab"], writes=[("sbn", nb_)])
                P.act(lambda e: e.activation(
                    pTn[nb_][:, 0:nl, :].rearrange("p a b -> p (a b)"), sbn[nb_][:, 0:nl * 128], AF.Exp),
                    reads=[("sbn", nb_)], writes=[("pTn", nb_)])
                P.act(lambda e: e.activation(
                    pTn[nb_][:, 5:7, :].rearrange("p a b -> p (a b)"), ps[:, by, 128:384], AF.Exp, scale=0.125),
                    reads=[PS(by)], writes=[("pTn", nb_)])

            def na_PV(i):
                qt, hh = nsteps[i]
                generic, kts, e0 = na_info(qt)
                nb_ = i % 2
                ob = qt % 2
                bka = i % 2
                srcs = [(s_, kt) for s_, kt in enumerate(kts)] + [(5, 16), (6, 17)]
                for j_, (s_, kt) in enumerate(srcs):
                    P.pe(lambda e, s_=s_, kt=kt, j_=j_, nn=len(srcs): e.matmul(
                        ps[:, bka, 0:65], pTn[nb_][:, s_, :], Vn[slot][:, kt, hh, 0:65],
                        start=(j_ == 0), stop=(j_ == nn - 1)),
                        reads=[("pTn", nb_), ("V", slot, kt // 4), ("Vones", slot)], writes=[PS(bka)])
                P.dve(lambda e: e.reciprocal(rzn[:, hh:hh + 1], ps[:, bka, 64:65]),
                      reads=[PS(bka)], writes=[("rzn", hh)])
                P.dve(lambda e: e.tensor_scalar(
                    out=o_tok[ob][:, hh * 64:(hh + 1) * 64], in0=ps[:, bka, 0:64], scalar1=rzn[:, hh:hh + 1],
                    scalar2=None, op0=ALU.mult),
                    reads=[PS(bka), ("rzn", hh)], writes=[("o_tok", ob, hh)])
                if hh == 1:
                    def part_b(qt=qt, ob=ob):
                        P.pe(lambda e: e.transpose(psb[:, 0:128], o_tok[ob], ident_b[:]),
                             reads=[("o_tok", ob, 0), ("o_tok", ob, 1), "ident_b"], writes=[PSB])
                        P.dve(lambda e: e.tensor_copy(oT_n[slot][:, qt * 128:(qt + 1) * 128], psb[:, 0:128]),
                              reads=[PSB], writes=[("oT", slot, qt // 4)])
                    ndef.setdefault(i + 2, []).append(part_b)

            nn_ = len(nsteps)

            def step(i, nbg=0):
                if i < nn_:
                    na_S(i)
                if 1 <= i <= nn_:
                    na_PV(i - 1)
                for fn_ in ndef.pop(i - 1, []):
                    fn_()

            run_with_bg(nn_, step, bg, tail=4)
            assert not ndef

        if n_na:
            na_load(0)
            na_load(1)
            emit_pipelined(na_qk_groups(0))
            for it in na_v_items(0):
                it()
        for pr in range(n_na):
            u = 4 + pr
            slot = pr % 2
            na_table(pr)
            bg = []
            if pr + 1 < n_na:
                bg += na_v_items(pr + 1)
            if pr >= 1:
                na_load_wout(pr - 1)
                bg += wout_items((pr - 1) % 2, woutb_n[(pr - 1) % 2], oT_n[(pr - 1) % 2])
            if dbg:
                dbg_dump(u, slot, qT_n[slot], kT_n[slot], Vn[slot], "p a b c -> p (a b c)")
            na_attention(pr, bg)
            if pr + 1 < n_na:
                emit_pipelined(na_qk_groups(pr + 1))
            if pr + 2 < n_na:
                na_load(pr + 2)
            if dbg:
                P.dma("sp", lambda e, u=u, slot=slot: e.dma_start(out=dbg_d["o"][u], in_=oT_n[slot]),
                      reads=[("oT", slot, t) for t in range(4)])
        if n_na:
            na_load_wout(n_na - 1)
            for it in wout_items((n_na - 1) % 2, woutb_n[(n_na - 1) % 2], oT_n[(n_na - 1) % 2], banks=(2, 4, 5, 6, 7)):
                it()

        if stage <= 4:
            P.barrier()
            if dbg:
                dump_hT()
            write_out()
            P.emit(nc)
            return nc

        P.barrier()
        norm_phase(2, LAT, sq1)
        P.barrier()
        if dbg:
            ffn_phase(2, LAT, wgu_d[1], wd_d[1], 0)
            P.barrier()
            dump_hT()
            write_out()
        else:
            ffn_phase(2, LAT, wgu_d[1], wd_d[1], 0,
                      final=lambda ci: write_out(tiles=range(4 * ci, 4 * ci + 4), banks=(6, 7), ooff=66048))
        P.emit(nc)
    return nc


def _rope_tables():
    t = np.arange(S, dtype=np.int32)
    row = (t // 64).astype(np.float32)
    col = (t % 64).astype(np.float32)
    inv_freq = (np.float32(10000.0) ** (-np.arange(16, dtype=np.float32) / np.float32(16))).astype(np.float32)
    ang_row = row[:, None] * inv_freq[None, :]
    ang_col = col[:, None] * inv_freq[None, :]
    cosT = np.zeros((128, S), np.float32)
    sinT = np.zeros((128, S), np.float32)
    rmat = np.zeros((128, 128), np.float32)
    for p in range(128):
        d = p % 64
        ang = ang_row if d < 32 else ang_col
        dd = d % 32
        i = dd % 16
        cosT[p] = np.cos(ang[:, i])
        if dd < 16:
            sinT[p] = -np.sin(ang[:, i])
            partner = p + 16
        else:
            sinT[p] = np.sin(ang[:, i])
            partner = p - 16
        rmat[partner, p] = 1.0
    return cosT, sinT, rmat


def _na_tables(rpb):
    p = np.arange(128)
    a = (p >= 64).astype(np.int64)
    kc = p % 64
    qc = np.arange(64)
    colvalid = np.zeros((64, 64), bool)
    for q in range(64):
        cs = min(max(q - 8, 0), 48)
        colvalid[cs:cs + 16, q] = True
    dc = kc[:, None] - qc[None, :] + 15
    dc_c = np.clip(dc, 0, 30)
    rels = [(-2 + e, True) for e in range(5)] + [(-3 + e, False) for e in range(7)]
    gath = np.zeros((8, 128, 12, 2, 64), np.float32)
    mask = np.zeros((128, 12, 2, 64), np.float32)
    for e, (rel, generic) in enumerate(rels):
        for b in range(2):
            dr = 2 * rel - b + a
            ok_r = (dr >= -7) & (dr <= 7)
            if generic:
                ok_r &= (dr >= -4) & (dr <= 3)
            valid = ok_r[:, None] & colvalid[kc, :]
            dr_c = np.clip(dr + 7, 0, 14)
            vals = rpb[:, dr_c[:, None], dc_c]
            gath[:, :, e, b, :] = np.where(valid[None], vals, np.float32(0.0))
            mask[:, e, b, :] = np.where(valid, np.float32(0.0), np.float32(NEG))
    nab = gath.reshape(4, 2, 128, 12 * 128).transpose(0, 2, 1, 3).reshape(4, 128, 2 * 12 * 128)
    nam = mask.reshape(128, 12 * 128)
    return np.ascontiguousarray(nab), np.ascontiguousarray(nam)


def _prep_shared(inp):
    f = lambda a: np.ascontiguousarray(np.asarray(a, dtype=np.float32))
    sh = {}
    wada = f(inp["w_ada"])[0]
    sh["wada"] = np.ascontiguousarray(wada.reshape(8, 128, 18, 512).transpose(2, 1, 0, 3).reshape(18, 128, 4096))
    sh["badaT"] = np.ascontiguousarray(f(inp["b_ada"])[0].reshape(72, 128).T)
    g = np.stack([f(inp["norm1"])[0], f(inp["norm2"])[0], f(inp["norm3"])[0]], 0)
    sh["gT"] = np.ascontiguousarray(g.reshape(3, 8, 128).transpose(2, 0, 1).reshape(128, 24))
    for i, nm in ((1, "ffn1"), (2, "ffn2")):
        wgu = f(inp[nm + "_w_gu"])[0]
        gcols = wgu[:, :FF].reshape(8, 128, NJ, 128)
        ucols = wgu[:, FF:].reshape(8, 128, NJ, 128)
        both = np.stack([gcols, ucols], axis=3)
        sh["wgu%d" % i] = np.ascontiguousarray(both.transpose(2, 1, 0, 3, 4).reshape(NJ, 128, 2048))
        sh["wd%d" % i] = f(inp[nm + "_w_down"])[0]
    win = f(inp["w_in"])[0].reshape(8, 128, 3072)
    units = []
    for u in range(8):
        if u < 4:
            cols = [u * 128, 512 + u * 128, 1024 + u * 128]
        else:
            cols = [1536 + (u - 4) * 128, 2048 + (u - 4) * 128, 2560 + (u - 4) * 128]
        blk = np.concatenate([win[:, :, c0:c0 + 128] for c0 in cols], axis=2)
        units.append(blk.transpose(1, 0, 2).reshape(128, 3072))
    sh["win"] = np.ascontiguousarray(np.stack(units, 0))
    sh["wout"] = np.ascontiguousarray(f(inp["w_out"])[0].reshape(8, 128, 1024))
    qkg = np.stack([np.tile(f(inp["diff_q_norm"])[0], 2), np.tile(f(inp["diff_k_norm"])[0], 2),
                    np.tile(f(inp["na_q_norm"])[0], 2), np.tile(f(inp["na_k_norm"])[0], 2)], 1)
    sh["qkg"] = np.ascontiguousarray(qkg)
    sh["gout"] = np.ascontiguousarray(f(inp["diff_out_norm"])[0].reshape(128, 1))
    sh["lamv"] = np.ascontiguousarray(np.concatenate([f(inp["lam_q1"])[0], f(inp["lam_k1"])[0],
                                                      f(inp["lam_q2"])[0], f(inp["lam_k2"])[0]]).reshape(1, 256))
    cosT, sinT, rmat = _rope_tables()
    sh["ropec"], sh["ropes"], sh["rmat"] = cosT, sinT, rmat
    sh["ident"] = np.eye(128, dtype=np.float32)
    sh["ones"] = np.ones((128, 128), np.float32)
    bo = np.zeros((128, 128), np.float32)
    bo[:64, :64] = 1.0
    bo[64:, 64:] = 1.0
    sh["bones"] = bo
    sh["nab"], sh["nam"] = _na_tables(f(inp["na_rpb"])[0])
    return sh


def make_in_maps(inp):
    sh = _prep_shared(inp)
    x = np.asarray(inp["x"], np.float32)
    ctx = np.asarray(inp["ctx"], np.float32)
    c = np.asarray(inp["c"], np.float32)
    c_ctx = np.asarray(inp["c_ctx"], np.float32)
    maps = []
    for b in range(8):
        m = dict(sh)
        m["x"] = np.ascontiguousarray(x[b])
        m["ctx"] = np.ascontiguousarray(ctx[b])
        m["cc"] = np.ascontiguousarray(np.stack([c[b], c_ctx], 0))
        maps.append(m)
    return maps


_NC_CACHE = {}


def kernel(**inputs):
    if "nc" not in _NC_CACHE:
        _NC_CACHE["nc"] = build()
    nc = _NC_CACHE["nc"]
    maps = make_in_maps(inputs)
    res = run_bass_kernel_spmd(nc, maps, core_ids=list(range(8)))
    return np.stack([np.asarray(r["out"], np.float32) for r in res.results], 0)
```
